# Optimizing a Trainium2 kernel written in Bass

```python
import math
import jax, jax.numpy as jnp
from jax import lax
import numpy as np

D_MODEL = 1024
BATCH = 2
SEQ = 16384
DEPTH = 2

N_MEM = 256
DN_HEADS = 4
DN_DK = 128
DN_DV = 128
DN_CONV = 4
DN_CHUNK = 64
SWA_HEADS = 8
SWA_KV_HEADS = 2
SWA_DH = 64
WINDOW = 128
XA_HEADS = 4
XA_DH = 128
D_FF = 2816
N_BRANCH = 3
BRANCH_W = 512
DEEPNORM_ALPHA = (2 * DEPTH) ** 0.25
DEEPNORM_BETA = (8 * DEPTH) ** -0.25
LN_EPS = 1e-5
RMS_EPS = 1e-6
NEG_INF = -1e30
IN_SPLITS = (DN_HEADS * DN_DK, DN_HEADS * DN_DK, DN_HEADS * DN_DV, DN_HEADS, DN_HEADS,
             DN_HEADS * DN_DV, SWA_HEADS * SWA_DH, SWA_KV_HEADS * SWA_DH, SWA_KV_HEADS * SWA_DH,
             XA_HEADS * XA_DH, N_BRANCH * D_MODEL)
D_IN = sum(IN_SPLITS)
VALUE_SEGMENTS = (2, 8)

kernel_name = "hybrid_deltanet_swa_sink_memxattn_macaron_deepnorm"


def layer_norm(x, g, b):
    xf = x.astype(jnp.float32)
    mu = xf.mean(-1, keepdims=True)
    var = jnp.square(xf - mu).mean(-1, keepdims=True)
    return ((xf - mu) * lax.rsqrt(var + LN_EPS) * g.astype(jnp.float32) + b.astype(jnp.float32)).astype(x.dtype)


def swiglu(x, w_gu, w_down):
    gate, up = jnp.split(x @ w_gu, 2, axis=-1)
    return (jax.nn.silu(gate) * up) @ w_down


def causal_depthwise_conv(x, w):
    K, C = w.shape
    return lax.conv_general_dilated(x, w[:, None, :].astype(x.dtype), window_strides=(1,),
                                    padding=[(K - 1, 0)], dimension_numbers=('NWC', 'WIO', 'NWC'),
                                    feature_group_count=C)


def gated_delta_rule(q, k, v, g, beta):
    f32 = jnp.float32
    B_, S_, H, dk = q.shape
    dv = v.shape[-1]
    C = DN_CHUNK
    N = S_ // C

    def chunks(t):
        t = t.astype(f32).reshape((B_, N, C, H) + t.shape[3:])
        return jnp.moveaxis(t, 3, 1)

    q = chunks(q) * (dk ** -0.5)
    k, v, beta, g = chunks(k), chunks(v), chunks(beta), chunks(g)
    g = jnp.cumsum(g, axis=-1)
    tril = jnp.tril(jnp.ones((C, C), bool))
    strict = jnp.tril(jnp.ones((C, C), bool), -1)
    decay = jnp.exp(jnp.where(tril, g[..., :, None] - g[..., None, :], NEG_INF))

    k_beta = k * beta[..., None]
    a = jnp.where(strict, jnp.einsum('bhncd,bhnsd->bhncs', k_beta, k) * decay, 0.0)
    t_mat = a + jnp.eye(C, dtype=f32)
    rhs = jnp.concatenate([v * beta[..., None], k_beta * jnp.exp(g)[..., None]], axis=-1)
    sol = lax.linalg.triangular_solve(t_mat, rhs, left_side=True, lower=True, unit_diagonal=True)
    u, w = sol[..., :dv], sol[..., dv:]

    qk = jnp.where(tril, jnp.einsum('bhncd,bhnsd->bhncs', q, k) * decay, 0.0)
    g_last = g[..., -1]
    k_tail = k * jnp.exp(g_last[..., None] - g)[..., None]
    q_dec = q * jnp.exp(g)[..., None]

    def step(S, xs):
        q_i, qk_i, u_i, w_i, kt_i, gl_i = xs
        v_new = u_i - jnp.einsum('bhcd,bhde->bhce', w_i, S)
        o = jnp.einsum('bhcd,bhde->bhce', q_i, S) + jnp.einsum('bhcs,bhse->bhce', qk_i, v_new)
        S = S * jnp.exp(gl_i)[..., None, None] + jnp.einsum('bhcd,bhce->bhde', kt_i, v_new)
        return S, o

    xs = tuple(jnp.moveaxis(t, 2, 0) for t in (q_dec, qk, u, w, k_tail, g_last))
    S0 = jnp.zeros((B_, H, dk, dv), f32)
    _, o = lax.scan(step, S0, xs)
    o = jnp.moveaxis(o, 0, 2)
    return jnp.moveaxis(o, 1, 3).reshape(B_, S_, H, dv)


def deltanet_branch(q, k, v, b, a, z, conv_w, a_log, dt_bias, norm_w):
    f32 = jnp.float32
    B_, S_, _ = q.shape
    qkv = jax.nn.silu(causal_depthwise_conv(jnp.concatenate([q, k, v], axis=-1), conv_w))
    q, k, v = jnp.split(qkv.astype(f32), 3, axis=-1)
    q = q.reshape(B_, S_, DN_HEADS, DN_DK)
    k = k.reshape(B_, S_, DN_HEADS, DN_DK)
    v = v.reshape(B_, S_, DN_HEADS, DN_DV)
    q = q * lax.rsqrt(jnp.sum(q * q, -1, keepdims=True) + RMS_EPS)
    k = k * lax.rsqrt(jnp.sum(k * k, -1, keepdims=True) + RMS_EPS)
    beta = jax.nn.sigmoid(b.astype(f32))
    g = -jnp.exp(a_log.astype(f32)) * jax.nn.softplus(a.astype(f32) + dt_bias.astype(f32))
    o = gated_delta_rule(q, k, v, g, beta)
    o = o * lax.rsqrt(jnp.mean(o * o, -1, keepdims=True) + RMS_EPS) * norm_w.astype(f32)
    o = o * jax.nn.silu(z.astype(f32).reshape(B_, S_, DN_HEADS, DN_DV))
    return o.reshape(B_, S_, DN_HEADS * DN_DV).astype(z.dtype)


def sliding_window_attention(q, k, v, sinks):
    f32 = jnp.float32
    B_, S_, _ = q.shape
    Hkv, G, dh, W = SWA_KV_HEADS, SWA_HEADS // SWA_KV_HEADS, SWA_DH, WINDOW
    nb = S_ // W
    qb = q.reshape(B_, nb, W, Hkv, G, dh)

    def band(t):
        tb = t.reshape(B_, nb, W, Hkv, dh)
        prev = jnp.pad(tb[:, :-1], ((0, 0), (1, 0), (0, 0), (0, 0), (0, 0)))
        return jnp.concatenate([prev, tb], axis=2)

    kb, vb = band(k), band(v)
    s = jnp.einsum('bnqhgd,bnkhd->bnhgqk', qb, kb).astype(f32) * (dh ** -0.5)
    q_pos = jnp.arange(nb)[:, None] * W + jnp.arange(W)[None, :]
    k_pos = jnp.arange(nb)[:, None] * W - W + jnp.arange(2 * W)[None, :]
    diff = q_pos[:, :, None] - k_pos[:, None, :]
    mask = (diff >= 0) & (diff < W) & (k_pos[:, None, :] >= 0)
    s = jnp.where(mask[None, :, None, None], s, NEG_INF)
    sink = sinks.astype(f32).reshape(Hkv, G)[None, None, :, :, None, None]
    m = jnp.maximum(s.max(-1, keepdims=True), sink)
    p = jnp.exp(s - m)
    p = p / (p.sum(-1, keepdims=True) + jnp.exp(sink - m))
    o = jnp.einsum('bnhgqk,bnkhd->bnqhgd', p.astype(v.dtype), vb)
    return o.reshape(B_, S_, SWA_HEADS * dh)


def memory_cross_attention(q, mem_n, w_mem_kv):
    B_, S_, _ = q.shape
    M = mem_n.shape[1]
    k, v = jnp.split(mem_n @ w_mem_kv, 2, axis=-1)
    k = k.reshape(B_, M, XA_HEADS, XA_DH)
    v = v.reshape(B_, M, XA_HEADS, XA_DH)
    q = q.reshape(B_, S_, XA_HEADS, XA_DH)
    s = jnp.einsum('bshd,bmhd->bhsm', q, k).astype(jnp.float32) * (XA_DH ** -0.5)
    p = jax.nn.softmax(s, axis=-1)
    o = jnp.einsum('bhsm,bmhd->bshd', p.astype(v.dtype), v)
    return o.reshape(B_, S_, XA_HEADS * XA_DH)


def hybrid_layer(x, mem_n, ln_g, ln_b, ffn1_w_gu, ffn1_w_down, w_in, dn_conv_w, dn_a_log,
                 dn_dt_bias, dn_norm_w, swa_sinks, w_mem_kv, w_branch, w_out, ffn2_w_gu, ffn2_w_down):
    B_, S_, D = x.shape
    h = layer_norm(DEEPNORM_ALPHA * x + 0.5 * swiglu(x, ffn1_w_gu, ffn1_w_down), ln_g[0], ln_b[0])
    split_idx = [int(i) for i in np.cumsum(IN_SPLITS)[:-1]]
    (dn_q, dn_k, dn_v, dn_b, dn_a, dn_z, sw_q, sw_k, sw_v, xa_q, gates) = jnp.split(h @ w_in, split_idx, axis=-1)
    o_dn = deltanet_branch(dn_q, dn_k, dn_v, dn_b, dn_a, dn_z, dn_conv_w, dn_a_log, dn_dt_bias, dn_norm_w)
    o_sw = sliding_window_attention(sw_q, sw_k, sw_v, swa_sinks)
    o_xa = memory_cross_attention(xa_q, mem_n, w_mem_kv)
    branches = jnp.stack([o_dn, o_sw, o_xa], axis=2)
    gates = jax.nn.sigmoid(gates.reshape(B_, S_, N_BRANCH, D))
    merged = jnp.sum(gates * jnp.einsum('bsnc,ncd->bsnd', branches, w_branch), axis=2)
    h = layer_norm(DEEPNORM_ALPHA * h + merged @ w_out, ln_g[1], ln_b[1])
    return layer_norm(DEEPNORM_ALPHA * h + 0.5 * swiglu(h, ffn2_w_gu, ffn2_w_down), ln_g[2], ln_b[2])


def setup_inputs(seed: int = 0) -> dict:
    key = jax.random.key(seed)
    ks = jax.random.split(key, 20)
    f32 = jnp.float32
    nrm = lambda k, shape, s: jax.random.normal(k, shape, f32) * s
    col_scale = np.concatenate([np.full(n, DEEPNORM_BETA if i in VALUE_SEGMENTS else 1.0, np.float32)
                                for i, n in enumerate(IN_SPLITS)])
    kv_scale = np.concatenate([np.ones(XA_HEADS * XA_DH, np.float32),
                               np.full(XA_HEADS * XA_DH, DEEPNORM_BETA, np.float32)])
    dt = jnp.exp(jax.random.uniform(ks[9], (DEPTH, DN_HEADS), f32, math.log(1e-3), math.log(1e-1)))
    return {
        "x": nrm(ks[0], (BATCH, SEQ, D_MODEL), 1.0),
        "mem": nrm(ks[1], (BATCH, N_MEM, D_MODEL), 1.0),
        "mem_ln_g": 1.0 + nrm(ks[2], (D_MODEL,), 0.02),
        "mem_ln_b": nrm(ks[3], (D_MODEL,), 0.02),
        "ln_g": 1.0 + nrm(ks[4], (DEPTH, 3, D_MODEL), 0.02),
        "ln_b": nrm(ks[5], (DEPTH, 3, D_MODEL), 0.02),
        "ffn1_w_gu": nrm(ks[6], (DEPTH, D_MODEL, 2 * D_FF), DEEPNORM_BETA * D_MODEL ** -0.5),
        "ffn1_w_down": nrm(ks[7], (DEPTH, D_FF, D_MODEL), DEEPNORM_BETA * D_FF ** -0.5),
        "w_in": nrm(ks[8], (DEPTH, D_MODEL, D_IN), D_MODEL ** -0.5) * jnp.asarray(col_scale),
        "dn_conv_w": nrm(ks[10], (DEPTH, DN_CONV, 3 * DN_HEADS * DN_DK), DN_CONV ** -0.5),
        "dn_a_log": jnp.log(jax.random.uniform(ks[11], (DEPTH, DN_HEADS), f32, 1.0, 16.0)),
        "dn_dt_bias": dt + jnp.log(-jnp.expm1(-dt)),
        "dn_norm_w": 1.0 + nrm(ks[12], (DEPTH, DN_DV), 0.02),
        "swa_sinks": nrm(ks[13], (DEPTH, SWA_HEADS), 0.5),
        "w_mem_kv": nrm(ks[14], (DEPTH, D_MODEL, 2 * XA_HEADS * XA_DH), D_MODEL ** -0.5) * jnp.asarray(kv_scale),
        "w_branch": nrm(ks[15], (DEPTH, N_BRANCH, BRANCH_W, D_MODEL), DEEPNORM_BETA * BRANCH_W ** -0.5),
        "w_out": nrm(ks[16], (DEPTH, D_MODEL, D_MODEL), DEEPNORM_BETA * D_MODEL ** -0.5),
        "ffn2_w_gu": nrm(ks[17], (DEPTH, D_MODEL, 2 * D_FF), DEEPNORM_BETA * D_MODEL ** -0.5),
        "ffn2_w_down": nrm(ks[18], (DEPTH, D_FF, D_MODEL), DEEPNORM_BETA * D_FF ** -0.5),
    }


def reference(x, mem, mem_ln_g, mem_ln_b, ln_g, ln_b, ffn1_w_gu, ffn1_w_down, w_in, dn_conv_w,
              dn_a_log, dn_dt_bias, dn_norm_w, swa_sinks, w_mem_kv, w_branch, w_out, ffn2_w_gu, ffn2_w_down):
    mem_n = layer_norm(mem, mem_ln_g, mem_ln_b)
    for l in range(DEPTH):
        x = hybrid_layer(x, mem_n, ln_g[l], ln_b[l], ffn1_w_gu[l], ffn1_w_down[l], w_in[l], dn_conv_w[l],
                         dn_a_log[l], dn_dt_bias[l], dn_norm_w[l], swa_sinks[l], w_mem_kv[l], w_branch[l],
                         w_out[l], ffn2_w_gu[l], ffn2_w_down[l])
    return x
```

```python
import math
import numpy as np
import concourse.bass as bass
import concourse.mybir as mybir
from concourse.bass_utils import run_bass_kernel_spmd

F32 = mybir.dt.float32
BF16 = mybir.dt.bfloat16
AF = mybir.ActivationFunctionType
ALU = mybir.AluOpType

D_MODEL = 1024
BATCH = 2
SEQ = 16384
DEPTH = 2
N_MEM = 256
D_FF = 2816
D_IN = 6408
NCORES = 8
ALPHA = (2 * DEPTH) ** 0.25
LN_EPS = 1e-5
RMS_EPS = 1e-6
SAME_ENGINE_SYNC = True
PSUM_READ_SERIALIZE = True


class T:
    def __init__(self, h, sem=None):
        self.h = h
        self.w = None
        self.r = {}
        self.sem = sem
        self.semval = 0

    def __getitem__(self, idx):
        return self.h[idx]


class TV:
    def __init__(self, parent, ap):
        self.p = parent
        self.ap = ap

    @property
    def is_psum(self):
        return getattr(self.p, "is_psum", False)

    @property
    def w(self):
        return self.p.w

    @w.setter
    def w(self, v):
        self.p.w = v

    @property
    def r(self):
        return self.p.r

    @r.setter
    def r(self, v):
        self.p.r = v

    def __getitem__(self, idx):
        return self.ap[idx]


class Sched:
    ENG = ("pe", "act", "dve", "pool", "sp")

    def __init__(self, nc):
        self.nc = nc
        self.prog = {e: [] for e in self.ENG}
        self.sem = {e: nc.alloc_semaphore("sem_" + e) for e in ("pe", "act", "dve", "pool")}
        self.cnt = {e: 0 for e in self.sem}
        self.seen = {e: {} for e in self.ENG}
        self.uid = 0
        self.final = []

    def tile(self, shape, dt, name=None, psum=False, dma=False):
        self.uid += 1
        name = f"{name or 't'}_{self.uid}"
        if psum:
            h = self.nc.alloc_psum_tensor(name, list(shape), dt)
        else:
            h = self.nc.alloc_sbuf_tensor(name, list(shape), dt)
        sem = self.nc.alloc_semaphore("ds_" + name) if dma else None
        t = T(h, sem)
        t.is_psum = psum
        return t

    def _dep(self, eng, ev):
        key, sem, val = ev
        if key == eng and (eng == "pe" or not SAME_ENGINE_SYNC):
            return
        if self.seen[eng].get(key, 0) >= val:
            return
        self.seen[eng][key] = val
        self.prog[eng].append(lambda e, sem=sem, val=val: e.wait_ge(sem, val))

    def _deps(self, eng, reads, writes):
        for t in reads:
            if t.w is not None:
                self._dep(eng, t.w)
            if PSUM_READ_SERIALIZE and getattr(t, "is_psum", False):
                for ev in t.r.values():
                    if ev[0] != eng:
                        self._dep(eng, ev)
        for t in writes:
            if t.w is not None:
                self._dep(eng, t.w)
            for ev in t.r.values():
                self._dep(eng, ev)

    def _mark(self, ev, reads, writes):
        for t in reads:
            t.r[ev[0]] = ev
        for t in writes:
            t.w = ev
            t.r = {}

    def op(self, eng, fn, reads=(), writes=()):
        self._deps(eng, reads, writes)
        self.cnt[eng] += 1
        val = self.cnt[eng]
        sem = self.sem[eng]
        self.prog[eng].append(lambda e, fn=fn, sem=sem: fn(e).then_inc(sem, 1))
        self._mark((eng, sem, val), reads, writes)

    def dma(self, q, out, in_, semtile, reads=(), writes=()):
        self._deps(q, reads, writes)
        semtile.semval += 16
        sem, val = semtile.sem, semtile.semval
        self.prog[q].append(lambda e, out=out, in_=in_, sem=sem: e.dma_start(out=out, in_=in_).then_inc(sem, 16))
        ev = (("d", id(semtile)), sem, val)
        self._mark(ev, reads, writes)
        return ev

    def finish(self, evs):
        for ev in evs:
            self._dep("sp", ev)

    def emit(self):
        nc = self.nc
        with nc.Block() as block:
            @block.tensor
            def _(e):
                for f in self.prog["pe"]:
                    f(e)

            @block.scalar
            def _(e):
                for f in self.prog["act"]:
                    f(e)

            @block.vector
            def _(e):
                for f in self.prog["dve"]:
                    f(e)

            @block.gpsimd
            def _(e):
                for f in self.prog["pool"]:
                    f(e)

            @block.sync
            def _(e):
                for f in self.prog["sp"]:
                    f(e)


class Ring:
    def __init__(self, tiles):
        self.tiles = tiles
        self.i = 0

    def next(self):
        t = self.tiles[self.i % len(self.tiles)]
        self.i += 1
        return t


def mm(s, out_t, out_ap, l_t, l_ap, r_t, r_ap, start, stop):
    s.op("pe", lambda e: e.matmul(out_ap, lhsT=l_ap, rhs=r_ap, start=start, stop=stop),
         reads=[l_t, r_t], writes=[out_t])


class FFNStage:
    def __init__(self, s, NT, wgu_d, wdn_d, lnp_t, ln_idx, consts, ps, resident=False, with_ffn=True):
        self.s, self.NT = s, NT
        self.wgu_d, self.wdn_d = wgu_d, wdn_d
        self.lnp, self.ln_idx = lnp_t, ln_idx
        self.c = consts
        self.ps = ps
        self.resident = resident
        if with_ffn:
            if resident:
                self.wgR = [s.tile([128, 8, 256], BF16, "wgR", dma=True) for _ in range(22)]
                self.wdR = [s.tile([128, 22, 128], BF16, "wdR", dma=True) for _ in range(8)]
                for j in range(22):
                    s.dma("pool", self.wgR[j][:], wgu_d[j], self.wgR[j], writes=[self.wgR[j]])
                for m in range(8):
                    s.dma("pool", self.wdR[m][:], wdn_d[m], self.wdR[m], writes=[self.wdR[m]])
            else:
                self.wg = Ring([s.tile([128, 8, 256], BF16, "wg", dma=True) for _ in range(3)])
                self.wd = Ring([s.tile([128, 22, 128], BF16, "wd", dma=True) for _ in range(2)])
            self.act = s.tile([128, 22, NT], BF16, "act")
            self.sg = Ring([s.tile([128, NT], F32, "sg") for _ in range(2)])
        self.ysq = Ring([s.tile([128, NT], F32, "ysq") for _ in range(2)])
        self.mean = s.tile([128, NT], F32, "mean")
        self.tmp = s.tile([128, NT], F32, "lntmp")
        self.rstd = s.tile([128, NT], F32, "rstd")
        self.d = Ring([s.tile([128, NT], F32, "lnd") for _ in range(2)])

    def run(self, x32, xb, hb):
        s, NT = self.s, self.NT
        ps = self.ps
        for j in range(22):
            if self.resident:
                wg = self.wgR[j]
            else:
                wg = self.wg.next()
                s.dma("pool", wg[:], self.wgu_d[j], wg, writes=[wg])
            pg, pu = ps["g"].next(), ps["u"].next()
            for k in range(8):
                mm(s, pg, pg[:, 0:NT], wg, wg[:, k, 0:128], xb, xb[:, k, :], k == 0, k == 7)
            for k in range(8):
                mm(s, pu, pu[:, 0:NT], wg, wg[:, k, 128:256], xb, xb[:, k, :], k == 0, k == 7)
            sg = self.sg.next()
            s.op("act", lambda e, sg=sg, pg=pg: e.activation(out=sg[:], in_=pg[:, 0:NT], func=AF.Silu),
                 reads=[pg], writes=[sg])
            s.op("dve", lambda e, sg=sg, pu=pu, j=j: e.tensor_tensor(out=self.act[:, j, :], in0=sg[:], in1=pu[:, 0:NT], op=ALU.mult),
                 reads=[sg, pu], writes=[self.act])
        for m in range(8):
            if self.resident:
                wd = self.wdR[m]
            else:
                wd = self.wd.next()
                s.dma("pool", wd[:], self.wdn_d[m], wd, writes=[wd])
            py = ps["y"].next()
            for k in range(22):
                mm(s, py, py[:, 0:NT], wd, wd[:, k, :], self.act, self.act[:, k, :], k == 0, k == 21)
            s.op("dve", lambda e, py=py, m=m: e.scalar_tensor_tensor(
                out=x32[:, m, :], in0=py[:, 0:NT], scalar=0.5 / ALPHA, in1=x32[:, m, :], op0=ALU.mult, op1=ALU.add),
                reads=[py, x32], writes=[x32])
        self.ln(x32, hb, self.ln_idx)

    def ln(self, y, hb, li):
        s, NT, ps = self.s, self.NT, self.ps
        pm, pe_ = ps["m"], ps["e"]
        for m in range(8):
            ysq = self.ysq.next()
            s.op("act", lambda e, ysq=ysq, m=m: e.activation(out=ysq[:], in_=y[:, m, :], func=AF.Square),
                 reads=[y], writes=[ysq])
            mm(s, pm, pm[:, 0:NT], self.c["onesD"], self.c["onesD"][:], y, y[:, m, :], m == 0, m == 7)
            mm(s, pe_, pe_[:, 0:NT], self.c["onesD"], self.c["onesD"][:], ysq, ysq[:], m == 0, m == 7)
        s.op("act", lambda e: e.activation(out=self.mean[:], in_=pm[:, 0:NT], func=AF.Copy), reads=[pm], writes=[self.mean])
        s.op("dve", lambda e: e.tensor_tensor(out=self.tmp[:], in0=self.mean[:], in1=self.mean[:], op=ALU.mult),
             reads=[self.mean], writes=[self.tmp])
        s.op("dve", lambda e: e.tensor_tensor(out=self.tmp[:], in0=pe_[:, 0:NT], in1=self.tmp[:], op=ALU.subtract),
             reads=[pe_, self.tmp], writes=[self.tmp])
        s.op("act", lambda e: e.activation(out=self.tmp[:], in_=self.tmp[:], func=AF.Sqrt, bias=self.c["epsln"][:, 0:1]),
             reads=[self.tmp, self.c["epsln"]], writes=[self.tmp])
        s.op("dve", lambda e: e.reciprocal(out=self.rstd[:], in_=self.tmp[:]), reads=[self.tmp], writes=[self.rstd])
        for m in range(8):
            d = self.d.next()
            s.op("dve", lambda e, d=d, m=m: e.tensor_tensor(out=d[:], in0=y[:, m, :], in1=self.mean[:], op=ALU.subtract),
                 reads=[y, self.mean], writes=[d])
            s.op("dve", lambda e, d=d: e.tensor_tensor(out=d[:], in0=d[:], in1=self.rstd[:], op=ALU.mult),
                 reads=[d, self.rstd], writes=[d])
            s.op("act", lambda e, d=d, m=m: e.activation(
                out=y[:, m, :], in_=d[:], func=AF.Identity,
                scale=self.lnp[:, li, 0, m:m + 1], bias=self.lnp[:, li, 1, m:m + 1]),
                reads=[d, self.lnp], writes=[y])
        if hb is not None:
            s.op("pool", lambda e: e.tensor_copy(out=hb[:], in_=y[:]), reads=[y], writes=[hb])


def make_psum(s):
    banks = [s.tile([128, 512], F32, f"ps{i}", psum=True) for i in range(8)]
    return {"g": Ring(banks[0:2]), "u": Ring(banks[2:4]), "y": Ring(banks[4:6]), "m": banks[6], "e": banks[7],
            "all": banks}


def load_consts(s, nc, cst_d):
    c = {}
    cst = s.tile([128, 3, 128], F32, "cst", dma=True)
    s.dma("sp", cst[:], cst_d, cst, writes=[cst])
    c["cst"] = cst
    onesD = s.tile([128, 128], F32, "onesD")
    s.op("dve", lambda e: e.tensor_copy(out=onesD[:], in_=cst[:, 0, :]), reads=[cst], writes=[onesD])
    c["onesD"] = onesD
    eps = s.tile([128, 1], F32, "epsln")
    s.op("dve", lambda e: e.memset(eps[:], LN_EPS / (ALPHA * ALPHA)), writes=[eps])
    c["epsln"] = eps
    return c


def build_A(NTOK, NT=256, proj=True, ln_idx=0):
    nc = bass.Bass("TRN2", target_bir_lowering=False)
    xT = nc.dram_tensor("xT", [8, 128, NTOK], F32, kind="ExternalInput").ap()
    wgu = nc.dram_tensor("wgu", [22, 128, 8, 256], F32, kind="ExternalInput").ap()
    wdn = nc.dram_tensor("wdn", [8, 128, 22, 128], F32, kind="ExternalInput").ap()
    if proj:
        wqkv = nc.dram_tensor("wqkv", [12, 128, 8, 128], F32, kind="ExternalInput").ap()
        wba = nc.dram_tensor("wba", [128, 8, 8], F32, kind="ExternalInput").ap()
    lnp_d = nc.dram_tensor("lnp", [128, 3, 2, 8], F32, kind="ExternalInput").ap()
    cst_d = nc.dram_tensor("cst", [128, 3, 128], F32, kind="ExternalInput").ap()
    hT = nc.dram_tensor("hT", [8, 128, NTOK], F32, kind="ExternalOutput").ap()
    if proj:
        qkvT = nc.dram_tensor("qkvT", [12, 128, NTOK], F32, kind="ExternalOutput").ap()
        baT = nc.dram_tensor("baT", [8, NTOK], F32, kind="ExternalOutput").ap()

    s = Sched(nc)
    c = load_consts(s, nc, cst_d)
    lnp = s.tile([128, 3, 2, 8], F32, "lnp", dma=True)
    s.dma("sp", lnp[:], lnp_d, lnp, writes=[lnp])
    ps = make_psum(s)
    ffn = FFNStage(s, NT, wgu, wdn, lnp, ln_idx, c, ps, resident=True)
    x32r = Ring([s.tile([128, 8, NT], F32, "x32", dma=True) for _ in range(2)])
    xb = s.tile([128, 8, NT], BF16, "xb")
    if proj:
        wb = s.tile([128, 8, 8], BF16, "wba", dma=True)
        s.dma("pool", wb[:], wba, wb, writes=[wb])
        hb = s.tile([128, 8, NT], BF16, "hb")
        wqR = [s.tile([128, 8, 128], BF16, "wqR", dma=True) for _ in range(12)]
        for m in range(12):
            s.dma("pool", wqR[m][:], wqkv[m], wqR[m], writes=[wqR[m]])
        qo = Ring([s.tile([128, NT], F32, "qo", dma=True) for _ in range(2)])
        bo = s.tile([8, NT], F32, "bo", dma=True)
    else:
        hb = None
    stores = []
    for t in range(NTOK // NT):
        cols = slice(t * NT, (t + 1) * NT)
        x32 = x32r.next()
        s.dma("sp", x32[:], xT[:, :, cols].rearrange("k p n -> p k n"), x32, writes=[x32])
        s.op("act", lambda e, x32=x32: e.activation(out=xb[:], in_=x32[:], func=AF.Copy), reads=[x32], writes=[xb])
        ffn.run(x32, xb, hb)
        stores.append(s.dma("sp", hT[:, :, cols].rearrange("k p n -> p k n"), x32[:], x32, reads=[x32]))
        if not proj:
            continue
        for m in range(12):
            w = wqR[m]
            pq = ps["g"].next()
            for k in range(8):
                mm(s, pq, pq[:, 0:NT], w, w[:, k, :], hb, hb[:, k, :], k == 0, k == 7)
            q = qo.next()
            s.op("act", lambda e, q=q, pq=pq: e.activation(out=q[:], in_=pq[:, 0:NT], func=AF.Copy), reads=[pq], writes=[q])
            stores.append(s.dma("sp", qkvT[m][:, cols], q[:], q, reads=[q]))
        pb = ps["u"].next()
        for k in range(8):
            mm(s, pb, pb[0:8, 0:NT], wb, wb[:, k, :], hb, hb[:, k, :], k == 0, k == 7)
        s.op("act", lambda e, pb=pb: e.activation(out=bo[:], in_=pb[0:8, 0:NT], func=AF.Copy), reads=[pb], writes=[bo])
        stores.append(s.dma("sp", baT[:, cols], bo[:], bo, reads=[bo]))
    s.finish(stores)
    s.emit()
    return nc


def lay_w_kmc(w, ncols_chunk=128):
    K, M = w.shape
    return np.ascontiguousarray(w.reshape(K // 128, 128, M // ncols_chunk, ncols_chunk).transpose(2, 1, 0, 3))


def lay_wgu(w):
    g = lay_w_kmc(w[:, :D_FF])
    u = lay_w_kmc(w[:, D_FF:])
    return np.ascontiguousarray(np.concatenate([g, u], axis=3))


def lay_lnp(g, b):
    a = np.stack([g.reshape(3, 8, 128), b.reshape(3, 8, 128)], axis=1)
    return np.ascontiguousarray(a.transpose(3, 0, 1, 2))


def make_cst():
    c = np.zeros((128, 3, 128), np.float32)
    c[:, 0, :] = 1.0 / D_MODEL
    c[:, 1, :] = np.eye(128, dtype=np.float32)
    c[:, 2, :] = 1.0
    return c


NEG = -1e30
B_DEBUG_STAGE = None
B_VAR = 0


def make_cstB():
    idx = np.arange(128)
    same = (idx[:, None] // 64) == (idx[None, :] // 64)
    c = np.zeros((128, 7, 128), np.float32)
    c[:, 0, :] = np.eye(128)
    c[:, 1, :] = 1.0
    c[:, 2, :] = (same & (idx[:, None] <= idx[None, :]))
    c[:, 3, :] = np.where(same & (idx[:, None] > idx[None, :]), 0.0, NEG)
    c[:, 4, :] = np.where(same & (idx[None, :] >= idx[:, None]), 0.0, NEG)
    c[63, 5, :] = 1.0
    c[127, 6, :] = 1.0
    col = np.zeros((128, 2), np.float32)
    col[64:, 0] = NEG
    col[:64, 1] = NEG
    return c, col


def build_B(S_LEN, NT=512):
    nc = bass.Bass("TRN2", target_bir_lowering=False)
    NBLK = S_LEN // 128
    qkvp = nc.dram_tensor("qkvp", [3, 128, S_LEN], F32, kind="ExternalInput").ap()
    bcol_d = nc.dram_tensor("bcol", [128, NBLK], F32, kind="ExternalInput").ap()
    acol_d = nc.dram_tensor("acol", [128, NBLK], F32, kind="ExternalInput").ap()
    smallp_d = nc.dram_tensor("smallp", [128, 32], F32, kind="ExternalInput").ap()
    cst_d = nc.dram_tensor("cstB", [128, 7, 128], F32, kind="ExternalInput").ap()
    onT = nc.dram_tensor("onT", [128, S_LEN], F32, kind="ExternalOutput").ap()

    s = Sched(nc)
    cst = s.tile([128, 7, 128], F32, "cstB", dma=True)
    s.dma("sp", cst[:], cst_d, cst, writes=[cst])
    smallp = s.tile([128, 32], F32, "smallp", dma=True)
    s.dma("sp", smallp[:], smallp_d, smallp, writes=[smallp])
    ccol = TV(smallp, smallp[:, 15:17])
    bcol = s.tile([128, NBLK], F32, "bcol", dma=True)
    s.dma("sp", bcol[:], bcol_d, bcol, writes=[bcol])
    acol = s.tile([128, NBLK], F32, "acol", dma=True)
    s.dma("sp", acol[:], acol_d, acol, writes=[acol])
    convw = TV(smallp, smallp[:, 0:12].rearrange("p (w j) -> p w j", j=4))
    scal = TV(smallp, smallp[:, 12:15])
    I_, ONES, TRI, MBS, MBT, SELA, SELB = [cst[:, i, :] for i in range(7)]

    banks = [s.tile([128, 512], F32, f"psB{i}", psum=True) for i in range(8)]
    big = Ring(banks[0:2])
    small = Ring([TV(bk, bk[:, q * 128:(q + 1) * 128]) for q in range(4) for bk in banks[3:8]])
    po_r = Ring([TV(banks[2], banks[2][:, q * 128:(q + 1) * 128]) for q in range(4)])

    def V(eng, fn, reads, writes):
        s.op(eng, fn, reads=reads, writes=writes)

    def mmf(out_t, out_ap, l_t, l_ap, r_t, r_ap, start=True, stop=True):
        mm(s, out_t, out_ap, l_t, l_ap, r_t, r_ap, start, stop)

    def col_tile(name, n=NBLK):
        return s.tile([128, n], F32, name)

    one_c = s.tile([128, 1], F32, "one_c")
    V("dve", lambda e: e.memset(one_c[:], 1.0), [], [one_c])
    eps_c = s.tile([128, 1], F32, "eps_c")
    V("dve", lambda e: e.memset(eps_c[:], RMS_EPS), [], [eps_c])
    ones128 = s.tile([128, 128], F32, "ones128")
    V("dve", lambda e: e.tensor_scalar(out=ones128[:], in0=ONES, scalar1=1.0 / 128, scalar2=None, op0=ALU.mult), [cst], [ones128])
    beta = col_tile("beta")
    V("act", lambda e: e.activation(out=beta[:], in_=bcol[:], func=AF.Sigmoid), [bcol], [beta])
    xg = col_tile("xg")
    V("dve", lambda e: e.tensor_scalar(out=xg[:], in0=acol[:], scalar1=scal[:, 1:2], scalar2=None, op0=ALU.add), [acol, scal], [xg])
    ax = col_tile("ax")
    V("dve", lambda e: e.tensor_scalar(out=ax[:], in0=xg[:], scalar1=-1.0, scalar2=None, op0=ALU.mult), [xg], [ax])
    V("dve", lambda e: e.tensor_tensor(out=ax[:], in0=ax[:], in1=xg[:], op=ALU.max), [ax, xg], [ax])
    V("act", lambda e: e.activation(out=ax[:], in_=ax[:], func=AF.Exp, scale=-1.0), [ax], [ax])
    V("act", lambda e: e.activation(out=ax[:], in_=ax[:], func=AF.Ln, bias=one_c[:, 0:1]), [ax, one_c], [ax])
    V("dve", lambda e: e.tensor_scalar(out=xg[:], in0=xg[:], scalar1=0.0, scalar2=None, op0=ALU.max), [xg], [xg])
    V("dve", lambda e: e.tensor_tensor(out=xg[:], in0=xg[:], in1=ax[:], op=ALU.add), [xg, ax], [xg])
    nea = s.tile([128, 1], F32, "nea")
    V("act", lambda e: e.activation(out=nea[:], in_=scal[:, 0:1], func=AF.Exp), [scal], [nea])
    V("dve", lambda e: e.tensor_scalar(out=nea[:], in0=nea[:], scalar1=-1.0, scalar2=None, op0=ALU.mult), [nea], [nea])
    gc = col_tile("gc")
    V("dve", lambda e: e.tensor_scalar(out=gc[:], in0=xg[:], scalar1=nea[:, 0:1], scalar2=None, op0=ALU.mult), [xg, nea], [gc])
    gcum, ngcum, bexpg, colA, colB, eglA, eglB = [col_tile(n) for n in ("gcum", "ngcum", "bexpg", "colA", "colB", "eglA", "eglB")]
    for c0 in range(0, NBLK, 512):
        c1 = min(NBLK, c0 + 512)
        w = c1 - c0
        pb = big.next()
        mmf(pb, pb[:, 0:w], cst, TRI, gc, gc[:, c0:c1])
        V("act", lambda e, pb=pb, c0=c0, c1=c1, w=w: e.activation(out=gcum[:, c0:c1], in_=pb[:, 0:w], func=AF.Copy), [pb], [gcum])
    V("dve", lambda e: e.tensor_scalar(out=ngcum[:], in0=gcum[:], scalar1=-1.0, scalar2=None, op0=ALU.mult), [gcum], [ngcum])
    V("act", lambda e: e.activation(out=bexpg[:], in_=gcum[:], func=AF.Exp), [gcum], [bexpg])
    V("dve", lambda e: e.tensor_tensor(out=bexpg[:], in0=bexpg[:], in1=beta[:], op=ALU.mult), [bexpg, beta], [bexpg])
    for (SEL, col, egl, mi) in ((SELA, colA, eglA, 0), (SELB, colB, eglB, 1)):
        for c0 in range(0, NBLK, 512):
            c1 = min(NBLK, c0 + 512)
            w = c1 - c0
            pb = big.next()
            mmf(pb, pb[:, 0:w], cst, SEL, gcum, gcum[:, c0:c1])
            V("act", lambda e, pb=pb, egl=egl, c0=c0, c1=c1, w=w: e.activation(out=egl[:, c0:c1], in_=pb[:, 0:w], func=AF.Exp), [pb], [egl])
            V("dve", lambda e, pb=pb, col=col, c0=c0, c1=c1, w=w: e.tensor_tensor(out=col[:, c0:c1], in0=pb[:, 0:w], in1=gcum[:, c0:c1], op=ALU.subtract),
              [pb, gcum], [col])
        V("act", lambda e, col=col, mi=mi: e.activation(out=col[:], in_=col[:], func=AF.Exp, bias=ccol[:, mi:mi + 1]), [col, ccol], [col])

    S_r = [s.tile([128, 128], F32, f"S{i}") for i in range(2)]
    V("dve", lambda e: e.memset(S_r[0][:], 0.0), [], [S_r[0]])
    state = {"S": 0}
    NTILE = S_LEN // NT
    NB = NT // 128

    def alloc_set():
        d = {}
        d["pre"] = s.tile([128, 3, NT + 8], F32, "pre", dma=True)
        d["cv"] = [s.tile([128, NT], F32, f"cv{i}") for i in range(3)]
        d["sq"] = s.tile([128, NT], F32, "sq")
        d["rin"] = s.tile([128, NT], F32, "rin")
        d["qT"] = s.tile([128, NT], F32, "qT")
        d["kT"] = s.tile([128, NT], F32, "kT")
        d["kT2"] = s.tile([128, NT], F32, "kT2")
        for nm in ("dGn", "dGp", "ES", "E2", "EG", "A", "Bm", "N", "A2", "B2", "RHSv", "RHSw", "ktA", "ktB", "u", "wT", "qkT", "qdT"):
            d[nm] = [s.tile([128, 128], F32, f"{nm}{i}") for i in range(NB)]
        d["A2b"] = [s.tile([128, 128], F32, f"A2b{i}") for i in range(NB)]
        d["B2b"] = [s.tile([128, 128], F32, f"B2b{i}") for i in range(NB)]
        return d

    sets = [alloc_set(), alloc_set()]
    vn_r = Ring([s.tile([128, 128], F32, f"vn{i}") for i in range(2)])
    oT_r = Ring([s.tile([128, NT], F32, f"oT{i}", dma=True) for i in range(2)])
    osq = s.tile([128, NT], F32, "osq")
    orin = s.tile([128, NT], F32, "orin")
    stores = []

    def prep(t, d):
        c0 = t * NT
        pre = d["pre"]
        if t == 0:
            V("dve", lambda e: e.memset(pre[:, :, 0:8], 0.0), [], [pre])
            s.dma("sp", pre[:, :, 8:NT + 8], qkvp[:, :, 0:NT].rearrange("w p n -> p w n"), pre, writes=[pre])
        else:
            s.dma("sp", pre[:], qkvp[:, :, c0 - 8:c0 + NT].rearrange("w p n -> p w n"), pre, writes=[pre])
        for w in range(3):
            cv = d["cv"][w]
            V("dve", lambda e, cv=cv, w=w: e.tensor_scalar(out=cv[:], in0=pre[:, w, 5:5 + NT], scalar1=convw[:, w, 0:1], scalar2=None, op0=ALU.mult),
              [pre, convw], [cv])
            for j in range(1, 4):
                V("dve", lambda e, cv=cv, w=w, j=j: e.scalar_tensor_tensor(
                    out=cv[:], in0=pre[:, w, 5 + j:5 + j + NT], scalar=convw[:, w, j:j + 1], in1=cv[:], op0=ALU.mult, op1=ALU.add),
                  [pre, convw, cv], [cv])
            V("act", lambda e, cv=cv: e.activation(out=cv[:], in_=cv[:], func=AF.Silu), [cv], [cv])
        yield
        for w, dst, sc in ((0, d["qT"], 128 ** -0.5), (1, d["kT"], 1.0)):
            cv = d["cv"][w]
            V("act", lambda e, cv=cv: e.activation(out=d["sq"][:], in_=cv[:], func=AF.Square), [cv], [d["sq"]])
            pb = big.next()
            mmf(pb, pb[:, 0:NT], cst, ONES, d["sq"], d["sq"][:])
            V("act", lambda e, pb=pb: e.activation(out=d["rin"][:], in_=pb[:, 0:NT], func=AF.Sqrt, bias=eps_c[:, 0:1]), [pb, eps_c], [d["rin"]])
            V("dve", lambda e: e.reciprocal(out=d["rin"][:], in_=d["rin"][:]), [d["rin"]], [d["rin"]])
            V("dve", lambda e, cv=cv, dst=dst, sc=sc: e.scalar_tensor_tensor(
                out=dst[:], in0=cv[:], scalar=sc, in1=d["rin"][:], op0=ALU.mult, op1=ALU.mult), [cv, d["rin"]], [dst])
        vT = d["cv"][2]
        qT, kT = d["qT"], d["kT"]
        kT2 = d["kT2"]
        V("act", lambda e: e.activation(out=kT2[:], in_=kT[:], func=AF.Copy), [kT], [kT2])
        yield
        blks = range(NB)

        def bs(i):
            return slice(i * 128, (i + 1) * 128)
        for i in blks:
            gb = t * NB + i
            V("dve", lambda e, i=i, gb=gb: e.tensor_scalar(out=d["dGn"][i][:], in0=I_, scalar1=ngcum[:, gb:gb + 1], scalar2=None, op0=ALU.mult),
              [cst, ngcum], [d["dGn"][i]])
            V("dve", lambda e, i=i, gb=gb: e.tensor_scalar(out=d["dGp"][i][:], in0=I_, scalar1=gcum[:, gb:gb + 1], scalar2=None, op0=ALU.mult),
              [cst, gcum], [d["dGp"][i]])
        for i in blks:
            gb = t * NB + i
            p1 = small.next()
            mmf(p1, p1[:], cst, ONES, d["dGn"][i], d["dGn"][i][:], True, False)
            mmf(p1, p1[:], cst, I_, cst, MBS, False, True)
            V("act", lambda e, i=i, gb=gb, p1=p1: e.activation(out=d["ES"][i][:], in_=p1[:], func=AF.Exp, bias=gcum[:, gb:gb + 1]),
              [p1, gcum], [d["ES"][i]])
            p2 = small.next()
            mmf(p2, p2[:], cst, ONES, d["dGp"][i], d["dGp"][i][:], True, False)
            mmf(p2, p2[:], cst, I_, cst, MBT, False, True)
            V("act", lambda e, i=i, gb=gb, p2=p2: e.activation(out=d["E2"][i][:], in_=p2[:], func=AF.Exp, bias=ngcum[:, gb:gb + 1]),
              [p2, ngcum], [d["E2"][i]])
            p3 = small.next()
            mmf(p3, p3[:], cst, ONES, d["dGp"][i], d["dGp"][i][:])
            V("act", lambda e, i=i, p3=p3: e.activation(out=d["EG"][i][:], in_=p3[:], func=AF.Exp), [p3], [d["EG"][i]])
        yield
        for i in blks:
            gb = t * NB + i
            pk = small.next()
            mmf(pk, pk[:], kT, kT[:, bs(i)], kT2, kT2[:, bs(i)])
            V("act", lambda e, i=i, gb=gb, pk=pk: e.activation(out=d["A"][i][:], in_=pk[:], func=AF.Identity, scale=beta[:, gb:gb + 1]),
              [pk, beta], [d["A"][i]])
            V("dve", lambda e, i=i: e.tensor_tensor(out=d["A"][i][:], in0=d["A"][i][:], in1=d["ES"][i][:], op=ALU.mult),
              [d["A"][i], d["ES"][i]], [d["A"][i]])
        yield
        for i in blks:
            pt = small.next()
            mmf(pt, pt[:], d["A"][i], d["A"][i][:], cst, I_)
            V("act", lambda e, i=i, pt=pt: e.activation(out=d["Bm"][i][:], in_=pt[:], func=AF.Copy), [pt], [d["Bm"][i]])
            if B_VAR != 1:
                V("dve", lambda e, i=i, pt=pt: e.scalar_tensor_tensor(
                    out=d["N"][i][:], in0=pt[:], scalar=-1.0, in1=I_, op0=ALU.mult, op1=ALU.add), [pt, cst], [d["N"][i]])
        yield
        curA = [d["A"][i] for i in blks]
        curB = [d["Bm"][i] for i in blks]
        for lvl in range(1, 6):
            nA = d["A2"] if lvl % 2 == 1 else d["A2b"]
            nB = d["B2"] if lvl % 2 == 1 else d["B2b"]
            for i in blks:
                pa = small.next()
                mmf(pa, pa[:], curB[i], curB[i][:], curA[i], curA[i][:])
                V("act", lambda e, i=i, pa=pa, nA=nA: e.activation(out=nA[i][:], in_=pa[:], func=AF.Copy), [pa], [nA[i]])
                if lvl < 5:
                    pbb = small.next()
                    mmf(pbb, pbb[:], curA[i], curA[i][:], curB[i], curB[i][:])
                    V("dve", lambda e, i=i, pbb=pbb, nB=nB: e.tensor_copy(out=nB[i][:], in_=pbb[:]), [pbb], [nB[i]])
            for i in blks:
                pn = small.next()
                mmf(pn, pn[:], nA[i], nA[i][:], d["N"][i], d["N"][i][:])
                V("dve", lambda e, i=i, pn=pn: e.tensor_tensor(out=d["N"][i][:], in0=pn[:], in1=d["N"][i][:], op=ALU.add),
                  [pn, d["N"][i]], [d["N"][i]])
            curA = [nA[i] for i in blks]
            curB = [nB[i] for i in blks]
            yield
        for i in blks:
            gb = t * NB + i
            pk = small.next()
            mmf(pk, pk[:], kT, kT[:, bs(i)], cst, I_)
            V("dve", lambda e, i=i, gb=gb, pk=pk: e.tensor_scalar(out=d["RHSw"][i][:], in0=pk[:], scalar1=bexpg[:, gb:gb + 1], scalar2=None, op0=ALU.mult),
              [pk, bexpg], [d["RHSw"][i]])
            V("dve", lambda e, i=i, gb=gb, pk=pk: e.tensor_scalar(out=d["ktA"][i][:], in0=pk[:], scalar1=colA[:, gb:gb + 1], scalar2=None, op0=ALU.mult),
              [pk, colA], [d["ktA"][i]])
            V("dve", lambda e, i=i, gb=gb, pk=pk: e.tensor_scalar(out=d["ktB"][i][:], in0=pk[:], scalar1=colB[:, gb:gb + 1], scalar2=None, op0=ALU.mult),
              [pk, colB], [d["ktB"][i]])
            pv = small.next()
            mmf(pv, pv[:], vT, vT[:, bs(i)], cst, I_)
            V("dve", lambda e, i=i, gb=gb, pv=pv: e.tensor_scalar(out=d["RHSv"][i][:], in0=pv[:], scalar1=beta[:, gb:gb + 1], scalar2=None, op0=ALU.mult),
              [pv, beta], [d["RHSv"][i]])
        yield
        for i in blks:
            pu = small.next()
            mmf(pu, pu[:], d["N"][i], d["N"][i][:], d["RHSv"][i], d["RHSv"][i][:])
            V("act", lambda e, i=i, pu=pu: e.activation(out=d["u"][i][:], in_=pu[:], func=AF.Copy), [pu], [d["u"][i]])
            pw = small.next()
            mmf(pw, pw[:], d["RHSw"][i], d["RHSw"][i][:], d["N"][i], d["N"][i][:])
            V("act", lambda e, i=i, pw=pw: e.activation(out=d["wT"][i][:], in_=pw[:], func=AF.Copy), [pw], [d["wT"][i]])
            pq = small.next()
            mmf(pq, pq[:], kT, kT[:, bs(i)], qT, qT[:, bs(i)])
            V("dve", lambda e, i=i, pq=pq: e.tensor_tensor(out=d["qkT"][i][:], in0=pq[:], in1=d["E2"][i][:], op=ALU.mult),
              [pq, d["E2"][i]], [d["qkT"][i]])
            V("dve", lambda e, i=i: e.tensor_tensor(out=d["qdT"][i][:], in0=qT[:, bs(i)], in1=d["EG"][i][:], op=ALU.mult),
              [qT, d["EG"][i]], [d["qdT"][i]])
        yield

    def recur(t, d):
        oT = oT_r.next()
        for i in range(NB):
            gb = t * NB + i
            po = po_r.next()
            for half, kt, egl in ((0, d["ktA"][i], eglA), (1, d["ktB"][i], eglB)):
                S = S_r[state["S"]]
                S2 = S_r[1 - state["S"]]
                hs = slice(half * 64, half * 64 + 64)
                p1 = small.next()
                mmf(p1, p1[:], d["wT"][i], d["wT"][i][:], S, S[:])
                vn = vn_r.next()
                V("dve", lambda e, i=i, p1=p1, vn=vn: e.tensor_tensor(out=vn[:], in0=d["u"][i][:], in1=p1[:], op=ALU.subtract),
                  [d["u"][i], p1], [vn])
                mmf(po, po[:, hs], S, S[:], d["qdT"][i], d["qdT"][i][:, hs], True, False)
                mmf(po, po[:, hs], vn, vn[:], d["qkT"][i], d["qkT"][i][:, hs], False, True)
                p2 = small.next()
                mmf(p2, p2[:], kt, kt[:], vn, vn[:])
                V("dve", lambda e, S=S, S2=S2, p2=p2, egl=egl, gb=gb: e.scalar_tensor_tensor(
                    out=S2[:], in0=S[:], scalar=egl[:, gb:gb + 1], in1=p2[:], op0=ALU.mult, op1=ALU.add),
                  [S, egl, p2], [S2])
                state["S"] = 1 - state["S"]
                yield
            V("act", lambda e, i=i, po=po, oT=oT: e.activation(out=oT[:, i * 128:(i + 1) * 128], in_=po[:], func=AF.Copy), [po], [oT])
        V("act", lambda e: e.activation(out=osq[:], in_=oT[:], func=AF.Square), [oT], [osq])
        pb = big.next()
        mmf(pb, pb[:, 0:NT], ones128, ones128[:], osq, osq[:])
        V("act", lambda e, pb=pb: e.activation(out=orin[:], in_=pb[:, 0:NT], func=AF.Sqrt, bias=eps_c[:, 0:1]), [pb, eps_c], [orin])
        V("dve", lambda e: e.reciprocal(out=orin[:], in_=orin[:]), [orin], [orin])
        V("dve", lambda e, oT=oT: e.scalar_tensor_tensor(out=oT[:], in0=oT[:], scalar=scal[:, 2:3], in1=orin[:], op0=ALU.mult, op1=ALU.mult),
          [oT, scal, orin], [oT])
        stores.append(s.dma("sp", onT[:, t * NT:(t + 1) * NT], oT[:], oT, reads=[oT]))
        yield

    def drain(g):
        for _ in g:
            pass

    if B_DEBUG_STAGE is not None:
        g = prep(0, sets[0])
        for _ in range(B_DEBUG_STAGE):
            next(g, None)
        oT = oT_r.next()
        V("dve", lambda e: e.memset(oT[:], 1.0), [], [oT])
        stores.append(s.dma("sp", onT[:, 0:NT], oT[:], oT, reads=[oT]))
        s.finish(stores)
        s.emit()
        return nc
    drain(prep(0, sets[0]))
    for t in range(NTILE):
        r = recur(t, sets[t % 2])
        p = prep(t + 1, sets[(t + 1) % 2]) if t + 1 < NTILE else iter(())
        rd = pd = False
        while not (rd and pd):
            if not pd:
                try:
                    next(p)
                except StopIteration:
                    pd = True
            if not rd:
                try:
                    next(r)
                except StopIteration:
                    rd = True
    s.finish(stores)
    s.emit()
    return nc


NCH_C = 37


def build_C(NTOK, NT=256):
    nc = bass.Bass("TRN2", target_bir_lowering=False)
    NQB = NT // 128
    hT = nc.dram_tensor("hT", [8, 128, 128 + NTOK], F32, kind="ExternalInput").ap()
    odn_d = nc.dram_tensor("odn", [4, 128, NTOK], F32, kind="ExternalInput").ap()
    memT = nc.dram_tensor("memT", [8, 128, N_MEM], F32, kind="ExternalInput").ap()
    wc_d = nc.dram_tensor("wc", [NCH_C, 128, 8, 128], F32, kind="ExternalInput").ap()
    wv_d = nc.dram_tensor("wv", [128, 8, 128], F32, kind="ExternalInput").ap()
    wmk_d = nc.dram_tensor("wmk", [4, 128, 8, 128], F32, kind="ExternalInput").ap()
    wmv_d = nc.dram_tensor("wmv", [128, 8, 512], F32, kind="ExternalInput").ap()
    wbr_d = nc.dram_tensor("wbr", [24, 128, 4, 128], F32, kind="ExternalInput").ap()
    wo_d = nc.dram_tensor("wo", [8, 128, 8, 128], F32, kind="ExternalInput").ap()
    lnp_d = nc.dram_tensor("lnp", [128, 3, 2, 8], F32, kind="ExternalInput").ap()
    mlnp_d = nc.dram_tensor("mlnp", [128, 1, 2, 8], F32, kind="ExternalInput").ap()
    cst_d = nc.dram_tensor("cst", [128, 3, 128], F32, kind="ExternalInput").ap()
    msk_d = nc.dram_tensor("msk", [128, 2, 512], F32, kind="ExternalInput").ap()
    sink_d = nc.dram_tensor("sinkc", [128, 4], F32, kind="ExternalInput").ap()
    outT = nc.dram_tensor("outT", [8, 128, NTOK], F32, kind="ExternalOutput").ap()

    s = Sched(nc)
    c = load_consts(s, nc, cst_d)
    cst = c["cst"]
    lnp = s.tile([128, 3, 2, 8], F32, "lnp", dma=True)
    s.dma("sp", lnp[:], lnp_d, lnp, writes=[lnp])
    mlnp = s.tile([128, 1, 2, 8], F32, "mlnp", dma=True)
    s.dma("sp", mlnp[:], mlnp_d, mlnp, writes=[mlnp])
    msk = s.tile([128, 2, 512], F32, "msk", dma=True)
    s.dma("sp", msk[:], msk_d, msk, writes=[msk])
    sinkc = s.tile([128, 4], F32, "sinkc", dma=True)
    s.dma("sp", sinkc[:], sink_d, sinkc, writes=[sinkc])
    esink = s.tile([128, 4], F32, "esink")
    s.op("act", lambda e: e.activation(out=esink[:], in_=sinkc[:], func=AF.Exp), reads=[sinkc], writes=[esink])
    onesb = s.tile([128, 128], BF16, "onesb")
    s.op("dve", lambda e: e.tensor_copy(out=onesb[:], in_=cst[:, 2, :]), reads=[cst], writes=[onesb])
    onespad = s.tile([128, 2, 128], BF16, "onespad")
    s.op("dve", lambda e: e.memset(onespad[:], 0.0), writes=[onespad])
    for pos in range(2):
        s.op("dve", lambda e, pos=pos: e.memset(onespad[:, pos, pos * 64:pos * 64 + 64], 1.0), writes=[onespad])
    ps = make_psum(s)
    pring = Ring(ps["all"][0:6])
    ffn = FFNStage(s, NT, None, None, lnp, 1, c, ps, with_ffn=False)
    ybuf = s.tile([128, 8, NT], F32, "ybuf", dma=True)

    def V(eng, fn, reads, writes):
        s.op(eng, fn, reads=reads, writes=writes)

    wi = Ring([s.tile([128, 8, 128], BF16, "wi", dma=True) for _ in range(3)])
    wcR = [s.tile([128, 8, 128], BF16, "wcR", dma=True) for _ in range(NCH_C)]
    for ci in [8] + [i for i in range(NCH_C) if i != 8]:
        s.dma("pool", wcR[ci][:], wc_d[ci], wcR[ci], writes=[wcR[ci]])
    wbr = Ring([s.tile([128, 4, 128], BF16, "wbr", dma=True) for _ in range(3)])

    merged = s.tile([128, 8, NT], F32, "merged", dma=True)
    mem32 = merged if NT == N_MEM else s.tile([128, 8, N_MEM], F32, "mem32", dma=True)
    s.dma("sp", mem32[:], memT.rearrange("k p n -> p k n"), mem32, writes=[mem32])
    memb = s.tile([128, 8, N_MEM], BF16, "memb")
    mst = [s.tile([128, N_MEM], F32, f"mst{i}") for i in range(4)]
    pm, pe_ = ps["m"], ps["e"]
    for m in range(8):
        V("act", lambda e, m=m: e.activation(out=mst[0][:], in_=mem32[:, m, :], func=AF.Square), [mem32], [mst[0]])
        mm(s, pm, pm[:, 0:N_MEM], c["onesD"], c["onesD"][:], mem32, mem32[:, m, :], m == 0, m == 7)
        mm(s, pe_, pe_[:, 0:N_MEM], c["onesD"], c["onesD"][:], mst[0], mst[0][:], m == 0, m == 7)
    eps1 = s.tile([128, 1], F32, "eps1")
    V("dve", lambda e: e.memset(eps1[:], LN_EPS), [], [eps1])
    V("act", lambda e: e.activation(out=mst[1][:], in_=pm[:, 0:N_MEM], func=AF.Copy), [pm], [mst[1]])
    V("dve", lambda e: e.tensor_tensor(out=mst[2][:], in0=mst[1][:], in1=mst[1][:], op=ALU.mult), [mst[1]], [mst[2]])
    V("dve", lambda e: e.tensor_tensor(out=mst[2][:], in0=pe_[:, 0:N_MEM], in1=mst[2][:], op=ALU.subtract), [pe_, mst[2]], [mst[2]])
    V("act", lambda e: e.activation(out=mst[2][:], in_=mst[2][:], func=AF.Sqrt, bias=eps1[:, 0:1]), [mst[2], eps1], [mst[2]])
    V("dve", lambda e: e.reciprocal(out=mst[2][:], in_=mst[2][:]), [mst[2]], [mst[2]])
    for m in range(8):
        V("dve", lambda e, m=m: e.tensor_tensor(out=mst[3][:], in0=mem32[:, m, :], in1=mst[1][:], op=ALU.subtract), [mem32, mst[1]], [mst[3]])
        V("dve", lambda e: e.tensor_tensor(out=mst[3][:], in0=mst[3][:], in1=mst[2][:], op=ALU.mult), [mst[3], mst[2]], [mst[3]])
        V("act", lambda e, m=m: e.activation(out=memb[:, m, :], in_=mst[3][:], func=AF.Identity,
                                             scale=mlnp[:, 0, 0, m:m + 1], bias=mlnp[:, 0, 1, m:m + 1]), [mst[3], mlnp], [memb])
    kmemT = s.tile([128, 4, N_MEM], BF16, "kmemT")
    for h in range(4):
        w = wi.next()
        s.dma("pool", w[:], wmk_d[h], w, writes=[w])
        p = pring.next()
        for k in range(8):
            mm(s, p, p[:, 0:N_MEM], w, w[:, k, :], memb, memb[:, k, :], k == 0, k == 7)
        V("act", lambda e, h=h, p=p: e.activation(out=kmemT[:, h, :], in_=p[:, 0:N_MEM], func=AF.Copy), [p], [kmemT])
    wmv = s.tile([128, 8, 512], BF16, "wmv", dma=True)
    s.dma("pool", wmv[:], wmv_d, wmv, writes=[wmv])
    vmem = s.tile([128, 2, 512], BF16, "vmem")
    for mc in range(2):
        p = pring.next()
        for k in range(8):
            mm(s, p, p[:], memb, memb[:, k, mc * 128:(mc + 1) * 128], wmv, wmv[:, k, :], k == 0, k == 7)
        V("act", lambda e, mc=mc, p=p: e.activation(out=vmem[:, mc, :], in_=p[:], func=AF.Copy), [p], [vmem])
    wvb = s.tile([128, 8, 128], BF16, "wvb", dma=True)
    s.dma("pool", wvb[:], wv_d, wvb, writes=[wvb])

    NSLOT = 4
    kT = s.tile([128, NSLOT, 128], BF16, "kTs")
    vpad = s.tile([128, NSLOT, 4, 128], BF16, "vpad")
    V("dve", lambda e: e.memset(vpad[:], 0.0), [], [vpad])
    h32 = s.tile([128, 8, NT], F32, "h32", dma=True)
    hb = s.tile([128, 8, NT], BF16, "hb")
    hh32 = s.tile([128, 8, 128], F32, "hh32", dma=True)
    hhb = s.tile([128, 8, 128], BF16, "hhb")
    zs = s.tile([128, 4, NT], F32, "zs")
    swq = s.tile([128, 4, NT], BF16, "swq")
    xaq = s.tile([128, 4, NT], BF16, "xaq")
    gsb = Ring([s.tile([128, NT], F32, f"gsb{i}") for i in range(3)])
    odn = s.tile([128, 4, NT], F32, "odn", dma=True)
    br = [s.tile([128, 4, NT], BF16, f"br{i}") for i in range(3)]
    pexp = Ring([s.tile([128, 512], F32, f"pexp{i}") for i in range(2)])
    pT = Ring([s.tile([128, 512], BF16, f"pT{i}") for i in range(3)])
    pm_sw = [s.tile([128, 512], BF16, f"pmsw{i}") for i in range(8)]
    rden = Ring([s.tile([128, NT], F32, f"rden{i}") for i in range(2)])
    mergedb = s.tile([128, 8, NT], BF16, "mergedb")
    gtmp = Ring([s.tile([128, NT], F32, f"gtmp{i}") for i in range(2)])
    stores = []

    def kv_block(src_b, cols, slot):
        w = wcR[8]
        p = pring.next()
        for k in range(8):
            mm(s, p, p[:, 0:128], w, w[:, k, :], src_b, src_b[:, k, cols], k == 0, k == 7)
        V("act", lambda e, p=p, slot=slot: e.activation(out=kT[:, slot, :], in_=p[:, 0:128], func=AF.Copy), [p], [kT])
        p2 = pring.next()
        for k in range(8):
            mm(s, p2, p2[:, 0:128], src_b, src_b[:, k, cols], wvb, wvb[:, k, :], k == 0, k == 7)
        for kv in range(2):
            for pos in range(2):
                V("dve", lambda e, p2=p2, slot=slot, kv=kv, pos=pos: e.tensor_copy(
                    out=vpad[:, slot, kv * 2 + pos, pos * 64:pos * 64 + 64], in_=p2[:, kv * 64:kv * 64 + 64]), [p2], [vpad])

    s.dma("sp", hh32[:], hT[:, :, 0:128].rearrange("k p n -> p k n"), hh32, writes=[hh32])
    V("act", lambda e: e.activation(out=hhb[:], in_=hh32[:], func=AF.Copy), [hh32], [hhb])
    kv_block(hhb, slice(0, 128), 0)

    def proj(ci, out_fn):
        w = wcR[ci]
        p = pring.next()
        for k in range(8):
            mm(s, p, p[:, 0:NT], w, w[:, k, :], hb, hb[:, k, :], k == 0, k == 7)
        out_fn(p)

    for t in range(NTOK // NT):
        c0 = t * NT
        s.dma("sp", h32[:], hT[:, :, 128 + c0:128 + c0 + NT].rearrange("k p n -> p k n"), h32, writes=[h32])
        V("act", lambda e: e.activation(out=hb[:], in_=h32[:], func=AF.Copy), [h32], [hb])
        s.dma("sp", odn[:], odn_d[:, :, c0:c0 + NT].rearrange("h p n -> p h n"), odn, writes=[odn])
        for j in range(4):
            proj(j, lambda p, j=j: V("act", lambda e: e.activation(out=zs[:, j, :], in_=p[:, 0:NT], func=AF.Silu), [p], [zs]))
        for j in range(4):
            proj(4 + j, lambda p, j=j: V("act", lambda e: e.activation(out=swq[:, j, :], in_=p[:, 0:NT], func=AF.Copy), [p], [swq]))
        for j in range(4):
            proj(9 + j, lambda p, j=j: V("act", lambda e: e.activation(out=xaq[:, j, :], in_=p[:, 0:NT], func=AF.Copy), [p], [xaq]))
        for qb in range(NQB):
            gb = t * NQB + qb
            kv_block(hb, slice(qb * 128, (qb + 1) * 128), (gb + 1) % NSLOT)
        V("dve", lambda e: e.tensor_tensor(out=br[0][:], in0=odn[:], in1=zs[:], op=ALU.mult), [odn, zs], [br[0]])
        for h in range(4):
            pts = []
            for mc in range(2):
                p = pring.next()
                mm(s, p, p[:, 0:NT], kmemT, kmemT[:, h, mc * 128:(mc + 1) * 128], xaq, xaq[:, h, :], True, True)
                pt = pT.next()
                V("act", lambda e, p=p, pt=pt: e.activation(out=pt[:, 0:NT], in_=p[:, 0:NT], func=AF.Exp, scale=128 ** -0.5), [p], [pt])
                pts.append(pt)
            po, pd = pring.next(), pring.next()
            for mc in range(2):
                mm(s, po, po[:, 0:NT], vmem, vmem[:, mc, h * 128:(h + 1) * 128], pts[mc], pts[mc][:, 0:NT], mc == 0, mc == 1)
            for mc in range(2):
                mm(s, pd, pd[:, 0:NT], onesb, onesb[:], pts[mc], pts[mc][:, 0:NT], mc == 0, mc == 1)
            rd = rden.next()
            V("dve", lambda e, pd=pd, rd=rd: e.reciprocal(out=rd[:], in_=pd[:, 0:NT]), [pd], [rd])
            V("dve", lambda e, po=po, rd=rd, h=h: e.tensor_tensor(out=br[2][:, h, :], in0=po[:, 0:NT], in1=rd[:], op=ALU.mult), [po, rd], [br[2]])
        mi = 0 if t == 0 else 1
        for h in range(8):
            rows = slice(0, 64) if h < 4 else slice(64, 128)
            ch = h % 4
            p = pring.next()
            for qb in range(NQB):
                gb = t * NQB + qb
                qcols = slice(qb * 128, (qb + 1) * 128)
                mm(s, p, p[:, qb * 256:qb * 256 + 128], kT, kT[rows, gb % NSLOT, :], swq, swq[rows, ch, qcols], True, True)
                mm(s, p, p[:, qb * 256 + 128:qb * 256 + 256], kT, kT[rows, (gb + 1) % NSLOT, :], swq, swq[rows, ch, qcols], True, True)
            pe2 = pexp.next()
            V("act", lambda e, p=p, pe2=pe2: e.activation(out=pe2[:, 0:NQB * 256], in_=p[:, 0:NQB * 256], func=AF.Exp, scale=64 ** -0.5), [p], [pe2])
            V("dve", lambda e, pe2=pe2, h=h, mi=mi: e.tensor_tensor(out=pm_sw[h][:, 0:NQB * 256], in0=pe2[:, 0:NQB * 256], in1=msk[:, mi, 0:NQB * 256], op=ALU.mult),
              [pe2, msk], [pm_sw[h]])
        for pr in range(4):
            kv = pr // 2
            po, pd = pring.next(), pring.next()
            for qb in range(NQB):
                gb = t * NQB + qb
                qc = slice(qb * 128, (qb + 1) * 128)
                n = 0
                for pos in range(2):
                    h = pr * 2 + pos
                    for part, slot in ((0, gb % NSLOT), (1, (gb + 1) % NSLOT)):
                        pc = slice(qb * 256 + part * 128, qb * 256 + part * 128 + 128)
                        mm(s, po, po[:, qc], vpad, vpad[:, slot, kv * 2 + pos, :], pm_sw[h], pm_sw[h][:, pc], n == 0, n == 3)
                        n += 1
                n = 0
                for pos in range(2):
                    h = pr * 2 + pos
                    for part in range(2):
                        pc = slice(qb * 256 + part * 128, qb * 256 + part * 128 + 128)
                        mm(s, pd, pd[:, qc], onespad, onespad[:, pos, :], pm_sw[h], pm_sw[h][:, pc], n == 0, n == 3)
                        n += 1
            rd = rden.next()
            V("dve", lambda e, pd=pd, rd=rd, pr=pr: e.tensor_scalar(out=rd[:], in0=pd[:, 0:NT], scalar1=esink[:, pr:pr + 1], scalar2=None, op0=ALU.add),
              [pd, esink], [rd])
            V("dve", lambda e, rd=rd: e.reciprocal(out=rd[:], in_=rd[:]), [rd], [rd])
            V("dve", lambda e, po=po, rd=rd, pr=pr: e.tensor_tensor(out=br[1][:, pr, :], in0=po[:, 0:NT], in1=rd[:], op=ALU.mult), [po, rd], [br[1]])
        for m in range(8):
            for n in range(3):
                gs = gsb.next()
                proj(13 + n * 8 + m, lambda p, gs=gs: V("act", lambda e: e.activation(out=gs[:], in_=p[:, 0:NT], func=AF.Sigmoid), [p], [gs]))
                w = wbr.next()
                s.dma("pool", w[:], wbr_d[n * 8 + m], w, writes=[w])
                p = pring.next()
                for k in range(4):
                    mm(s, p, p[:, 0:NT], w, w[:, k, :], br[n], br[n][:, k, :], k == 0, k == 3)
                if n == 0:
                    V("dve", lambda e, p=p, m=m, gs=gs: e.tensor_tensor(out=merged[:, m, :], in0=p[:, 0:NT], in1=gs[:], op=ALU.mult),
                      [p, gs], [merged])
                else:
                    g = gtmp.next()
                    V("dve", lambda e, p=p, gs=gs, g=g: e.tensor_tensor(out=g[:], in0=p[:, 0:NT], in1=gs[:], op=ALU.mult),
                      [p, gs], [g])
                    V("pool", lambda e, m=m, g=g: e.tensor_tensor(out=merged[:, m, :], in0=merged[:, m, :], in1=g[:], op=ALU.add),
                      [merged, g], [merged])
        V("act", lambda e: e.activation(out=mergedb[:], in_=merged[:], func=AF.Copy), [merged], [mergedb])
        for m in range(8):
            w = wi.next()
            s.dma("pool", w[:], wo_d[m], w, writes=[w])
            p = pring.next()
            for k in range(8):
                mm(s, p, p[:, 0:NT], w, w[:, k, :], mergedb, mergedb[:, k, :], k == 0, k == 7)
            V("dve", lambda e, p=p, m=m: e.scalar_tensor_tensor(out=ybuf[:, m, :], in0=p[:, 0:NT], scalar=1.0 / ALPHA, in1=h32[:, m, :],
                                                              op0=ALU.mult, op1=ALU.add), [p, h32], [ybuf])
        ffn.ln(ybuf, None, 1)
        stores.append(s.dma("sp", outT[:, :, c0:c0 + NT].rearrange("k p n -> p k n"), ybuf[:], ybuf, reads=[ybuf]))
    s.finish(stores)
    s.emit()
    return nc


def lay_wc(w_in):
    swq = w_in[:, 2056:2568].reshape(1024, 8, 64)
    swq_p = np.concatenate([np.concatenate([swq[:, j], swq[:, 4 + j]], axis=1) for j in range(4)], axis=1)
    wcat = np.concatenate([w_in[:, 1544:2056], swq_p, w_in[:, 2568:2696], w_in[:, 2824:3336], w_in[:, 3336:6408]], axis=1)
    return lay_w_kmc(wcat)


def lay_pkc(w):
    K, C = w.shape
    return np.ascontiguousarray(w.reshape(K // 128, 128, C).transpose(1, 0, 2))


def lay_wbr(wb):
    a = wb.reshape(3, 4, 128, 8, 128).transpose(0, 3, 2, 1, 4)
    return np.ascontiguousarray(a.reshape(24, 128, 4, 128))


def make_masks(halo_valid):
    k = np.arange(128)[:, None]
    q = np.arange(128)[None, :]
    prev = (k > q).astype(np.float32)
    cur = (k <= q).astype(np.float32)
    std = np.concatenate([prev, cur, prev, cur], axis=1)
    first = np.concatenate([prev * halo_valid, cur, prev, cur], axis=1)
    return np.ascontiguousarray(np.stack([first, std], axis=1)).astype(np.float32)


def lay_sink(sinks):
    c = np.zeros((128, 4), np.float32)
    for pr in range(4):
        c[:64, pr] = sinks[2 * pr]
        c[64:, pr] = sinks[2 * pr + 1]
    return c


_PROGS = {}
TOK_PER_CORE = SEQ * BATCH // NCORES
NSEG = SEQ // TOK_PER_CORE


def _prog(name):
    if name not in _PROGS:
        if name == "A":
            _PROGS[name] = build_A(TOK_PER_CORE)
        elif name == "B":
            _PROGS[name] = build_B(SEQ)
        elif name == "F":
            _PROGS[name] = build_A(TOK_PER_CORE, proj=False, ln_idx=2)
        else:
            _PROGS[name] = build_C(TOK_PER_CORE)
    return _PROGS[name]


def _run(name, in_maps):
    res = run_bass_kernel_spmd(_prog(name), in_maps, core_ids=list(range(NCORES)))
    return res.results


def kernel(x, mem, mem_ln_g, mem_ln_b, ln_g, ln_b, ffn1_w_gu, ffn1_w_down, w_in, dn_conv_w,
           dn_a_log, dn_dt_bias, dn_norm_w, swa_sinks, w_mem_kv, w_branch, w_out, ffn2_w_gu, ffn2_w_down):
    f = lambda a: np.asarray(a, dtype=np.float32)
    cur = f(x)
    mem = f(mem)
    cst = make_cst()
    cstB, ccol = make_cstB()
    NT_ = TOK_PER_CORE
    for l in range(DEPTH):
        win = f(w_in[l])
        lnp = lay_lnp(f(ln_g[l]), f(ln_b[l]))
        commonA = dict(wgu=lay_wgu(f(ffn1_w_gu[l])), wdn=lay_w_kmc(f(ffn1_w_down[l])), wqkv=lay_w_kmc(win[:, :1536]),
                       wba=lay_pkc(win[:, 1536:1544]), lnp=lnp, cst=cst)
        insA = []
        for c in range(NCORES):
            b, sg = divmod(c, NSEG)
            xs = cur[b, sg * NT_:(sg + 1) * NT_]
            insA.append(dict(xT=np.ascontiguousarray(xs.T.reshape(8, 128, NT_)), **commonA))
        rA = _run("A", insA)
        convw = f(dn_conv_w[l])
        insB = []
        for c in range(NCORES):
            b, hd = divmod(c, 4)
            qkvp = np.concatenate([rA[b * NSEG + sg]["qkvT"][[hd, 4 + hd, 8 + hd]] for sg in range(NSEG)], axis=2)
            brow = np.concatenate([rA[b * NSEG + sg]["baT"][hd] for sg in range(NSEG)])
            arow = np.concatenate([rA[b * NSEG + sg]["baT"][4 + hd] for sg in range(NSEG)])
            cw = np.stack([convw[:, w * 512 + hd * 128: w * 512 + hd * 128 + 128] for w in range(3)], 0)
            smallp = np.zeros((128, 32), np.float32)
            smallp[:, 0:12] = cw.transpose(2, 0, 1).reshape(128, 12)
            smallp[:, 12] = f(dn_a_log[l])[hd]
            smallp[:, 13] = f(dn_dt_bias[l])[hd]
            smallp[:, 14] = f(dn_norm_w[l])
            smallp[:, 15:17] = ccol
            insB.append(dict(qkvp=np.ascontiguousarray(qkvp), bcol=np.ascontiguousarray(brow.reshape(SEQ // 128, 128).T),
                             acol=np.ascontiguousarray(arow.reshape(SEQ // 128, 128).T),
                             smallp=smallp, cstB=cstB))
        rB = _run("B", insB)
        wkv = f(w_mem_kv[l])
        mg = np.stack([f(mem_ln_g)] * 3)
        mb = np.stack([f(mem_ln_b)] * 3)
        commonC = dict(wc=lay_wc(win), wv=lay_pkc(win[:, 2696:2824]), wmk=lay_w_kmc(wkv[:, :512]), wmv=lay_pkc(wkv[:, 512:]),
                       wbr=lay_wbr(f(w_branch[l])), wo=lay_w_kmc(f(w_out[l])),
                       lnp=lnp, mlnp=np.ascontiguousarray(lay_lnp(mg, mb)[:, 0:1]),
                       cst=cst, sinkc=lay_sink(f(swa_sinks[l])))
        insC = []
        for c in range(NCORES):
            b, sg = divmod(c, NSEG)
            hT = rA[c]["hT"]
            halo = rA[c - 1]["hT"][:, :, -128:] if sg > 0 else np.zeros((8, 128, 128), np.float32)
            odn = np.stack([rB[b * 4 + hd]["onT"][:, sg * NT_:(sg + 1) * NT_] for hd in range(4)], 0)
            insC.append(dict(hT=np.ascontiguousarray(np.concatenate([halo, hT], axis=2)), odn=np.ascontiguousarray(odn),
                             memT=np.ascontiguousarray(mem[b].T.reshape(8, 128, N_MEM)),
                             msk=make_masks(1.0 if sg > 0 else 0.0), **commonC))
        rC = _run("C", insC)
        commonF = dict(wgu=lay_wgu(f(ffn2_w_gu[l])), wdn=lay_w_kmc(f(ffn2_w_down[l])), lnp=lnp, cst=cst)
        rF = _run("F", [dict(xT=rC[c]["outT"], **commonF) for c in range(NCORES)])
        nxt = np.empty_like(cur)
        for c in range(NCORES):
            b, sg = divmod(c, NSEG)
            nxt[b, sg * NT_:(sg + 1) * NT_] = rF[c]["hT"].reshape(D_MODEL, NT_).T
        cur = nxt
    return cur
```

```python
import math
import numpy as np
import concourse.bass as bass
import concourse.mybir as mybir
from concourse.bass_utils import run_bass_kernel_spmd

F32 = mybir.dt.float32
BF16 = mybir.dt.bfloat16
AF = mybir.ActivationFunctionType
ALU = mybir.AluOpType

D_MODEL = 1024
BATCH = 2
SEQ = 16384
DEPTH = 2
N_MEM = 256
D_FF = 2816
D_IN = 6408
NCORES = 8
ALPHA = (2 * DEPTH) ** 0.25
LN_EPS = 1e-5
RMS_EPS = 1e-6
SAME_ENGINE_SYNC = True
PSUM_READ_SERIALIZE = True


class T:
    def __init__(self, h, sem=None):
        self.h = h
        self.w = None
        self.r = {}
        self.sem = sem
        self.semval = 0

    def __getitem__(self, idx):
        return self.h[idx]


class TV:
    def __init__(self, parent, ap):
        self.p = parent
        self.ap = ap

    @property
    def is_psum(self):
        return getattr(self.p, "is_psum", False)

    @property
    def w(self):
        return self.p.w

    @w.setter
    def w(self, v):
        self.p.w = v

    @property
    def r(self):
        return self.p.r

    @r.setter
    def r(self, v):
        self.p.r = v

    def __getitem__(self, idx):
        return self.ap[idx]


class Sched:
    ENG = ("pe", "act", "dve", "pool", "sp")

    def __init__(self, nc):
        self.nc = nc
        self.prog = {e: [] for e in self.ENG}
        self.sem = {e: nc.alloc_semaphore("sem_" + e) for e in ("pe", "act", "dve", "pool")}
        self.cnt = {e: 0 for e in self.sem}
        self.seen = {e: {} for e in self.ENG}
        self.uid = 0
        self.final = []

    def tile(self, shape, dt, name=None, psum=False, dma=False):
        self.uid += 1
        name = f"{name or 't'}_{self.uid}"
        if psum:
            h = self.nc.alloc_psum_tensor(name, list(shape), dt)
        else:
            h = self.nc.alloc_sbuf_tensor(name, list(shape), dt)
        sem = self.nc.alloc_semaphore("ds_" + name) if dma else None
        t = T(h, sem)
        t.is_psum = psum
        return t

    def _dep(self, eng, ev):
        key, sem, val = ev
        if key == eng and (eng == "pe" or not SAME_ENGINE_SYNC):
            return
        if self.seen[eng].get(key, 0) >= val:
            return
        self.seen[eng][key] = val
        self.prog[eng].append(lambda e, sem=sem, val=val: e.wait_ge(sem, val))

    def _deps(self, eng, reads, writes):
        for t in reads:
            if t.w is not None:
                self._dep(eng, t.w)
            if PSUM_READ_SERIALIZE and getattr(t, "is_psum", False):
                for ev in t.r.values():
                    if ev[0] != eng:
                        self._dep(eng, ev)
        for t in writes:
            if t.w is not None:
                self._dep(eng, t.w)
            for ev in t.r.values():
                self._dep(eng, ev)

    def _mark(self, ev, reads, writes):
        for t in reads:
            t.r[ev[0]] = ev
        for t in writes:
            t.w = ev
            t.r = {}

    def op(self, eng, fn, reads=(), writes=()):
        self._deps(eng, reads, writes)
        self.cnt[eng] += 1
        val = self.cnt[eng]
        sem = self.sem[eng]
        self.prog[eng].append(lambda e, fn=fn, sem=sem: fn(e).then_inc(sem, 1))
        self._mark((eng, sem, val), reads, writes)

    def dma(self, q, out, in_, semtile, reads=(), writes=()):
        self._deps(q, reads, writes)
        semtile.semval += 16
        sem, val = semtile.sem, semtile.semval
        self.prog[q].append(lambda e, out=out, in_=in_, sem=sem: e.dma_start(out=out, in_=in_).then_inc(sem, 16))
        ev = (("d", id(semtile)), sem, val)
        self._mark(ev, reads, writes)
        return ev

    def finish(self, evs):
        for ev in evs:
            self._dep("sp", ev)

    def emit(self):
        nc = self.nc
        with nc.Block() as block:
            @block.tensor
            def _(e):
                for f in self.prog["pe"]:
                    f(e)

            @block.scalar
            def _(e):
                for f in self.prog["act"]:
                    f(e)

            @block.vector
            def _(e):
                for f in self.prog["dve"]:
                    f(e)

            @block.gpsimd
            def _(e):
                for f in self.prog["pool"]:
                    f(e)

            @block.sync
            def _(e):
                for f in self.prog["sp"]:
                    f(e)


class Ring:
    def __init__(self, tiles):
        self.tiles = tiles
        self.i = 0

    def next(self):
        t = self.tiles[self.i % len(self.tiles)]
        self.i += 1
        return t


def mm(s, out_t, out_ap, l_t, l_ap, r_t, r_ap, start, stop):
    s.op("pe", lambda e: e.matmul(out_ap, lhsT=l_ap, rhs=r_ap, start=start, stop=stop),
         reads=[l_t, r_t], writes=[out_t])


class FFNStage:
    def __init__(self, s, NT, wgu_d, wdn_d, lnp_t, ln_idx, consts, ps, resident=False, with_ffn=True):
        self.s, self.NT = s, NT
        self.wgu_d, self.wdn_d = wgu_d, wdn_d
        self.lnp, self.ln_idx = lnp_t, ln_idx
        self.c = consts
        self.ps = ps
        self.resident = resident
        if with_ffn:
            if resident:
                self.wgR = [s.tile([128, 8, 256], BF16, "wgR", dma=True) for _ in range(22)]
                self.wdR = [s.tile([128, 22, 128], BF16, "wdR", dma=True) for _ in range(8)]
                for j in range(22):
                    s.dma("pool", self.wgR[j][:], wgu_d[j], self.wgR[j], writes=[self.wgR[j]])
                for m in range(8):
                    s.dma("pool", self.wdR[m][:], wdn_d[m], self.wdR[m], writes=[self.wdR[m]])
            else:
                self.wg = Ring([s.tile([128, 8, 256], BF16, "wg", dma=True) for _ in range(3)])
                self.wd = Ring([s.tile([128, 22, 128], BF16, "wd", dma=True) for _ in range(2)])
            self.act = s.tile([128, 22, NT], BF16, "act")
            self.sg = Ring([s.tile([128, NT], F32, "sg") for _ in range(2)])
        self.ysq = Ring([s.tile([128, NT], F32, "ysq") for _ in range(2)])
        self.mean = s.tile([128, NT], F32, "mean")
        self.tmp = s.tile([128, NT], F32, "lntmp")
        self.rstd = s.tile([128, NT], F32, "rstd")
        self.d = Ring([s.tile([128, NT], F32, "lnd") for _ in range(2)])

    def run(self, x32, xb, hb, prefetch=None):
        s, NT = self.s, self.NT
        ps = self.ps
        for j in range(22):
            if self.resident:
                wg = self.wgR[j]
            else:
                wg = self.wg.next()
                s.dma("pool", wg[:], self.wgu_d[j], wg, writes=[wg])
            pg, pu = ps["g"].next(), ps["u"].next()
            for k in range(8):
                mm(s, pg, pg[:, 0:NT], wg, wg[:, k, 0:128], xb, xb[:, k, :], k == 0, k == 7)
            for k in range(8):
                mm(s, pu, pu[:, 0:NT], wg, wg[:, k, 128:256], xb, xb[:, k, :], k == 0, k == 7)
            sg = self.sg.next()
            s.op("act", lambda e, sg=sg, pg=pg: e.activation(out=sg[:], in_=pg[:, 0:NT], func=AF.Silu),
                 reads=[pg], writes=[sg])
            s.op("dve", lambda e, sg=sg, pu=pu, j=j: e.tensor_tensor(out=self.act[:, j, :], in0=sg[:], in1=pu[:, 0:NT], op=ALU.mult),
                 reads=[sg, pu], writes=[self.act])
        for m in range(8):
            if self.resident:
                wd = self.wdR[m]
            else:
                wd = self.wd.next()
                s.dma("pool", wd[:], self.wdn_d[m], wd, writes=[wd])
            py = ps["y"].next()
            for k in range(22):
                mm(s, py, py[:, 0:NT], wd, wd[:, k, :], self.act, self.act[:, k, :], k == 0, k == 21)
            s.op("dve", lambda e, py=py, m=m: e.scalar_tensor_tensor(
                out=x32[:, m, :], in0=py[:, 0:NT], scalar=0.5 / ALPHA, in1=x32[:, m, :], op0=ALU.mult, op1=ALU.add),
                reads=[py, x32], writes=[x32])
        if prefetch is not None:
            prefetch()
        self.ln(x32, hb, self.ln_idx)

    def ln(self, y, hb, li):
        s, NT, ps = self.s, self.NT, self.ps
        pm, pe_ = ps["m"], ps["e"]
        for m in range(8):
            ysq = self.ysq.next()
            s.op("act", lambda e, ysq=ysq, m=m: e.activation(out=ysq[:], in_=y[:, m, :], func=AF.Square),
                 reads=[y], writes=[ysq])
            mm(s, pm, pm[:, 0:NT], self.c["onesD"], self.c["onesD"][:], y, y[:, m, :], m == 0, m == 7)
            mm(s, pe_, pe_[:, 0:NT], self.c["onesD"], self.c["onesD"][:], ysq, ysq[:], m == 0, m == 7)
        s.op("act", lambda e: e.activation(out=self.mean[:], in_=pm[:, 0:NT], func=AF.Copy), reads=[pm], writes=[self.mean])
        s.op("dve", lambda e: e.tensor_tensor(out=self.tmp[:], in0=self.mean[:], in1=self.mean[:], op=ALU.mult),
             reads=[self.mean], writes=[self.tmp])
        s.op("dve", lambda e: e.tensor_tensor(out=self.tmp[:], in0=pe_[:, 0:NT], in1=self.tmp[:], op=ALU.subtract),
             reads=[pe_, self.tmp], writes=[self.tmp])
        s.op("act", lambda e: e.activation(out=self.tmp[:], in_=self.tmp[:], func=AF.Sqrt, bias=self.c["epsln"][:, 0:1]),
             reads=[self.tmp, self.c["epsln"]], writes=[self.tmp])
        s.op("dve", lambda e: e.reciprocal(out=self.rstd[:], in_=self.tmp[:]), reads=[self.tmp], writes=[self.rstd])
        for m in range(8):
            d = self.d.next()
            s.op("dve", lambda e, d=d, m=m: e.tensor_tensor(out=d[:], in0=y[:, m, :], in1=self.mean[:], op=ALU.subtract),
                 reads=[y, self.mean], writes=[d])
            s.op("dve", lambda e, d=d: e.tensor_tensor(out=d[:], in0=d[:], in1=self.rstd[:], op=ALU.mult),
                 reads=[d, self.rstd], writes=[d])
            s.op("act", lambda e, d=d, m=m: e.activation(
                out=y[:, m, :], in_=d[:], func=AF.Identity,
                scale=self.lnp[:, li, 0, m:m + 1], bias=self.lnp[:, li, 1, m:m + 1]),
                reads=[d, self.lnp], writes=[y])
        if hb is not None:
            s.op("pool", lambda e: e.tensor_copy(out=hb[:], in_=y[:]), reads=[y], writes=[hb])


def make_psum(s):
    banks = [s.tile([128, 512], F32, f"ps{i}", psum=True) for i in range(8)]
    return {"g": Ring(banks[0:2]), "u": Ring(banks[2:4]), "y": Ring(banks[4:6]), "m": banks[6], "e": banks[7],
            "all": banks}


def load_consts(s, nc, cst_d):
    c = {}
    cst = s.tile([128, 3, 128], F32, "cst", dma=True)
    s.dma("sp", cst[:], cst_d, cst, writes=[cst])
    c["cst"] = cst
    onesD = s.tile([128, 128], F32, "onesD")
    s.op("dve", lambda e: e.tensor_copy(out=onesD[:], in_=cst[:, 0, :]), reads=[cst], writes=[onesD])
    c["onesD"] = onesD
    eps = s.tile([128, 1], F32, "epsln")
    s.op("dve", lambda e: e.memset(eps[:], LN_EPS / (ALPHA * ALPHA)), writes=[eps])
    c["epsln"] = eps
    return c


def build_A(NTOK, NT=256, proj=True, ln_idx=0):
    nc = bass.Bass("TRN2", target_bir_lowering=False)
    xT = nc.dram_tensor("xT", [8, 128, NTOK], F32, kind="ExternalInput").ap()
    wgu = nc.dram_tensor("wgu", [22, 128, 8, 256], F32, kind="ExternalInput").ap()
    wdn = nc.dram_tensor("wdn", [8, 128, 22, 128], F32, kind="ExternalInput").ap()
    if proj:
        wqkv = nc.dram_tensor("wqkv", [12, 128, 8, 128], F32, kind="ExternalInput").ap()
        wba = nc.dram_tensor("wba", [128, 8, 8], F32, kind="ExternalInput").ap()
    lnp_d = nc.dram_tensor("lnp", [128, 3, 2, 8], F32, kind="ExternalInput").ap()
    cst_d = nc.dram_tensor("cst", [128, 3, 128], F32, kind="ExternalInput").ap()
    hT = nc.dram_tensor("hT", [8, 128, NTOK], F32, kind="ExternalOutput").ap()
    if proj:
        qkvT = nc.dram_tensor("qkvT", [12, 128, NTOK], F32, kind="ExternalOutput").ap()
        baT = nc.dram_tensor("baT", [8, NTOK], F32, kind="ExternalOutput").ap()

    s = Sched(nc)
    c = load_consts(s, nc, cst_d)
    lnp = s.tile([128, 3, 2, 8], F32, "lnp", dma=True)
    s.dma("sp", lnp[:], lnp_d, lnp, writes=[lnp])
    ps = make_psum(s)
    ffn = FFNStage(s, NT, wgu, wdn, lnp, ln_idx, c, ps, resident=True)
    x32r = [s.tile([128, 8, NT], F32, "x32", dma=True) for _ in range(2)]
    xbr = [s.tile([128, 8, NT], BF16, "xb") for _ in range(2)]
    if proj:
        wb = s.tile([128, 8, 8], BF16, "wba", dma=True)
        s.dma("pool", wb[:], wba, wb, writes=[wb])
        wqR = [s.tile([128, 8, 128], BF16, "wqR", dma=True) for _ in range(12)]
        for m in range(12):
            s.dma("pool", wqR[m][:], wqkv[m], wqR[m], writes=[wqR[m]])
        qo = Ring([s.tile([128, NT], F32, "qo", dma=True) for _ in range(2)])
        bo = s.tile([8, NT], F32, "bo", dma=True)
    stores = []
    NTILES = NTOK // NT

    def load(t):
        x32, xb = x32r[t % 2], xbr[t % 2]
        cols = slice(t * NT, (t + 1) * NT)
        s.dma("sp", x32[:], xT[:, :, cols].rearrange("k p n -> p k n"), x32, writes=[x32])
        s.op("act", lambda e: e.activation(out=xb[:], in_=x32[:], func=AF.Copy), reads=[x32], writes=[xb])

    load(0)
    for t in range(NTILES):
        cols = slice(t * NT, (t + 1) * NT)
        x32, xb = x32r[t % 2], xbr[t % 2]
        hb = xb if proj else None
        ffn.run(x32, xb, hb, prefetch=(lambda t=t: load(t + 1)) if t + 1 < NTILES else None)
        stores.append(s.dma("sp", hT[:, :, cols].rearrange("k p n -> p k n"), x32[:], x32, reads=[x32]))
        if not proj:
            continue
        for m in range(12):
            w = wqR[m]
            pq = ps["g"].next()
            for k in range(8):
                mm(s, pq, pq[:, 0:NT], w, w[:, k, :], hb, hb[:, k, :], k == 0, k == 7)
            q = qo.next()
            s.op("act", lambda e, q=q, pq=pq: e.activation(out=q[:], in_=pq[:, 0:NT], func=AF.Copy), reads=[pq], writes=[q])
            stores.append(s.dma("sp", qkvT[m][:, cols], q[:], q, reads=[q]))
        pb = ps["u"].next()
        for k in range(8):
            mm(s, pb, pb[0:8, 0:NT], wb, wb[:, k, :], hb, hb[:, k, :], k == 0, k == 7)
        s.op("act", lambda e, pb=pb: e.activation(out=bo[:], in_=pb[0:8, 0:NT], func=AF.Copy), reads=[pb], writes=[bo])
        stores.append(s.dma("sp", baT[:, cols], bo[:], bo, reads=[bo]))
    s.finish(stores)
    s.emit()
    return nc


def lay_w_kmc(w, ncols_chunk=128):
    K, M = w.shape
    return np.ascontiguousarray(w.reshape(K // 128, 128, M // ncols_chunk, ncols_chunk).transpose(2, 1, 0, 3))


def lay_wgu(w):
    g = lay_w_kmc(w[:, :D_FF])
    u = lay_w_kmc(w[:, D_FF:])
    return np.ascontiguousarray(np.concatenate([g, u], axis=3))


def lay_lnp(g, b):
    a = np.stack([g.reshape(3, 8, 128), b.reshape(3, 8, 128)], axis=1)
    return np.ascontiguousarray(a.transpose(3, 0, 1, 2))


def make_cst():
    c = np.zeros((128, 3, 128), np.float32)
    c[:, 0, :] = 1.0 / D_MODEL
    c[:, 1, :] = np.eye(128, dtype=np.float32)
    c[:, 2, :] = 1.0
    return c


NEG = -1e30
B_DEBUG_STAGE = None
B_VAR = 0


def make_cstB():
    idx = np.arange(128)
    same = (idx[:, None] // 64) == (idx[None, :] // 64)
    c = np.zeros((128, 7, 128), np.float32)
    c[:, 0, :] = np.eye(128)
    c[:, 1, :] = 1.0
    c[:, 2, :] = (same & (idx[:, None] <= idx[None, :]))
    c[:, 3, :] = np.where(same & (idx[:, None] > idx[None, :]), 0.0, NEG)
    c[:, 4, :] = np.where(same & (idx[None, :] >= idx[:, None]), 0.0, NEG)
    c[63, 5, :] = 1.0
    c[127, 6, :] = 1.0
    col = np.zeros((128, 2), np.float32)
    col[64:, 0] = NEG
    col[:64, 1] = NEG
    return c, col


def build_B(S_LEN, NT=512):
    nc = bass.Bass("TRN2", target_bir_lowering=False)
    NBLK = S_LEN // 128
    qkvp = nc.dram_tensor("qkvp", [3, 128, S_LEN], F32, kind="ExternalInput").ap()
    bcol_d = nc.dram_tensor("bcol", [128, NBLK], F32, kind="ExternalInput").ap()
    acol_d = nc.dram_tensor("acol", [128, NBLK], F32, kind="ExternalInput").ap()
    smallp_d = nc.dram_tensor("smallp", [128, 32], F32, kind="ExternalInput").ap()
    cst_d = nc.dram_tensor("cstB", [128, 7, 128], F32, kind="ExternalInput").ap()
    onT = nc.dram_tensor("onT", [128, S_LEN], F32, kind="ExternalOutput").ap()

    s = Sched(nc)
    cst = s.tile([128, 7, 128], F32, "cstB", dma=True)
    s.dma("sp", cst[:], cst_d, cst, writes=[cst])
    smallp = s.tile([128, 32], F32, "smallp", dma=True)
    s.dma("sp", smallp[:], smallp_d, smallp, writes=[smallp])
    ccol = TV(smallp, smallp[:, 15:17])
    bcol = s.tile([128, NBLK], F32, "bcol", dma=True)
    s.dma("sp", bcol[:], bcol_d, bcol, writes=[bcol])
    acol = s.tile([128, NBLK], F32, "acol", dma=True)
    s.dma("sp", acol[:], acol_d, acol, writes=[acol])
    convw = TV(smallp, smallp[:, 0:12].rearrange("p (w j) -> p w j", j=4))
    scal = TV(smallp, smallp[:, 12:15])
    I_, ONES, TRI, MBS, MBT, SELA, SELB = [cst[:, i, :] for i in range(7)]

    banks = [s.tile([128, 512], F32, f"psB{i}", psum=True) for i in range(8)]
    big = Ring(banks[0:2])
    small = Ring([TV(bk, bk[:, q * 128:(q + 1) * 128]) for q in range(4) for bk in banks[3:8]])
    po_r = Ring([TV(banks[2], banks[2][:, q * 128:(q + 1) * 128]) for q in range(4)])

    def V(eng, fn, reads, writes):
        s.op(eng, fn, reads=reads, writes=writes)

    def mmf(out_t, out_ap, l_t, l_ap, r_t, r_ap, start=True, stop=True):
        mm(s, out_t, out_ap, l_t, l_ap, r_t, r_ap, start, stop)

    def col_tile(name, n=NBLK):
        return s.tile([128, n], F32, name)

    one_c = s.tile([128, 1], F32, "one_c")
    V("dve", lambda e: e.memset(one_c[:], 1.0), [], [one_c])
    eps_c = s.tile([128, 1], F32, "eps_c")
    V("dve", lambda e: e.memset(eps_c[:], RMS_EPS), [], [eps_c])
    ones128 = s.tile([128, 128], F32, "ones128")
    V("dve", lambda e: e.tensor_scalar(out=ones128[:], in0=ONES, scalar1=1.0 / 128, scalar2=None, op0=ALU.mult), [cst], [ones128])
    beta = col_tile("beta")
    V("act", lambda e: e.activation(out=beta[:], in_=bcol[:], func=AF.Sigmoid), [bcol], [beta])
    xg = col_tile("xg")
    V("dve", lambda e: e.tensor_scalar(out=xg[:], in0=acol[:], scalar1=scal[:, 1:2], scalar2=None, op0=ALU.add), [acol, scal], [xg])
    ax = col_tile("ax")
    V("dve", lambda e: e.tensor_scalar(out=ax[:], in0=xg[:], scalar1=-1.0, scalar2=None, op0=ALU.mult), [xg], [ax])
    V("dve", lambda e: e.tensor_tensor(out=ax[:], in0=ax[:], in1=xg[:], op=ALU.max), [ax, xg], [ax])
    V("act", lambda e: e.activation(out=ax[:], in_=ax[:], func=AF.Exp, scale=-1.0), [ax], [ax])
    V("act", lambda e: e.activation(out=ax[:], in_=ax[:], func=AF.Ln, bias=one_c[:, 0:1]), [ax, one_c], [ax])
    V("dve", lambda e: e.tensor_scalar(out=xg[:], in0=xg[:], scalar1=0.0, scalar2=None, op0=ALU.max), [xg], [xg])
    V("dve", lambda e: e.tensor_tensor(out=xg[:], in0=xg[:], in1=ax[:], op=ALU.add), [xg, ax], [xg])
    nea = s.tile([128, 1], F32, "nea")
    V("act", lambda e: e.activation(out=nea[:], in_=scal[:, 0:1], func=AF.Exp), [scal], [nea])
    V("dve", lambda e: e.tensor_scalar(out=nea[:], in0=nea[:], scalar1=-1.0, scalar2=None, op0=ALU.mult), [nea], [nea])
    gc = col_tile("gc")
    V("dve", lambda e: e.tensor_scalar(out=gc[:], in0=xg[:], scalar1=nea[:, 0:1], scalar2=None, op0=ALU.mult), [xg, nea], [gc])
    gcum, ngcum, bexpg, colA, colB, eglA, eglB = [col_tile(n) for n in ("gcum", "ngcum", "bexpg", "colA", "colB", "eglA", "eglB")]
    for c0 in range(0, NBLK, 512):
        c1 = min(NBLK, c0 + 512)
        w = c1 - c0
        pb = big.next()
        mmf(pb, pb[:, 0:w], cst, TRI, gc, gc[:, c0:c1])
        V("act", lambda e, pb=pb, c0=c0, c1=c1, w=w: e.activation(out=gcum[:, c0:c1], in_=pb[:, 0:w], func=AF.Copy), [pb], [gcum])
    V("dve", lambda e: e.tensor_scalar(out=ngcum[:], in0=gcum[:], scalar1=-1.0, scalar2=None, op0=ALU.mult), [gcum], [ngcum])
    V("act", lambda e: e.activation(out=bexpg[:], in_=gcum[:], func=AF.Exp), [gcum], [bexpg])
    V("dve", lambda e: e.tensor_tensor(out=bexpg[:], in0=bexpg[:], in1=beta[:], op=ALU.mult), [bexpg, beta], [bexpg])
    for (SEL, col, egl, mi) in ((SELA, colA, eglA, 0), (SELB, colB, eglB, 1)):
        for c0 in range(0, NBLK, 512):
            c1 = min(NBLK, c0 + 512)
            w = c1 - c0
            pb = big.next()
            mmf(pb, pb[:, 0:w], cst, SEL, gcum, gcum[:, c0:c1])
            V("act", lambda e, pb=pb, egl=egl, c0=c0, c1=c1, w=w: e.activation(out=egl[:, c0:c1], in_=pb[:, 0:w], func=AF.Exp), [pb], [egl])
            V("dve", lambda e, pb=pb, col=col, c0=c0, c1=c1, w=w: e.tensor_tensor(out=col[:, c0:c1], in0=pb[:, 0:w], in1=gcum[:, c0:c1], op=ALU.subtract),
              [pb, gcum], [col])
        V("act", lambda e, col=col, mi=mi: e.activation(out=col[:], in_=col[:], func=AF.Exp, bias=ccol[:, mi:mi + 1]), [col, ccol], [col])

    S_r = [s.tile([128, 128], F32, f"S{i}") for i in range(2)]
    V("dve", lambda e: e.memset(S_r[0][:], 0.0), [], [S_r[0]])
    state = {"S": 0}
    NTILE = S_LEN // NT
    NB = NT // 128

    def alloc_set():
        d = {}
        d["pre"] = s.tile([128, 3, NT + 8], F32, "pre", dma=True)
        d["cv"] = [s.tile([128, NT], F32, f"cv{i}") for i in range(3)]
        d["sq"] = s.tile([128, NT], F32, "sq")
        d["rin"] = s.tile([128, NT], F32, "rin")
        d["qT"] = s.tile([128, NT], F32, "qT")
        d["kT"] = s.tile([128, NT], F32, "kT")
        d["kT2"] = s.tile([128, NT], F32, "kT2")
        for nm in ("dGn", "dGp", "ES", "E2", "EG", "A", "Bm", "N", "A2", "B2", "RHSv", "RHSw", "ktA", "ktB", "u", "wT", "qkT", "qdT"):
            d[nm] = [s.tile([128, 128], F32, f"{nm}{i}") for i in range(NB)]
        d["A2b"] = [s.tile([128, 128], F32, f"A2b{i}") for i in range(NB)]
        d["B2b"] = [s.tile([128, 128], F32, f"B2b{i}") for i in range(NB)]
        return d

    sets = [alloc_set(), alloc_set()]
    vn_r = Ring([s.tile([128, 128], F32, f"vn{i}") for i in range(2)])
    oT_r = Ring([s.tile([128, NT], F32, f"oT{i}", dma=True) for i in range(2)])
    osq = s.tile([128, NT], F32, "osq")
    orin = s.tile([128, NT], F32, "orin")
    stores = []

    def prep(t, d):
        c0 = t * NT
        pre = d["pre"]
        if t == 0:
            V("dve", lambda e: e.memset(pre[:, :, 0:8], 0.0), [], [pre])
            s.dma("sp", pre[:, :, 8:NT + 8], qkvp[:, :, 0:NT].rearrange("w p n -> p w n"), pre, writes=[pre])
        else:
            s.dma("sp", pre[:], qkvp[:, :, c0 - 8:c0 + NT].rearrange("w p n -> p w n"), pre, writes=[pre])
        for w in range(3):
            cv = d["cv"][w]
            V("dve", lambda e, cv=cv, w=w: e.tensor_scalar(out=cv[:], in0=pre[:, w, 5:5 + NT], scalar1=convw[:, w, 0:1], scalar2=None, op0=ALU.mult),
              [pre, convw], [cv])
            for j in range(1, 4):
                V("dve", lambda e, cv=cv, w=w, j=j: e.scalar_tensor_tensor(
                    out=cv[:], in0=pre[:, w, 5 + j:5 + j + NT], scalar=convw[:, w, j:j + 1], in1=cv[:], op0=ALU.mult, op1=ALU.add),
                  [pre, convw, cv], [cv])
            V("act", lambda e, cv=cv: e.activation(out=cv[:], in_=cv[:], func=AF.Silu), [cv], [cv])
        yield
        for w, dst, sc in ((0, d["qT"], 128 ** -0.5), (1, d["kT"], 1.0)):
            cv = d["cv"][w]
            V("act", lambda e, cv=cv: e.activation(out=d["sq"][:], in_=cv[:], func=AF.Square), [cv], [d["sq"]])
            pb = big.next()
            mmf(pb, pb[:, 0:NT], cst, ONES, d["sq"], d["sq"][:])
            V("act", lambda e, pb=pb: e.activation(out=d["rin"][:], in_=pb[:, 0:NT], func=AF.Sqrt, bias=eps_c[:, 0:1]), [pb, eps_c], [d["rin"]])
            V("dve", lambda e: e.reciprocal(out=d["rin"][:], in_=d["rin"][:]), [d["rin"]], [d["rin"]])
            V("dve", lambda e, cv=cv, dst=dst, sc=sc: e.scalar_tensor_tensor(
                out=dst[:], in0=cv[:], scalar=sc, in1=d["rin"][:], op0=ALU.mult, op1=ALU.mult), [cv, d["rin"]], [dst])
        vT = d["cv"][2]
        qT, kT = d["qT"], d["kT"]
        kT2 = d["kT2"]
        V("act", lambda e: e.activation(out=kT2[:], in_=kT[:], func=AF.Copy), [kT], [kT2])
        yield
        blks = range(NB)

        def bs(i):
            return slice(i * 128, (i + 1) * 128)
        for i in blks:
            gb = t * NB + i
            V("dve", lambda e, i=i, gb=gb: e.tensor_scalar(out=d["dGn"][i][:], in0=I_, scalar1=ngcum[:, gb:gb + 1], scalar2=None, op0=ALU.mult),
              [cst, ngcum], [d["dGn"][i]])
            V("dve", lambda e, i=i, gb=gb: e.tensor_scalar(out=d["dGp"][i][:], in0=I_, scalar1=gcum[:, gb:gb + 1], scalar2=None, op0=ALU.mult),
              [cst, gcum], [d["dGp"][i]])
        for i in blks:
            gb = t * NB + i
            p1 = small.next()
            mmf(p1, p1[:], cst, ONES, d["dGn"][i], d["dGn"][i][:], True, False)
            mmf(p1, p1[:], cst, I_, cst, MBS, False, True)
            V("act", lambda e, i=i, gb=gb, p1=p1: e.activation(out=d["ES"][i][:], in_=p1[:], func=AF.Exp, bias=gcum[:, gb:gb + 1]),
              [p1, gcum], [d["ES"][i]])
            p2 = small.next()
            mmf(p2, p2[:], cst, ONES, d["dGp"][i], d["dGp"][i][:], True, False)
            mmf(p2, p2[:], cst, I_, cst, MBT, False, True)
            V("act", lambda e, i=i, gb=gb, p2=p2: e.activation(out=d["E2"][i][:], in_=p2[:], func=AF.Exp, bias=ngcum[:, gb:gb + 1]),
              [p2, ngcum], [d["E2"][i]])
            p3 = small.next()
            mmf(p3, p3[:], cst, ONES, d["dGp"][i], d["dGp"][i][:])
            V("act", lambda e, i=i, p3=p3: e.activation(out=d["EG"][i][:], in_=p3[:], func=AF.Exp), [p3], [d["EG"][i]])
        yield
        for i in blks:
            gb = t * NB + i
            pk = small.next()
            mmf(pk, pk[:], kT, kT[:, bs(i)], kT2, kT2[:, bs(i)])
            V("act", lambda e, i=i, gb=gb, pk=pk: e.activation(out=d["A"][i][:], in_=pk[:], func=AF.Identity, scale=beta[:, gb:gb + 1]),
              [pk, beta], [d["A"][i]])
            V("dve", lambda e, i=i: e.tensor_tensor(out=d["A"][i][:], in0=d["A"][i][:], in1=d["ES"][i][:], op=ALU.mult),
              [d["A"][i], d["ES"][i]], [d["A"][i]])
        yield
        for i in blks:
            pt = small.next()
            mmf(pt, pt[:], d["A"][i], d["A"][i][:], cst, I_)
            V("act", lambda e, i=i, pt=pt: e.activation(out=d["Bm"][i][:], in_=pt[:], func=AF.Copy), [pt], [d["Bm"][i]])
            if B_VAR != 1:
                V("dve", lambda e, i=i, pt=pt: e.scalar_tensor_tensor(
                    out=d["N"][i][:], in0=pt[:], scalar=-1.0, in1=I_, op0=ALU.mult, op1=ALU.add), [pt, cst], [d["N"][i]])
        yield
        curA = [d["A"][i] for i in blks]
        curB = [d["Bm"][i] for i in blks]
        for lvl in range(1, 6):
            nA = d["A2"] if lvl % 2 == 1 else d["A2b"]
            nB = d["B2"] if lvl % 2 == 1 else d["B2b"]
            for i in blks:
                pa = small.next()
                mmf(pa, pa[:], curB[i], curB[i][:], curA[i], curA[i][:])
                V("act", lambda e, i=i, pa=pa, nA=nA: e.activation(out=nA[i][:], in_=pa[:], func=AF.Copy), [pa], [nA[i]])
                if lvl < 5:
                    pbb = small.next()
                    mmf(pbb, pbb[:], curA[i], curA[i][:], curB[i], curB[i][:])
                    V("dve", lambda e, i=i, pbb=pbb, nB=nB: e.tensor_copy(out=nB[i][:], in_=pbb[:]), [pbb], [nB[i]])
            for i in blks:
                pn = small.next()
                mmf(pn, pn[:], nA[i], nA[i][:], d["N"][i], d["N"][i][:])
                V("dve", lambda e, i=i, pn=pn: e.tensor_tensor(out=d["N"][i][:], in0=pn[:], in1=d["N"][i][:], op=ALU.add),
                  [pn, d["N"][i]], [d["N"][i]])
            curA = [nA[i] for i in blks]
            curB = [nB[i] for i in blks]
            yield
        for i in blks:
            gb = t * NB + i
            pk = small.next()
            mmf(pk, pk[:], kT, kT[:, bs(i)], cst, I_)
            V("dve", lambda e, i=i, gb=gb, pk=pk: e.tensor_scalar(out=d["RHSw"][i][:], in0=pk[:], scalar1=bexpg[:, gb:gb + 1], scalar2=None, op0=ALU.mult),
              [pk, bexpg], [d["RHSw"][i]])
            V("dve", lambda e, i=i, gb=gb, pk=pk: e.tensor_scalar(out=d["ktA"][i][:], in0=pk[:], scalar1=colA[:, gb:gb + 1], scalar2=None, op0=ALU.mult),
              [pk, colA], [d["ktA"][i]])
            V("dve", lambda e, i=i, gb=gb, pk=pk: e.tensor_scalar(out=d["ktB"][i][:], in0=pk[:], scalar1=colB[:, gb:gb + 1], scalar2=None, op0=ALU.mult),
              [pk, colB], [d["ktB"][i]])
            pv = small.next()
            mmf(pv, pv[:], vT, vT[:, bs(i)], cst, I_)
            V("dve", lambda e, i=i, gb=gb, pv=pv: e.tensor_scalar(out=d["RHSv"][i][:], in0=pv[:], scalar1=beta[:, gb:gb + 1], scalar2=None, op0=ALU.mult),
              [pv, beta], [d["RHSv"][i]])
        yield
        for i in blks:
            pu = small.next()
            mmf(pu, pu[:], d["N"][i], d["N"][i][:], d["RHSv"][i], d["RHSv"][i][:])
            V("act", lambda e, i=i, pu=pu: e.activation(out=d["u"][i][:], in_=pu[:], func=AF.Copy), [pu], [d["u"][i]])
            pw = small.next()
            mmf(pw, pw[:], d["RHSw"][i], d["RHSw"][i][:], d["N"][i], d["N"][i][:])
            V("act", lambda e, i=i, pw=pw: e.activation(out=d["wT"][i][:], in_=pw[:], func=AF.Copy), [pw], [d["wT"][i]])
            pq = small.next()
            mmf(pq, pq[:], kT, kT[:, bs(i)], qT, qT[:, bs(i)])
            V("dve", lambda e, i=i, pq=pq: e.tensor_tensor(out=d["qkT"][i][:], in0=pq[:], in1=d["E2"][i][:], op=ALU.mult),
              [pq, d["E2"][i]], [d["qkT"][i]])
            V("dve", lambda e, i=i: e.tensor_tensor(out=d["qdT"][i][:], in0=qT[:, bs(i)], in1=d["EG"][i][:], op=ALU.mult),
              [qT, d["EG"][i]], [d["qdT"][i]])
        yield

    def recur(t, d):
        oT = oT_r.next()
        for i in range(NB):
            gb = t * NB + i
            po = po_r.next()
            for half, kt, egl in ((0, d["ktA"][i], eglA), (1, d["ktB"][i], eglB)):
                S = S_r[state["S"]]
                S2 = S_r[1 - state["S"]]
                hs = slice(half * 64, half * 64 + 64)
                p1 = small.next()
                mmf(p1, p1[:], d["wT"][i], d["wT"][i][:], S, S[:])
                vn = vn_r.next()
                V("dve", lambda e, i=i, p1=p1, vn=vn: e.tensor_tensor(out=vn[:], in0=d["u"][i][:], in1=p1[:], op=ALU.subtract),
                  [d["u"][i], p1], [vn])
                mmf(po, po[:, hs], S, S[:], d["qdT"][i], d["qdT"][i][:, hs], True, False)
                mmf(po, po[:, hs], vn, vn[:], d["qkT"][i], d["qkT"][i][:, hs], False, True)
                p2 = small.next()
                mmf(p2, p2[:], kt, kt[:], vn, vn[:])
                V("dve", lambda e, S=S, S2=S2, p2=p2, egl=egl, gb=gb: e.scalar_tensor_tensor(
                    out=S2[:], in0=S[:], scalar=egl[:, gb:gb + 1], in1=p2[:], op0=ALU.mult, op1=ALU.add),
                  [S, egl, p2], [S2])
                state["S"] = 1 - state["S"]
                yield
            V("act", lambda e, i=i, po=po, oT=oT: e.activation(out=oT[:, i * 128:(i + 1) * 128], in_=po[:], func=AF.Copy), [po], [oT])
        V("act", lambda e: e.activation(out=osq[:], in_=oT[:], func=AF.Square), [oT], [osq])
        pb = big.next()
        mmf(pb, pb[:, 0:NT], ones128, ones128[:], osq, osq[:])
        V("act", lambda e, pb=pb: e.activation(out=orin[:], in_=pb[:, 0:NT], func=AF.Sqrt, bias=eps_c[:, 0:1]), [pb, eps_c], [orin])
        V("dve", lambda e: e.reciprocal(out=orin[:], in_=orin[:]), [orin], [orin])
        V("dve", lambda e, oT=oT: e.scalar_tensor_tensor(out=oT[:], in0=oT[:], scalar=scal[:, 2:3], in1=orin[:], op0=ALU.mult, op1=ALU.mult),
          [oT, scal, orin], [oT])
        stores.append(s.dma("sp", onT[:, t * NT:(t + 1) * NT], oT[:], oT, reads=[oT]))
        yield

    def drain(g):
        for _ in g:
            pass

    if B_DEBUG_STAGE is not None:
        g = prep(0, sets[0])
        for _ in range(B_DEBUG_STAGE):
            next(g, None)
        oT = oT_r.next()
        V("dve", lambda e: e.memset(oT[:], 1.0), [], [oT])
        stores.append(s.dma("sp", onT[:, 0:NT], oT[:], oT, reads=[oT]))
        s.finish(stores)
        s.emit()
        return nc
    drain(prep(0, sets[0]))
    for t in range(NTILE):
        r = recur(t, sets[t % 2])
        p = prep(t + 1, sets[(t + 1) % 2]) if t + 1 < NTILE else iter(())
        rd = pd = False
        while not (rd and pd):
            if not pd:
                try:
                    next(p)
                except StopIteration:
                    pd = True
            if not rd:
                try:
                    next(r)
                except StopIteration:
                    rd = True
    s.finish(stores)
    s.emit()
    return nc


NCH_C = 37


def build_C(NTOK, NT=256):
    nc = bass.Bass("TRN2", target_bir_lowering=False)
    NQB = NT // 128
    hT = nc.dram_tensor("hT", [8, 128, 128 + NTOK], F32, kind="ExternalInput").ap()
    odn_d = nc.dram_tensor("odn", [4, 128, NTOK], F32, kind="ExternalInput").ap()
    memT = nc.dram_tensor("memT", [8, 128, N_MEM], F32, kind="ExternalInput").ap()
    wc_d = nc.dram_tensor("wc", [NCH_C, 128, 8, 128], F32, kind="ExternalInput").ap()
    wv_d = nc.dram_tensor("wv", [128, 8, 128], F32, kind="ExternalInput").ap()
    wmk_d = nc.dram_tensor("wmk", [4, 128, 8, 128], F32, kind="ExternalInput").ap()
    wmv_d = nc.dram_tensor("wmv", [128, 8, 512], F32, kind="ExternalInput").ap()
    wbr_d = nc.dram_tensor("wbr", [24, 128, 4, 128], F32, kind="ExternalInput").ap()
    wo_d = nc.dram_tensor("wo", [8, 128, 8, 128], F32, kind="ExternalInput").ap()
    lnp_d = nc.dram_tensor("lnp", [128, 3, 2, 8], F32, kind="ExternalInput").ap()
    mlnp_d = nc.dram_tensor("mlnp", [128, 1, 2, 8], F32, kind="ExternalInput").ap()
    cst_d = nc.dram_tensor("cst", [128, 3, 128], F32, kind="ExternalInput").ap()
    msk_d = nc.dram_tensor("msk", [128, 2, 512], F32, kind="ExternalInput").ap()
    sink_d = nc.dram_tensor("sinkc", [128, 4], F32, kind="ExternalInput").ap()
    outT = nc.dram_tensor("outT", [8, 128, NTOK], F32, kind="ExternalOutput").ap()

    s = Sched(nc)
    c = load_consts(s, nc, cst_d)
    cst = c["cst"]
    lnp = s.tile([128, 3, 2, 8], F32, "lnp", dma=True)
    s.dma("sp", lnp[:], lnp_d, lnp, writes=[lnp])
    mlnp = s.tile([128, 1, 2, 8], F32, "mlnp", dma=True)
    s.dma("sp", mlnp[:], mlnp_d, mlnp, writes=[mlnp])
    msk = s.tile([128, 2, 512], F32, "msk", dma=True)
    s.dma("sp", msk[:], msk_d, msk, writes=[msk])
    sinkc = s.tile([128, 4], F32, "sinkc", dma=True)
    s.dma("sp", sinkc[:], sink_d, sinkc, writes=[sinkc])
    esink = s.tile([128, 4], F32, "esink")
    s.op("act", lambda e: e.activation(out=esink[:], in_=sinkc[:], func=AF.Exp), reads=[sinkc], writes=[esink])
    onesb = s.tile([128, 128], BF16, "onesb")
    s.op("dve", lambda e: e.tensor_copy(out=onesb[:], in_=cst[:, 2, :]), reads=[cst], writes=[onesb])
    onespad = s.tile([128, 2, 128], BF16, "onespad")
    s.op("dve", lambda e: e.memset(onespad[:], 0.0), writes=[onespad])
    for pos in range(2):
        s.op("dve", lambda e, pos=pos: e.memset(onespad[:, pos, pos * 64:pos * 64 + 64], 1.0), writes=[onespad])
    ps = make_psum(s)
    pring = Ring(ps["all"][0:6])
    ffn = FFNStage(s, NT, None, None, lnp, 1, c, ps, with_ffn=False)
    ybuf = s.tile([128, 8, NT], F32, "ybuf", dma=True)

    def V(eng, fn, reads, writes):
        s.op(eng, fn, reads=reads, writes=writes)

    wi = Ring([s.tile([128, 8, 128], BF16, "wi", dma=True) for _ in range(3)])
    wcR = [s.tile([128, 8, 128], BF16, "wcR", dma=True) for _ in range(NCH_C)]
    for ci in [8] + [i for i in range(NCH_C) if i != 8]:
        s.dma("pool", wcR[ci][:], wc_d[ci], wcR[ci], writes=[wcR[ci]])
    wbr = Ring([s.tile([128, 4, 128], BF16, "wbr", dma=True) for _ in range(3)])

    merged = s.tile([128, 8, NT], F32, "merged", dma=True)
    mem32 = merged if NT == N_MEM else s.tile([128, 8, N_MEM], F32, "mem32", dma=True)
    s.dma("sp", mem32[:], memT.rearrange("k p n -> p k n"), mem32, writes=[mem32])
    memb = s.tile([128, 8, N_MEM], BF16, "memb")
    mst = [s.tile([128, N_MEM], F32, f"mst{i}") for i in range(4)]
    pm, pe_ = ps["m"], ps["e"]
    for m in range(8):
        V("act", lambda e, m=m: e.activation(out=mst[0][:], in_=mem32[:, m, :], func=AF.Square), [mem32], [mst[0]])
        mm(s, pm, pm[:, 0:N_MEM], c["onesD"], c["onesD"][:], mem32, mem32[:, m, :], m == 0, m == 7)
        mm(s, pe_, pe_[:, 0:N_MEM], c["onesD"], c["onesD"][:], mst[0], mst[0][:], m == 0, m == 7)
    eps1 = s.tile([128, 1], F32, "eps1")
    V("dve", lambda e: e.memset(eps1[:], LN_EPS), [], [eps1])
    V("act", lambda e: e.activation(out=mst[1][:], in_=pm[:, 0:N_MEM], func=AF.Copy), [pm], [mst[1]])
    V("dve", lambda e: e.tensor_tensor(out=mst[2][:], in0=mst[1][:], in1=mst[1][:], op=ALU.mult), [mst[1]], [mst[2]])
    V("dve", lambda e: e.tensor_tensor(out=mst[2][:], in0=pe_[:, 0:N_MEM], in1=mst[2][:], op=ALU.subtract), [pe_, mst[2]], [mst[2]])
    V("act", lambda e: e.activation(out=mst[2][:], in_=mst[2][:], func=AF.Sqrt, bias=eps1[:, 0:1]), [mst[2], eps1], [mst[2]])
    V("dve", lambda e: e.reciprocal(out=mst[2][:], in_=mst[2][:]), [mst[2]], [mst[2]])
    for m in range(8):
        V("dve", lambda e, m=m: e.tensor_tensor(out=mst[3][:], in0=mem32[:, m, :], in1=mst[1][:], op=ALU.subtract), [mem32, mst[1]], [mst[3]])
        V("dve", lambda e: e.tensor_tensor(out=mst[3][:], in0=mst[3][:], in1=mst[2][:], op=ALU.mult), [mst[3], mst[2]], [mst[3]])
        V("act", lambda e, m=m: e.activation(out=memb[:, m, :], in_=mst[3][:], func=AF.Identity,
                                             scale=mlnp[:, 0, 0, m:m + 1], bias=mlnp[:, 0, 1, m:m + 1]), [mst[3], mlnp], [memb])
    kmemT = s.tile([128, 4, N_MEM], BF16, "kmemT")
    for h in range(4):
        w = wi.next()
        s.dma("pool", w[:], wmk_d[h], w, writes=[w])
        p = pring.next()
        for k in range(8):
            mm(s, p, p[:, 0:N_MEM], w, w[:, k, :], memb, memb[:, k, :], k == 0, k == 7)
        V("act", lambda e, h=h, p=p: e.activation(out=kmemT[:, h, :], in_=p[:, 0:N_MEM], func=AF.Copy), [p], [kmemT])
    wmv = s.tile([128, 8, 512], BF16, "wmv", dma=True)
    s.dma("pool", wmv[:], wmv_d, wmv, writes=[wmv])
    vmem = s.tile([128, 2, 512], BF16, "vmem")
    for mc in range(2):
        p = pring.next()
        for k in range(8):
            mm(s, p, p[:], memb, memb[:, k, mc * 128:(mc + 1) * 128], wmv, wmv[:, k, :], k == 0, k == 7)
        V("act", lambda e, mc=mc, p=p: e.activation(out=vmem[:, mc, :], in_=p[:], func=AF.Copy), [p], [vmem])
    wvb = s.tile([128, 8, 128], BF16, "wvb", dma=True)
    s.dma("pool", wvb[:], wv_d, wvb, writes=[wvb])

    NSLOT = 4
    kT = s.tile([128, NSLOT, 128], BF16, "kTs")
    vpad = s.tile([128, NSLOT, 4, 128], BF16, "vpad")
    V("dve", lambda e: e.memset(vpad[:], 0.0), [], [vpad])
    h32 = s.tile([128, 8, NT], F32, "h32", dma=True)
    hbr = [s.tile([128, 8, NT], BF16, "hb", dma=True) for _ in range(2)]
    hh32 = s.tile([128, 8, 128], F32, "hh32", dma=True)
    hhb = s.tile([128, 8, 128], BF16, "hhb")
    zs = s.tile([128, 4, NT], F32, "zs")
    swq = s.tile([128, 4, NT], BF16, "swq")
    xaq = s.tile([128, 4, NT], BF16, "xaq")
    gsb = Ring([s.tile([128, NT], F32, f"gsb{i}") for i in range(3)])
    odn = s.tile([128, 4, NT], F32, "odn", dma=True)
    br = [s.tile([128, 4, NT], BF16, f"br{i}") for i in range(3)]
    pexp = Ring([s.tile([128, 512], F32, f"pexp{i}") for i in range(2)])
    pT = Ring([s.tile([128, 512], BF16, f"pT{i}") for i in range(3)])
    pm_sw = [s.tile([128, 512], BF16, f"pmsw{i}") for i in range(8)]
    rden = Ring([s.tile([128, NT], F32, f"rden{i}") for i in range(2)])
    mergedb = s.tile([128, 8, NT], BF16, "mergedb")
    gtmp = Ring([s.tile([128, NT], F32, f"gtmp{i}") for i in range(2)])
    stores = []

    def kv_block(src_b, cols, slot):
        w = wcR[8]
        p = pring.next()
        for k in range(8):
            mm(s, p, p[:, 0:128], w, w[:, k, :], src_b, src_b[:, k, cols], k == 0, k == 7)
        V("act", lambda e, p=p, slot=slot: e.activation(out=kT[:, slot, :], in_=p[:, 0:128], func=AF.Copy), [p], [kT])
        p2 = pring.next()
        for k in range(8):
            mm(s, p2, p2[:, 0:128], src_b, src_b[:, k, cols], wvb, wvb[:, k, :], k == 0, k == 7)
        for kv in range(2):
            for pos in range(2):
                V("dve", lambda e, p2=p2, slot=slot, kv=kv, pos=pos: e.tensor_copy(
                    out=vpad[:, slot, kv * 2 + pos, pos * 64:pos * 64 + 64], in_=p2[:, kv * 64:kv * 64 + 64]), [p2], [vpad])

    s.dma("sp", hh32[:], hT[:, :, 0:128].rearrange("k p n -> p k n"), hh32, writes=[hh32])
    V("act", lambda e: e.activation(out=hhb[:], in_=hh32[:], func=AF.Copy), [hh32], [hhb])
    kv_block(hhb, slice(0, 128), 0)

    cur = {}

    def load_h(t):
        hbn = hbr[t % 2]
        s.dma("pool", hbn[:], hT[:, :, 128 + t * NT:128 + (t + 1) * NT].rearrange("k p n -> p k n"), hbn, writes=[hbn])

    def proj(ci, out_fn):
        hb = cur["hb"]
        w = wcR[ci]
        p = pring.next()
        for k in range(8):
            mm(s, p, p[:, 0:NT], w, w[:, k, :], hb, hb[:, k, :], k == 0, k == 7)
        out_fn(p)

    NTILES = NTOK // NT
    load_h(0)
    for t in range(NTILES):
        c0 = t * NT
        hb = hbr[t % 2]
        cur["hb"] = hb
        s.dma("sp", h32[:], hT[:, :, 128 + c0:128 + c0 + NT].rearrange("k p n -> p k n"), h32, writes=[h32])
        s.dma("sp", odn[:], odn_d[:, :, c0:c0 + NT].rearrange("h p n -> p h n"), odn, writes=[odn])
        for j in range(4):
            proj(j, lambda p, j=j: V("act", lambda e: e.activation(out=zs[:, j, :], in_=p[:, 0:NT], func=AF.Silu), [p], [zs]))
        for j in range(4):
            proj(4 + j, lambda p, j=j: V("act", lambda e: e.activation(out=swq[:, j, :], in_=p[:, 0:NT], func=AF.Copy), [p], [swq]))
        for j in range(4):
            proj(9 + j, lambda p, j=j: V("act", lambda e: e.activation(out=xaq[:, j, :], in_=p[:, 0:NT], func=AF.Copy), [p], [xaq]))
        for qb in range(NQB):
            gb = t * NQB + qb
            kv_block(hb, slice(qb * 128, (qb + 1) * 128), (gb + 1) % NSLOT)
        V("dve", lambda e: e.tensor_tensor(out=br[0][:], in0=odn[:], in1=zs[:], op=ALU.mult), [odn, zs], [br[0]])
        for h in range(4):
            pts = []
            for mc in range(2):
                p = pring.next()
                mm(s, p, p[:, 0:NT], kmemT, kmemT[:, h, mc * 128:(mc + 1) * 128], xaq, xaq[:, h, :], True, True)
                pt = pT.next()
                V("act", lambda e, p=p, pt=pt: e.activation(out=pt[:, 0:NT], in_=p[:, 0:NT], func=AF.Exp, scale=128 ** -0.5), [p], [pt])
                pts.append(pt)
            po, pd = pring.next(), pring.next()
            for mc in range(2):
                mm(s, po, po[:, 0:NT], vmem, vmem[:, mc, h * 128:(h + 1) * 128], pts[mc], pts[mc][:, 0:NT], mc == 0, mc == 1)
            for mc in range(2):
                mm(s, pd, pd[:, 0:NT], onesb, onesb[:], pts[mc], pts[mc][:, 0:NT], mc == 0, mc == 1)
            rd = rden.next()
            V("dve", lambda e, pd=pd, rd=rd: e.reciprocal(out=rd[:], in_=pd[:, 0:NT]), [pd], [rd])
            V("dve", lambda e, po=po, rd=rd, h=h: e.tensor_tensor(out=br[2][:, h, :], in0=po[:, 0:NT], in1=rd[:], op=ALU.mult), [po, rd], [br[2]])
        mi = 0 if t == 0 else 1
        for h in range(8):
            rows = slice(0, 64) if h < 4 else slice(64, 128)
            ch = h % 4
            p = pring.next()
            for qb in range(NQB):
                gb = t * NQB + qb
                qcols = slice(qb * 128, (qb + 1) * 128)
                mm(s, p, p[:, qb * 256:qb * 256 + 128], kT, kT[rows, gb % NSLOT, :], swq, swq[rows, ch, qcols], True, True)
                mm(s, p, p[:, qb * 256 + 128:qb * 256 + 256], kT, kT[rows, (gb + 1) % NSLOT, :], swq, swq[rows, ch, qcols], True, True)
            pe2 = pexp.next()
            V("act", lambda e, p=p, pe2=pe2: e.activation(out=pe2[:, 0:NQB * 256], in_=p[:, 0:NQB * 256], func=AF.Exp, scale=64 ** -0.5), [p], [pe2])
            V("dve", lambda e, pe2=pe2, h=h, mi=mi: e.tensor_tensor(out=pm_sw[h][:, 0:NQB * 256], in0=pe2[:, 0:NQB * 256], in1=msk[:, mi, 0:NQB * 256], op=ALU.mult),
              [pe2, msk], [pm_sw[h]])
        for pr in range(4):
            kv = pr // 2
            po, pd = pring.next(), pring.next()
            for qb in range(NQB):
                gb = t * NQB + qb
                qc = slice(qb * 128, (qb + 1) * 128)
                n = 0
                for pos in range(2):
                    h = pr * 2 + pos
                    for part, slot in ((0, gb % NSLOT), (1, (gb + 1) % NSLOT)):
                        pc = slice(qb * 256 + part * 128, qb * 256 + part * 128 + 128)
                        mm(s, po, po[:, qc], vpad, vpad[:, slot, kv * 2 + pos, :], pm_sw[h], pm_sw[h][:, pc], n == 0, n == 3)
                        n += 1
                n = 0
                for pos in range(2):
                    h = pr * 2 + pos
                    for part in range(2):
                        pc = slice(qb * 256 + part * 128, qb * 256 + part * 128 + 128)
                        mm(s, pd, pd[:, qc], onespad, onespad[:, pos, :], pm_sw[h], pm_sw[h][:, pc], n == 0, n == 3)
                        n += 1
            rd = rden.next()
            V("dve", lambda e, pd=pd, rd=rd, pr=pr: e.tensor_scalar(out=rd[:], in0=pd[:, 0:NT], scalar1=esink[:, pr:pr + 1], scalar2=None, op0=ALU.add),
              [pd, esink], [rd])
            V("dve", lambda e, rd=rd: e.reciprocal(out=rd[:], in_=rd[:]), [rd], [rd])
            V("dve", lambda e, po=po, rd=rd, pr=pr: e.tensor_tensor(out=br[1][:, pr, :], in0=po[:, 0:NT], in1=rd[:], op=ALU.mult), [po, rd], [br[1]])
        if t + 1 < NTILES:
            load_h(t + 1)
        for m in range(8):
            for n in range(3):
                gs = gsb.next()
                proj(13 + n * 8 + m, lambda p, gs=gs: V("act", lambda e: e.activation(out=gs[:], in_=p[:, 0:NT], func=AF.Sigmoid), [p], [gs]))
                w = wbr.next()
                s.dma("pool", w[:], wbr_d[n * 8 + m], w, writes=[w])
                p = pring.next()
                for k in range(4):
                    mm(s, p, p[:, 0:NT], w, w[:, k, :], br[n], br[n][:, k, :], k == 0, k == 3)
                if n == 0:
                    V("dve", lambda e, p=p, m=m, gs=gs: e.tensor_tensor(out=merged[:, m, :], in0=p[:, 0:NT], in1=gs[:], op=ALU.mult),
                      [p, gs], [merged])
                else:
                    g = gtmp.next()
                    V("dve", lambda e, p=p, gs=gs, g=g: e.tensor_tensor(out=g[:], in0=p[:, 0:NT], in1=gs[:], op=ALU.mult),
                      [p, gs], [g])
                    V("dve", lambda e, m=m, g=g: e.tensor_tensor(out=merged[:, m, :], in0=merged[:, m, :], in1=g[:], op=ALU.add),
                      [merged, g], [merged])
        V("act", lambda e: e.activation(out=mergedb[:], in_=merged[:], func=AF.Copy), [merged], [mergedb])
        for m in range(8):
            w = wi.next()
            s.dma("pool", w[:], wo_d[m], w, writes=[w])
            p = pring.next()
            for k in range(8):
                mm(s, p, p[:, 0:NT], w, w[:, k, :], mergedb, mergedb[:, k, :], k == 0, k == 7)
            V("dve", lambda e, p=p, m=m: e.scalar_tensor_tensor(out=ybuf[:, m, :], in0=p[:, 0:NT], scalar=1.0 / ALPHA, in1=h32[:, m, :],
                                                              op0=ALU.mult, op1=ALU.add), [p, h32], [ybuf])
        ffn.ln(ybuf, None, 1)
        stores.append(s.dma("sp", outT[:, :, c0:c0 + NT].rearrange("k p n -> p k n"), ybuf[:], ybuf, reads=[ybuf]))
    s.finish(stores)
    s.emit()
    return nc


def lay_wc(w_in):
    swq = w_in[:, 2056:2568].reshape(1024, 8, 64)
    swq_p = np.concatenate([np.concatenate([swq[:, j], swq[:, 4 + j]], axis=1) for j in range(4)], axis=1)
    wcat = np.concatenate([w_in[:, 1544:2056], swq_p, w_in[:, 2568:2696], w_in[:, 2824:3336], w_in[:, 3336:6408]], axis=1)
    return lay_w_kmc(wcat)


def lay_pkc(w):
    K, C = w.shape
    return np.ascontiguousarray(w.reshape(K // 128, 128, C).transpose(1, 0, 2))


def lay_wbr(wb):
    a = wb.reshape(3, 4, 128, 8, 128).transpose(0, 3, 2, 1, 4)
    return np.ascontiguousarray(a.reshape(24, 128, 4, 128))


def make_masks(halo_valid):
    k = np.arange(128)[:, None]
    q = np.arange(128)[None, :]
    prev = (k > q).astype(np.float32)
    cur = (k <= q).astype(np.float32)
    std = np.concatenate([prev, cur, prev, cur], axis=1)
    first = np.concatenate([prev * halo_valid, cur, prev, cur], axis=1)
    return np.ascontiguousarray(np.stack([first, std], axis=1)).astype(np.float32)


def lay_sink(sinks):
    c = np.zeros((128, 4), np.float32)
    for pr in range(4):
        c[:64, pr] = sinks[2 * pr]
        c[64:, pr] = sinks[2 * pr + 1]
    return c


_PROGS = {}
TOK_PER_CORE = SEQ * BATCH // NCORES
NSEG = SEQ // TOK_PER_CORE


def _prog(name):
    if name not in _PROGS:
        if name == "A":
            _PROGS[name] = build_A(TOK_PER_CORE)
        elif name == "B":
            _PROGS[name] = build_B(SEQ)
        elif name == "F":
            _PROGS[name] = build_A(TOK_PER_CORE, proj=False, ln_idx=2)
        else:
            _PROGS[name] = build_C(TOK_PER_CORE)
    return _PROGS[name]


def _run(name, in_maps):
    res = run_bass_kernel_spmd(_prog(name), in_maps, core_ids=list(range(NCORES)))
    return res.results


def kernel(x, mem, mem_ln_g, mem_ln_b, ln_g, ln_b, ffn1_w_gu, ffn1_w_down, w_in, dn_conv_w,
           dn_a_log, dn_dt_bias, dn_norm_w, swa_sinks, w_mem_kv, w_branch, w_out, ffn2_w_gu, ffn2_w_down):
    f = lambda a: np.asarray(a, dtype=np.float32)
    cur = f(x)
    mem = f(mem)
    cst = make_cst()
    cstB, ccol = make_cstB()
    NT_ = TOK_PER_CORE
    for l in range(DEPTH):
        win = f(w_in[l])
        lnp = lay_lnp(f(ln_g[l]), f(ln_b[l]))
        commonA = dict(wgu=lay_wgu(f(ffn1_w_gu[l])), wdn=lay_w_kmc(f(ffn1_w_down[l])), wqkv=lay_w_kmc(win[:, :1536]),
                       wba=lay_pkc(win[:, 1536:1544]), lnp=lnp, cst=cst)
        insA = []
        for c in range(NCORES):
            b, sg = divmod(c, NSEG)
            xs = cur[b, sg * NT_:(sg + 1) * NT_]
            insA.append(dict(xT=np.ascontiguousarray(xs.T.reshape(8, 128, NT_)), **commonA))
        rA = _run("A", insA)
        convw = f(dn_conv_w[l])
        insB = []
        for c in range(NCORES):
            b, hd = divmod(c, 4)
            qkvp = np.concatenate([rA[b * NSEG + sg]["qkvT"][[hd, 4 + hd, 8 + hd]] for sg in range(NSEG)], axis=2)
            brow = np.concatenate([rA[b * NSEG + sg]["baT"][hd] for sg in range(NSEG)])
            arow = np.concatenate([rA[b * NSEG + sg]["baT"][4 + hd] for sg in range(NSEG)])
            cw = np.stack([convw[:, w * 512 + hd * 128: w * 512 + hd * 128 + 128] for w in range(3)], 0)
            smallp = np.zeros((128, 32), np.float32)
            smallp[:, 0:12] = cw.transpose(2, 0, 1).reshape(128, 12)
            smallp[:, 12] = f(dn_a_log[l])[hd]
            smallp[:, 13] = f(dn_dt_bias[l])[hd]
            smallp[:, 14] = f(dn_norm_w[l])
            smallp[:, 15:17] = ccol
            insB.append(dict(qkvp=np.ascontiguousarray(qkvp), bcol=np.ascontiguousarray(brow.reshape(SEQ // 128, 128).T),
                             acol=np.ascontiguousarray(arow.reshape(SEQ // 128, 128).T),
                             smallp=smallp, cstB=cstB))
        rB = _run("B", insB)
        wkv = f(w_mem_kv[l])
        mg = np.stack([f(mem_ln_g)] * 3)
        mb = np.stack([f(mem_ln_b)] * 3)
        commonC = dict(wc=lay_wc(win), wv=lay_pkc(win[:, 2696:2824]), wmk=lay_w_kmc(wkv[:, :512]), wmv=lay_pkc(wkv[:, 512:]),
                       wbr=lay_wbr(f(w_branch[l])), wo=lay_w_kmc(f(w_out[l])),
                       lnp=lnp, mlnp=np.ascontiguousarray(lay_lnp(mg, mb)[:, 0:1]),
                       cst=cst, sinkc=lay_sink(f(swa_sinks[l])))
        insC = []
        for c in range(NCORES):
            b, sg = divmod(c, NSEG)
            hT = rA[c]["hT"]
            halo = rA[c - 1]["hT"][:, :, -128:] if sg > 0 else np.zeros((8, 128, 128), np.float32)
            odn = np.stack([rB[b * 4 + hd]["onT"][:, sg * NT_:(sg + 1) * NT_] for hd in range(4)], 0)
            insC.append(dict(hT=np.ascontiguousarray(np.concatenate([halo, hT], axis=2)), odn=np.ascontiguousarray(odn),
                             memT=np.ascontiguousarray(mem[b].T.reshape(8, 128, N_MEM)),
                             msk=make_masks(1.0 if sg > 0 else 0.0), **commonC))
        rC = _run("C", insC)
        commonF = dict(wgu=lay_wgu(f(ffn2_w_gu[l])), wdn=lay_w_kmc(f(ffn2_w_down[l])), lnp=lnp, cst=cst)
        rF = _run("F", [dict(xT=rC[c]["outT"], **commonF) for c in range(NCORES)])
        nxt = np.empty_like(cur)
        for c in range(NCORES):
            b, sg = divmod(c, NSEG)
            nxt[b, sg * NT_:(sg + 1) * NT_] = rF[c]["hT"].reshape(D_MODEL, NT_).T
        cur = nxt
    return cur
```

```python
import math
import numpy as np
import concourse.bass as bass
import concourse.mybir as mybir
from concourse.bass_utils import run_bass_kernel_spmd

F32 = mybir.dt.float32
BF16 = mybir.dt.bfloat16
AF = mybir.ActivationFunctionType
ALU = mybir.AluOpType

D_MODEL = 1024
BATCH = 2
SEQ = 16384
DEPTH = 2
N_MEM = 256
D_FF = 2816
D_IN = 6408
NCORES = 8
ALPHA = (2 * DEPTH) ** 0.25
LN_EPS = 1e-5
RMS_EPS = 1e-6
SAME_ENGINE_SYNC = True
PSUM_READ_SERIALIZE = True


class T:
    def __init__(self, h, sem=None):
        self.h = h
        self.w = None
        self.r = {}
        self.sem = sem
        self.semval = 0

    def __getitem__(self, idx):
        return self.h[idx]


class TV:
    def __init__(self, parent, ap):
        self.p = parent
        self.ap = ap

    @property
    def is_psum(self):
        return getattr(self.p, "is_psum", False)

    @property
    def w(self):
        return self.p.w

    @w.setter
    def w(self, v):
        self.p.w = v

    @property
    def r(self):
        return self.p.r

    @r.setter
    def r(self, v):
        self.p.r = v

    def __getitem__(self, idx):
        return self.ap[idx]


class Sched:
    ENG = ("pe", "act", "dve", "pool", "sp")

    def __init__(self, nc):
        self.nc = nc
        self.prog = {e: [] for e in self.ENG}
        self.sem = {e: nc.alloc_semaphore("sem_" + e) for e in ("pe", "act", "dve", "pool")}
        self.cnt = {e: 0 for e in self.sem}
        self.seen = {e: {} for e in self.ENG}
        self.uid = 0
        self.final = []

    def tile(self, shape, dt, name=None, psum=False, dma=False):
        self.uid += 1
        name = f"{name or 't'}_{self.uid}"
        if psum:
            h = self.nc.alloc_psum_tensor(name, list(shape), dt)
        else:
            h = self.nc.alloc_sbuf_tensor(name, list(shape), dt)
        sem = self.nc.alloc_semaphore("ds_" + name) if dma else None
        t = T(h, sem)
        t.is_psum = psum
        return t

    def _dep(self, eng, ev):
        key, sem, val = ev
        if key == eng and (eng == "pe" or not SAME_ENGINE_SYNC):
            return
        if self.seen[eng].get(key, 0) >= val:
            return
        self.seen[eng][key] = val
        self.prog[eng].append(lambda e, sem=sem, val=val: e.wait_ge(sem, val))

    def _deps(self, eng, reads, writes):
        for t in reads:
            if t.w is not None:
                self._dep(eng, t.w)
            if PSUM_READ_SERIALIZE and getattr(t, "is_psum", False):
                for ev in t.r.values():
                    if ev[0] != eng:
                        self._dep(eng, ev)
        for t in writes:
            if t.w is not None:
                self._dep(eng, t.w)
            for ev in t.r.values():
                self._dep(eng, ev)

    def _mark(self, ev, reads, writes):
        for t in reads:
            t.r[ev[0]] = ev
        for t in writes:
            t.w = ev
            t.r = {}

    def op(self, eng, fn, reads=(), writes=()):
        self._deps(eng, reads, writes)
        self.cnt[eng] += 1
        val = self.cnt[eng]
        sem = self.sem[eng]
        self.prog[eng].append(lambda e, fn=fn, sem=sem: fn(e).then_inc(sem, 1))
        self._mark((eng, sem, val), reads, writes)

    def dma(self, q, out, in_, semtile, reads=(), writes=()):
        self._deps(q, reads, writes)
        semtile.semval += 16
        sem, val = semtile.sem, semtile.semval
        self.prog[q].append(lambda e, out=out, in_=in_, sem=sem: e.dma_start(out=out, in_=in_).then_inc(sem, 16))
        ev = (("d", id(semtile)), sem, val)
        self._mark(ev, reads, writes)
        return ev

    def finish(self, evs):
        for ev in evs:
            self._dep("sp", ev)

    def emit(self):
        nc = self.nc
        with nc.Block() as block:
            @block.tensor
            def _(e):
                for f in self.prog["pe"]:
                    f(e)

            @block.scalar
            def _(e):
                for f in self.prog["act"]:
                    f(e)

            @block.vector
            def _(e):
                for f in self.prog["dve"]:
                    f(e)

            @block.gpsimd
            def _(e):
                for f in self.prog["pool"]:
                    f(e)

            @block.sync
            def _(e):
                for f in self.prog["sp"]:
                    f(e)


class Ring:
    def __init__(self, tiles):
        self.tiles = tiles
        self.i = 0

    def next(self):
        t = self.tiles[self.i % len(self.tiles)]
        self.i += 1
        return t


def mm(s, out_t, out_ap, l_t, l_ap, r_t, r_ap, start, stop):
    s.op("pe", lambda e: e.matmul(out_ap, lhsT=l_ap, rhs=r_ap, start=start, stop=stop),
         reads=[l_t, r_t], writes=[out_t])


class FFNStage:
    def __init__(self, s, NT, wgu_d, wdn_d, lnp_t, ln_idx, consts, ps, resident=False, with_ffn=True):
        self.s, self.NT = s, NT
        self.wgu_d, self.wdn_d = wgu_d, wdn_d
        self.lnp, self.ln_idx = lnp_t, ln_idx
        self.c = consts
        self.ps = ps
        self.resident = resident
        if with_ffn:
            if resident:
                self.wgR = [s.tile([128, 8, 256], BF16, "wgR", dma=True) for _ in range(22)]
                self.wdR = [s.tile([128, 22, 128], BF16, "wdR", dma=True) for _ in range(8)]
                for j in range(22):
                    s.dma("pool", self.wgR[j][:], wgu_d[j], self.wgR[j], writes=[self.wgR[j]])
                for m in range(8):
                    s.dma("pool", self.wdR[m][:], wdn_d[m], self.wdR[m], writes=[self.wdR[m]])
            else:
                self.wg = Ring([s.tile([128, 8, 256], BF16, "wg", dma=True) for _ in range(3)])
                self.wd = Ring([s.tile([128, 22, 128], BF16, "wd", dma=True) for _ in range(2)])
            self.act = s.tile([128, 22, NT], BF16, "act")
            self.sg = Ring([s.tile([128, NT], F32, "sg") for _ in range(2)])
        self.ysq = Ring([s.tile([128, NT], F32, "ysq") for _ in range(2)])
        self.mean = s.tile([128, NT], F32, "mean")
        self.tmp = s.tile([128, NT], F32, "lntmp")
        self.rstd = s.tile([128, NT], F32, "rstd")
        self.d = Ring([s.tile([128, NT], F32, "lnd") for _ in range(2)])

    def run(self, x32, xb, hb, prefetch=None):
        s, NT = self.s, self.NT
        ps = self.ps
        for j in range(22):
            if self.resident:
                wg = self.wgR[j]
            else:
                wg = self.wg.next()
                s.dma("pool", wg[:], self.wgu_d[j], wg, writes=[wg])
            pg, pu = ps["g"].next(), ps["u"].next()
            for k in range(8):
                mm(s, pg, pg[:, 0:NT], wg, wg[:, k, 0:128], xb, xb[:, k, :], k == 0, k == 7)
            for k in range(8):
                mm(s, pu, pu[:, 0:NT], wg, wg[:, k, 128:256], xb, xb[:, k, :], k == 0, k == 7)
            sg = self.sg.next()
            s.op("act", lambda e, sg=sg, pg=pg: e.activation(out=sg[:], in_=pg[:, 0:NT], func=AF.Silu),
                 reads=[pg], writes=[sg])
            s.op("dve", lambda e, sg=sg, pu=pu, j=j: e.tensor_tensor(out=self.act[:, j, :], in0=sg[:], in1=pu[:, 0:NT], op=ALU.mult),
                 reads=[sg, pu], writes=[self.act])
        for m in range(8):
            if self.resident:
                wd = self.wdR[m]
            else:
                wd = self.wd.next()
                s.dma("pool", wd[:], self.wdn_d[m], wd, writes=[wd])
            py = ps["y"].next()
            for k in range(22):
                mm(s, py, py[:, 0:NT], wd, wd[:, k, :], self.act, self.act[:, k, :], k == 0, k == 21)
            s.op("dve", lambda e, py=py, m=m: e.scalar_tensor_tensor(
                out=x32[:, m, :], in0=py[:, 0:NT], scalar=0.5 / ALPHA, in1=x32[:, m, :], op0=ALU.mult, op1=ALU.add),
                reads=[py, x32], writes=[x32])
        if prefetch is not None:
            prefetch()
        self.ln(x32, hb, self.ln_idx)

    def ln(self, y, hb, li):
        s, NT, ps = self.s, self.NT, self.ps
        pm, pe_ = ps["m"], ps["e"]
        for m in range(8):
            ysq = self.ysq.next()
            s.op("act", lambda e, ysq=ysq, m=m: e.activation(out=ysq[:], in_=y[:, m, :], func=AF.Square),
                 reads=[y], writes=[ysq])
            mm(s, pm, pm[:, 0:NT], self.c["onesD"], self.c["onesD"][:], y, y[:, m, :], m == 0, m == 7)
            mm(s, pe_, pe_[:, 0:NT], self.c["onesD"], self.c["onesD"][:], ysq, ysq[:], m == 0, m == 7)
        s.op("act", lambda e: e.activation(out=self.mean[:], in_=pm[:, 0:NT], func=AF.Copy), reads=[pm], writes=[self.mean])
        s.op("dve", lambda e: e.tensor_tensor(out=self.tmp[:], in0=self.mean[:], in1=self.mean[:], op=ALU.mult),
             reads=[self.mean], writes=[self.tmp])
        s.op("dve", lambda e: e.tensor_tensor(out=self.tmp[:], in0=pe_[:, 0:NT], in1=self.tmp[:], op=ALU.subtract),
             reads=[pe_, self.tmp], writes=[self.tmp])
        s.op("act", lambda e: e.activation(out=self.tmp[:], in_=self.tmp[:], func=AF.Sqrt, bias=self.c["epsln"][:, 0:1]),
             reads=[self.tmp, self.c["epsln"]], writes=[self.tmp])
        s.op("dve", lambda e: e.reciprocal(out=self.rstd[:], in_=self.tmp[:]), reads=[self.tmp], writes=[self.rstd])
        for m in range(8):
            d = self.d.next()
            s.op("dve", lambda e, d=d, m=m: e.tensor_tensor(out=d[:], in0=y[:, m, :], in1=self.mean[:], op=ALU.subtract),
                 reads=[y, self.mean], writes=[d])
            s.op("dve", lambda e, d=d: e.tensor_tensor(out=d[:], in0=d[:], in1=self.rstd[:], op=ALU.mult),
                 reads=[d, self.rstd], writes=[d])
            s.op("act", lambda e, d=d, m=m: e.activation(
                out=y[:, m, :], in_=d[:], func=AF.Identity,
                scale=self.lnp[:, li, 0, m:m + 1], bias=self.lnp[:, li, 1, m:m + 1]),
                reads=[d, self.lnp], writes=[y])
        if hb is not None:
            s.op("pool", lambda e: e.tensor_copy(out=hb[:], in_=y[:]), reads=[y], writes=[hb])


def make_psum(s):
    banks = [s.tile([128, 512], F32, f"ps{i}", psum=True) for i in range(8)]
    return {"g": Ring(banks[0:2]), "u": Ring(banks[2:4]), "y": Ring(banks[4:6]), "m": banks[6], "e": banks[7],
            "all": banks}


def load_consts(s, nc, cst_d):
    c = {}
    cst = s.tile([128, 3, 128], F32, "cst", dma=True)
    s.dma("sp", cst[:], cst_d, cst, writes=[cst])
    c["cst"] = cst
    onesD = s.tile([128, 128], F32, "onesD")
    s.op("dve", lambda e: e.tensor_copy(out=onesD[:], in_=cst[:, 0, :]), reads=[cst], writes=[onesD])
    c["onesD"] = onesD
    eps = s.tile([128, 1], F32, "epsln")
    s.op("dve", lambda e: e.memset(eps[:], LN_EPS / (ALPHA * ALPHA)), writes=[eps])
    c["epsln"] = eps
    return c


def build_A(NTOK, NT=256, proj=True, ln_idx=0):
    nc = bass.Bass("TRN2", target_bir_lowering=False)
    xT = nc.dram_tensor("xT", [8, 128, NTOK], F32, kind="ExternalInput").ap()
    wgu = nc.dram_tensor("wgu", [22, 128, 8, 256], F32, kind="ExternalInput").ap()
    wdn = nc.dram_tensor("wdn", [8, 128, 22, 128], F32, kind="ExternalInput").ap()
    if proj:
        wqkv = nc.dram_tensor("wqkv", [12, 128, 8, 128], F32, kind="ExternalInput").ap()
        wba = nc.dram_tensor("wba", [128, 8, 8], F32, kind="ExternalInput").ap()
    lnp_d = nc.dram_tensor("lnp", [128, 3, 2, 8], F32, kind="ExternalInput").ap()
    cst_d = nc.dram_tensor("cst", [128, 3, 128], F32, kind="ExternalInput").ap()
    hT = nc.dram_tensor("hT", [8, 128, NTOK], F32, kind="ExternalOutput").ap()
    if proj:
        qkvT = nc.dram_tensor("qkvT", [12, 128, NTOK], F32, kind="ExternalOutput").ap()
        baT = nc.dram_tensor("baT", [8, NTOK], F32, kind="ExternalOutput").ap()

    s = Sched(nc)
    c = load_consts(s, nc, cst_d)
    lnp = s.tile([128, 3, 2, 8], F32, "lnp", dma=True)
    s.dma("sp", lnp[:], lnp_d, lnp, writes=[lnp])
    ps = make_psum(s)
    ffn = FFNStage(s, NT, wgu, wdn, lnp, ln_idx, c, ps, resident=True)
    x32r = [s.tile([128, 8, NT], F32, "x32", dma=True) for _ in range(2)]
    xbr = [s.tile([128, 8, NT], BF16, "xb") for _ in range(2)]
    if proj:
        wb = s.tile([128, 8, 8], BF16, "wba", dma=True)
        s.dma("pool", wb[:], wba, wb, writes=[wb])
        wqR = [s.tile([128, 8, 128], BF16, "wqR", dma=True) for _ in range(12)]
        for m in range(12):
            s.dma("pool", wqR[m][:], wqkv[m], wqR[m], writes=[wqR[m]])
        qo = Ring([s.tile([128, NT], F32, "qo", dma=True) for _ in range(4)])
        bo = s.tile([8, NT], F32, "bo", dma=True)
    stores = []
    NTILES = NTOK // NT

    def load(t):
        x32, xb = x32r[t % 2], xbr[t % 2]
        cols = slice(t * NT, (t + 1) * NT)
        s.dma("sp", x32[:], xT[:, :, cols].rearrange("k p n -> p k n"), x32, writes=[x32])
        s.op("act", lambda e: e.activation(out=xb[:], in_=x32[:], func=AF.Copy), reads=[x32], writes=[xb])

    load(0)
    for t in range(NTILES):
        cols = slice(t * NT, (t + 1) * NT)
        x32, xb = x32r[t % 2], xbr[t % 2]
        hb = xb if proj else None
        ffn.run(x32, xb, hb, prefetch=(lambda t=t: load(t + 1)) if t + 1 < NTILES else None)
        stores.append(s.dma("sp", hT[:, :, cols].rearrange("k p n -> p k n"), x32[:], x32, reads=[x32]))
        if not proj:
            continue
        for m in range(12):
            w = wqR[m]
            pq = ps["g"].next()
            for k in range(8):
                mm(s, pq, pq[:, 0:NT], w, w[:, k, :], hb, hb[:, k, :], k == 0, k == 7)
            q = qo.next()
            s.op("act", lambda e, q=q, pq=pq: e.activation(out=q[:], in_=pq[:, 0:NT], func=AF.Copy), reads=[pq], writes=[q])
            stores.append(s.dma("sp", qkvT[m][:, cols], q[:], q, reads=[q]))
        pb = ps["u"].next()
        for k in range(8):
            mm(s, pb, pb[0:8, 0:NT], wb, wb[:, k, :], hb, hb[:, k, :], k == 0, k == 7)
        s.op("act", lambda e, pb=pb: e.activation(out=bo[:], in_=pb[0:8, 0:NT], func=AF.Copy), reads=[pb], writes=[bo])
        stores.append(s.dma("sp", baT[:, cols], bo[:], bo, reads=[bo]))
    s.finish(stores)
    s.emit()
    return nc


def lay_w_kmc(w, ncols_chunk=128):
    K, M = w.shape
    return np.ascontiguousarray(w.reshape(K // 128, 128, M // ncols_chunk, ncols_chunk).transpose(2, 1, 0, 3))


def lay_wgu(w):
    g = lay_w_kmc(w[:, :D_FF])
    u = lay_w_kmc(w[:, D_FF:])
    return np.ascontiguousarray(np.concatenate([g, u], axis=3))


def lay_lnp(g, b):
    a = np.stack([g.reshape(3, 8, 128), b.reshape(3, 8, 128)], axis=1)
    return np.ascontiguousarray(a.transpose(3, 0, 1, 2))


def make_cst():
    c = np.zeros((128, 3, 128), np.float32)
    c[:, 0, :] = 1.0 / D_MODEL
    c[:, 1, :] = np.eye(128, dtype=np.float32)
    c[:, 2, :] = 1.0
    return c


NEG = -1e30
B_DEBUG_STAGE = None
CONV_ENG = "dve"
AUX_ENG = "pool"
B_VAR = 0


def make_cstB():
    idx = np.arange(128)
    same = (idx[:, None] // 64) == (idx[None, :] // 64)
    c = np.zeros((128, 7, 128), np.float32)
    c[:, 0, :] = np.eye(128)
    c[:, 1, :] = 1.0
    c[:, 2, :] = (same & (idx[:, None] <= idx[None, :]))
    c[:, 3, :] = np.where(same & (idx[:, None] > idx[None, :]), 0.0, NEG)
    c[:, 4, :] = np.where(same & (idx[None, :] >= idx[:, None]), 0.0, NEG)
    c[63, 5, :] = 1.0
    c[127, 6, :] = 1.0
    col = np.zeros((128, 2), np.float32)
    col[64:, 0] = NEG
    col[:64, 1] = NEG
    return c, col


def build_B(S_LEN, NT=512):
    nc = bass.Bass("TRN2", target_bir_lowering=False)
    NBLK = S_LEN // 128
    qkvp = nc.dram_tensor("qkvp", [3, 128, S_LEN], F32, kind="ExternalInput").ap()
    bcol_d = nc.dram_tensor("bcol", [128, NBLK], F32, kind="ExternalInput").ap()
    acol_d = nc.dram_tensor("acol", [128, NBLK], F32, kind="ExternalInput").ap()
    smallp_d = nc.dram_tensor("smallp", [128, 32], F32, kind="ExternalInput").ap()
    cst_d = nc.dram_tensor("cstB", [128, 7, 128], F32, kind="ExternalInput").ap()
    onT = nc.dram_tensor("onT", [128, S_LEN], F32, kind="ExternalOutput").ap()

    s = Sched(nc)
    cst = s.tile([128, 7, 128], F32, "cstB", dma=True)
    s.dma("sp", cst[:], cst_d, cst, writes=[cst])
    smallp = s.tile([128, 32], F32, "smallp", dma=True)
    s.dma("sp", smallp[:], smallp_d, smallp, writes=[smallp])
    ccol = TV(smallp, smallp[:, 15:17])
    bcol = s.tile([128, NBLK], F32, "bcol", dma=True)
    s.dma("sp", bcol[:], bcol_d, bcol, writes=[bcol])
    acol = s.tile([128, NBLK], F32, "acol", dma=True)
    s.dma("sp", acol[:], acol_d, acol, writes=[acol])
    convw = TV(smallp, smallp[:, 0:12].rearrange("p (w j) -> p w j", j=4))
    scal = TV(smallp, smallp[:, 12:15])
    I_, ONES, TRI, MBS, MBT, SELA, SELB = [cst[:, i, :] for i in range(7)]

    banks = [s.tile([128, 512], F32, f"psB{i}", psum=True) for i in range(8)]
    big = Ring(banks[0:2])
    small = Ring([TV(bk, bk[:, q * 128:(q + 1) * 128]) for q in range(4) for bk in banks[3:8]])
    po_r = Ring([TV(banks[2], banks[2][:, q * 128:(q + 1) * 128]) for q in range(4)])

    def V(eng, fn, reads, writes):
        s.op(eng, fn, reads=reads, writes=writes)

    def mmf(out_t, out_ap, l_t, l_ap, r_t, r_ap, start=True, stop=True):
        mm(s, out_t, out_ap, l_t, l_ap, r_t, r_ap, start, stop)

    def col_tile(name, n=NBLK):
        return s.tile([128, n], F32, name)

    one_c = s.tile([128, 1], F32, "one_c")
    V("dve", lambda e: e.memset(one_c[:], 1.0), [], [one_c])
    eps_c = s.tile([128, 1], F32, "eps_c")
    V("dve", lambda e: e.memset(eps_c[:], RMS_EPS), [], [eps_c])
    ones128 = s.tile([128, 128], F32, "ones128")
    V("dve", lambda e: e.tensor_scalar(out=ones128[:], in0=ONES, scalar1=1.0 / 128, scalar2=None, op0=ALU.mult), [cst], [ones128])
    beta = col_tile("beta")
    V("act", lambda e: e.activation(out=beta[:], in_=bcol[:], func=AF.Sigmoid), [bcol], [beta])
    xg = col_tile("xg")
    V("dve", lambda e: e.tensor_scalar(out=xg[:], in0=acol[:], scalar1=scal[:, 1:2], scalar2=None, op0=ALU.add), [acol, scal], [xg])
    ax = col_tile("ax")
    V("dve", lambda e: e.tensor_scalar(out=ax[:], in0=xg[:], scalar1=-1.0, scalar2=None, op0=ALU.mult), [xg], [ax])
    V("dve", lambda e: e.tensor_tensor(out=ax[:], in0=ax[:], in1=xg[:], op=ALU.max), [ax, xg], [ax])
    V("act", lambda e: e.activation(out=ax[:], in_=ax[:], func=AF.Exp, scale=-1.0), [ax], [ax])
    V("act", lambda e: e.activation(out=ax[:], in_=ax[:], func=AF.Ln, bias=one_c[:, 0:1]), [ax, one_c], [ax])
    V("dve", lambda e: e.tensor_scalar(out=xg[:], in0=xg[:], scalar1=0.0, scalar2=None, op0=ALU.max), [xg], [xg])
    V("dve", lambda e: e.tensor_tensor(out=xg[:], in0=xg[:], in1=ax[:], op=ALU.add), [xg, ax], [xg])
    nea = s.tile([128, 1], F32, "nea")
    V("act", lambda e: e.activation(out=nea[:], in_=scal[:, 0:1], func=AF.Exp), [scal], [nea])
    V("dve", lambda e: e.tensor_scalar(out=nea[:], in0=nea[:], scalar1=-1.0, scalar2=None, op0=ALU.mult), [nea], [nea])
    gc = col_tile("gc")
    V("dve", lambda e: e.tensor_scalar(out=gc[:], in0=xg[:], scalar1=nea[:, 0:1], scalar2=None, op0=ALU.mult), [xg, nea], [gc])
    gcum, ngcum, bexpg, colA, colB, eglA, eglB = [col_tile(n) for n in ("gcum", "ngcum", "bexpg", "colA", "colB", "eglA", "eglB")]
    for c0 in range(0, NBLK, 512):
        c1 = min(NBLK, c0 + 512)
        w = c1 - c0
        pb = big.next()
        mmf(pb, pb[:, 0:w], cst, TRI, gc, gc[:, c0:c1])
        V("act", lambda e, pb=pb, c0=c0, c1=c1, w=w: e.activation(out=gcum[:, c0:c1], in_=pb[:, 0:w], func=AF.Copy), [pb], [gcum])
    V("dve", lambda e: e.tensor_scalar(out=ngcum[:], in0=gcum[:], scalar1=-1.0, scalar2=None, op0=ALU.mult), [gcum], [ngcum])
    V("act", lambda e: e.activation(out=bexpg[:], in_=gcum[:], func=AF.Exp), [gcum], [bexpg])
    V("dve", lambda e: e.tensor_tensor(out=bexpg[:], in0=bexpg[:], in1=beta[:], op=ALU.mult), [bexpg, beta], [bexpg])
    for (SEL, col, egl, mi) in ((SELA, colA, eglA, 0), (SELB, colB, eglB, 1)):
        for c0 in range(0, NBLK, 512):
            c1 = min(NBLK, c0 + 512)
            w = c1 - c0
            pb = big.next()
            mmf(pb, pb[:, 0:w], cst, SEL, gcum, gcum[:, c0:c1])
            V("act", lambda e, pb=pb, egl=egl, c0=c0, c1=c1, w=w: e.activation(out=egl[:, c0:c1], in_=pb[:, 0:w], func=AF.Exp), [pb], [egl])
            V("dve", lambda e, pb=pb, col=col, c0=c0, c1=c1, w=w: e.tensor_tensor(out=col[:, c0:c1], in0=pb[:, 0:w], in1=gcum[:, c0:c1], op=ALU.subtract),
              [pb, gcum], [col])
        V("act", lambda e, col=col, mi=mi: e.activation(out=col[:], in_=col[:], func=AF.Exp, bias=ccol[:, mi:mi + 1]), [col, ccol], [col])

    S_r = [s.tile([128, 128], F32, f"S{i}") for i in range(2)]
    V("dve", lambda e: e.memset(S_r[0][:], 0.0), [], [S_r[0]])
    state = {"S": 0}
    NTILE = S_LEN // NT
    NB = NT // 128

    def alloc_set():
        d = {}
        d["pre"] = s.tile([128, 3, NT + 8], F32, "pre", dma=True)
        d["cv"] = [s.tile([128, NT], F32, f"cv{i}") for i in range(3)]
        d["sq"] = s.tile([128, NT], F32, "sq")
        d["rin"] = s.tile([128, NT], F32, "rin")
        d["qT"] = s.tile([128, NT], F32, "qT")
        d["kT"] = s.tile([128, NT], F32, "kT")
        d["kT2"] = s.tile([128, NT], F32, "kT2")
        for nm in ("dGn", "dGp", "ES", "E2", "EG", "A", "Bm", "N", "A2", "B2", "RHSv", "RHSw", "ktA", "ktB", "u", "wT", "qkT", "qdT"):
            d[nm] = [s.tile([128, 128], F32, f"{nm}{i}") for i in range(NB)]
        d["A2b"] = [s.tile([128, 128], F32, f"A2b{i}") for i in range(NB)]
        d["B2b"] = [s.tile([128, 128], F32, f"B2b{i}") for i in range(NB)]
        return d

    sets = [alloc_set(), alloc_set()]
    vn_r = Ring([s.tile([128, 128], F32, f"vn{i}") for i in range(2)])
    oT_r = Ring([s.tile([128, NT], F32, f"oT{i}", dma=True) for i in range(2)])
    osq = s.tile([128, NT], F32, "osq")
    orin = s.tile([128, NT], F32, "orin")
    stores = []

    def prep(t, d):
        c0 = t * NT
        pre = d["pre"]
        if t == 0:
            V("dve", lambda e: e.memset(pre[:, :, 0:8], 0.0), [], [pre])
            s.dma("sp", pre[:, :, 8:NT + 8], qkvp[:, :, 0:NT].rearrange("w p n -> p w n"), pre, writes=[pre])
        else:
            s.dma("sp", pre[:], qkvp[:, :, c0 - 8:c0 + NT].rearrange("w p n -> p w n"), pre, writes=[pre])
        for w in range(3):
            cv = d["cv"][w]
            V("dve", lambda e, cv=cv, w=w: e.tensor_scalar(out=cv[:], in0=pre[:, w, 5:5 + NT], scalar1=convw[:, w, 0:1], scalar2=None, op0=ALU.mult),
              [pre, convw], [cv])
            for j in range(1, 4):
                V("dve", lambda e, cv=cv, w=w, j=j: e.scalar_tensor_tensor(
                    out=cv[:], in0=pre[:, w, 5 + j:5 + j + NT], scalar=convw[:, w, j:j + 1], in1=cv[:], op0=ALU.mult, op1=ALU.add),
                  [pre, convw, cv], [cv])
            V("act", lambda e, cv=cv: e.activation(out=cv[:], in_=cv[:], func=AF.Silu), [cv], [cv])
        yield
        for w, dst, sc in ((0, d["qT"], 128 ** -0.5), (1, d["kT"], 1.0)):
            cv = d["cv"][w]
            V("act", lambda e, cv=cv: e.activation(out=d["sq"][:], in_=cv[:], func=AF.Square), [cv], [d["sq"]])
            pb = big.next()
            mmf(pb, pb[:, 0:NT], cst, ONES, d["sq"], d["sq"][:])
            V("act", lambda e, pb=pb: e.activation(out=d["rin"][:], in_=pb[:, 0:NT], func=AF.Sqrt, bias=eps_c[:, 0:1]), [pb, eps_c], [d["rin"]])
            V("dve", lambda e: e.reciprocal(out=d["rin"][:], in_=d["rin"][:]), [d["rin"]], [d["rin"]])
            V("dve", lambda e, cv=cv, dst=dst, sc=sc: e.scalar_tensor_tensor(
                out=dst[:], in0=cv[:], scalar=sc, in1=d["rin"][:], op0=ALU.mult, op1=ALU.mult), [cv, d["rin"]], [dst])
        vT = d["cv"][2]
        qT, kT = d["qT"], d["kT"]
        kT2 = d["kT2"]
        V("act", lambda e: e.activation(out=kT2[:], in_=kT[:], func=AF.Copy), [kT], [kT2])
        yield
        blks = range(NB)

        def bs(i):
            return slice(i * 128, (i + 1) * 128)
        for i in blks:
            gb = t * NB + i
            V("dve", lambda e, i=i, gb=gb: e.tensor_scalar(out=d["dGn"][i][:], in0=I_, scalar1=ngcum[:, gb:gb + 1], scalar2=None, op0=ALU.mult),
              [cst, ngcum], [d["dGn"][i]])
            V("dve", lambda e, i=i, gb=gb: e.tensor_scalar(out=d["dGp"][i][:], in0=I_, scalar1=gcum[:, gb:gb + 1], scalar2=None, op0=ALU.mult),
              [cst, gcum], [d["dGp"][i]])
        for i in blks:
            gb = t * NB + i
            p1 = small.next()
            mmf(p1, p1[:], cst, ONES, d["dGn"][i], d["dGn"][i][:], True, False)
            mmf(p1, p1[:], cst, I_, cst, MBS, False, True)
            V("act", lambda e, i=i, gb=gb, p1=p1: e.activation(out=d["ES"][i][:], in_=p1[:], func=AF.Exp, bias=gcum[:, gb:gb + 1]),
              [p1, gcum], [d["ES"][i]])
            p2 = small.next()
            mmf(p2, p2[:], cst, ONES, d["dGp"][i], d["dGp"][i][:], True, False)
            mmf(p2, p2[:], cst, I_, cst, MBT, False, True)
            V("act", lambda e, i=i, gb=gb, p2=p2: e.activation(out=d["E2"][i][:], in_=p2[:], func=AF.Exp, bias=ngcum[:, gb:gb + 1]),
              [p2, ngcum], [d["E2"][i]])
            p3 = small.next()
            mmf(p3, p3[:], cst, ONES, d["dGp"][i], d["dGp"][i][:])
            V("act", lambda e, i=i, p3=p3: e.activation(out=d["EG"][i][:], in_=p3[:], func=AF.Exp), [p3], [d["EG"][i]])
        yield
        for i in blks:
            gb = t * NB + i
            pk = small.next()
            mmf(pk, pk[:], kT, kT[:, bs(i)], kT2, kT2[:, bs(i)])
            V("act", lambda e, i=i, gb=gb, pk=pk: e.activation(out=d["A"][i][:], in_=pk[:], func=AF.Identity, scale=beta[:, gb:gb + 1]),
              [pk, beta], [d["A"][i]])
            V("dve", lambda e, i=i: e.tensor_tensor(out=d["A"][i][:], in0=d["A"][i][:], in1=d["ES"][i][:], op=ALU.mult),
              [d["A"][i], d["ES"][i]], [d["A"][i]])
        yield
        for i in blks:
            pt = small.next()
            mmf(pt, pt[:], d["A"][i], d["A"][i][:], cst, I_)
            V("act", lambda e, i=i, pt=pt: e.activation(out=d["Bm"][i][:], in_=pt[:], func=AF.Copy), [pt], [d["Bm"][i]])
            if B_VAR != 1:
                V("dve", lambda e, i=i, pt=pt: e.scalar_tensor_tensor(
                    out=d["N"][i][:], in0=pt[:], scalar=-1.0, in1=I_, op0=ALU.mult, op1=ALU.add), [pt, cst], [d["N"][i]])
        yield
        curA = [d["A"][i] for i in blks]
        curB = [d["Bm"][i] for i in blks]
        for lvl in range(1, 6):
            nA = d["A2"] if lvl % 2 == 1 else d["A2b"]
            nB = d["B2"] if lvl % 2 == 1 else d["B2b"]
            for i in blks:
                pa = small.next()
                mmf(pa, pa[:], curB[i], curB[i][:], curA[i], curA[i][:])
                V("act", lambda e, i=i, pa=pa, nA=nA: e.activation(out=nA[i][:], in_=pa[:], func=AF.Copy), [pa], [nA[i]])
                if lvl < 5:
                    pbb = small.next()
                    mmf(pbb, pbb[:], curA[i], curA[i][:], curB[i], curB[i][:])
                    V("dve", lambda e, i=i, pbb=pbb, nB=nB: e.tensor_copy(out=nB[i][:], in_=pbb[:]), [pbb], [nB[i]])
            for i in blks:
                pn = small.next()
                mmf(pn, pn[:], nA[i], nA[i][:], d["N"][i], d["N"][i][:])
                V("dve", lambda e, i=i, pn=pn: e.tensor_tensor(out=d["N"][i][:], in0=pn[:], in1=d["N"][i][:], op=ALU.add),
                  [pn, d["N"][i]], [d["N"][i]])
            curA = [nA[i] for i in blks]
            curB = [nB[i] for i in blks]
            yield
        for i in blks:
            gb = t * NB + i
            pk = small.next()
            mmf(pk, pk[:], kT, kT[:, bs(i)], cst, I_)
            V("dve", lambda e, i=i, gb=gb, pk=pk: e.tensor_scalar(out=d["RHSw"][i][:], in0=pk[:], scalar1=bexpg[:, gb:gb + 1], scalar2=None, op0=ALU.mult),
              [pk, bexpg], [d["RHSw"][i]])
            V("dve", lambda e, i=i, gb=gb, pk=pk: e.tensor_scalar(out=d["ktA"][i][:], in0=pk[:], scalar1=colA[:, gb:gb + 1], scalar2=None, op0=ALU.mult),
              [pk, colA], [d["ktA"][i]])
            V("dve", lambda e, i=i, gb=gb, pk=pk: e.tensor_scalar(out=d["ktB"][i][:], in0=pk[:], scalar1=colB[:, gb:gb + 1], scalar2=None, op0=ALU.mult),
              [pk, colB], [d["ktB"][i]])
            pv = small.next()
            mmf(pv, pv[:], vT, vT[:, bs(i)], cst, I_)
            V("dve", lambda e, i=i, gb=gb, pv=pv: e.tensor_scalar(out=d["RHSv"][i][:], in0=pv[:], scalar1=beta[:, gb:gb + 1], scalar2=None, op0=ALU.mult),
              [pv, beta], [d["RHSv"][i]])
        yield
        for i in blks:
            pu = small.next()
            mmf(pu, pu[:], d["N"][i], d["N"][i][:], d["RHSv"][i], d["RHSv"][i][:])
            V("act", lambda e, i=i, pu=pu: e.activation(out=d["u"][i][:], in_=pu[:], func=AF.Copy), [pu], [d["u"][i]])
            pw = small.next()
            mmf(pw, pw[:], d["RHSw"][i], d["RHSw"][i][:], d["N"][i], d["N"][i][:])
            V("act", lambda e, i=i, pw=pw: e.activation(out=d["wT"][i][:], in_=pw[:], func=AF.Copy), [pw], [d["wT"][i]])
            pq = small.next()
            mmf(pq, pq[:], kT, kT[:, bs(i)], qT, qT[:, bs(i)])
            V("dve", lambda e, i=i, pq=pq: e.tensor_tensor(out=d["qkT"][i][:], in0=pq[:], in1=d["E2"][i][:], op=ALU.mult),
              [pq, d["E2"][i]], [d["qkT"][i]])
            V("dve", lambda e, i=i: e.tensor_tensor(out=d["qdT"][i][:], in0=qT[:, bs(i)], in1=d["EG"][i][:], op=ALU.mult),
              [qT, d["EG"][i]], [d["qdT"][i]])
        yield

    def recur(t, d):
        oT = oT_r.next()
        for i in range(NB):
            gb = t * NB + i
            po = po_r.next()
            for half, kt, egl in ((0, d["ktA"][i], eglA), (1, d["ktB"][i], eglB)):
                S = S_r[state["S"]]
                S2 = S_r[1 - state["S"]]
                hs = slice(half * 64, half * 64 + 64)
                p1 = small.next()
                mmf(p1, p1[:], d["wT"][i], d["wT"][i][:], S, S[:])
                vn = vn_r.next()
                V("dve", lambda e, i=i, p1=p1, vn=vn: e.tensor_tensor(out=vn[:], in0=d["u"][i][:], in1=p1[:], op=ALU.subtract),
                  [d["u"][i], p1], [vn])
                mmf(po, po[:, hs], S, S[:], d["qdT"][i], d["qdT"][i][:, hs], True, False)
                mmf(po, po[:, hs], vn, vn[:], d["qkT"][i], d["qkT"][i][:, hs], False, True)
                p2 = small.next()
                mmf(p2, p2[:], kt, kt[:], vn, vn[:])
                V("dve", lambda e, S=S, S2=S2, p2=p2, egl=egl, gb=gb: e.scalar_tensor_tensor(
                    out=S2[:], in0=S[:], scalar=egl[:, gb:gb + 1], in1=p2[:], op0=ALU.mult, op1=ALU.add),
                  [S, egl, p2], [S2])
                state["S"] = 1 - state["S"]
                yield
            V("act", lambda e, i=i, po=po, oT=oT: e.activation(out=oT[:, i * 128:(i + 1) * 128], in_=po[:], func=AF.Copy), [po], [oT])
        V("act", lambda e: e.activation(out=osq[:], in_=oT[:], func=AF.Square), [oT], [osq])
        pb = big.next()
        mmf(pb, pb[:, 0:NT], ones128, ones128[:], osq, osq[:])
        V("act", lambda e, pb=pb: e.activation(out=orin[:], in_=pb[:, 0:NT], func=AF.Sqrt, bias=eps_c[:, 0:1]), [pb, eps_c], [orin])
        V("dve", lambda e: e.reciprocal(out=orin[:], in_=orin[:]), [orin], [orin])
        V("dve", lambda e, oT=oT: e.scalar_tensor_tensor(out=oT[:], in0=oT[:], scalar=scal[:, 2:3], in1=orin[:], op0=ALU.mult, op1=ALU.mult),
          [oT, scal, orin], [oT])
        stores.append(s.dma("sp", onT[:, t * NT:(t + 1) * NT], oT[:], oT, reads=[oT]))
        yield

    def drain(g):
        for _ in g:
            pass

    if B_DEBUG_STAGE is not None:
        g = prep(0, sets[0])
        for _ in range(B_DEBUG_STAGE):
            next(g, None)
        oT = oT_r.next()
        V("dve", lambda e: e.memset(oT[:], 1.0), [], [oT])
        stores.append(s.dma("sp", onT[:, 0:NT], oT[:], oT, reads=[oT]))
        s.finish(stores)
        s.emit()
        return nc
    drain(prep(0, sets[0]))
    for t in range(NTILE):
        r = recur(t, sets[t % 2])
        p = prep(t + 1, sets[(t + 1) % 2]) if t + 1 < NTILE else iter(())
        rd = pd = False
        while not (rd and pd):
            if not pd:
                try:
                    next(p)
                except StopIteration:
                    pd = True
            if not rd:
                try:
                    next(r)
                except StopIteration:
                    rd = True
    s.finish(stores)
    s.emit()
    return nc


NCH_C = 37


def build_C(NTOK, NT=256):
    nc = bass.Bass("TRN2", target_bir_lowering=False)
    NQB = NT // 128
    hT = nc.dram_tensor("hT", [8, 128, 128 + NTOK], F32, kind="ExternalInput").ap()
    odn_d = nc.dram_tensor("odn", [4, 128, NTOK], F32, kind="ExternalInput").ap()
    memT = nc.dram_tensor("memT", [8, 128, N_MEM], F32, kind="ExternalInput").ap()
    wc_d = nc.dram_tensor("wc", [NCH_C, 128, 8, 128], F32, kind="ExternalInput").ap()
    wv_d = nc.dram_tensor("wv", [128, 8, 128], F32, kind="ExternalInput").ap()
    wmk_d = nc.dram_tensor("wmk", [4, 128, 8, 128], F32, kind="ExternalInput").ap()
    wmv_d = nc.dram_tensor("wmv", [128, 8, 512], F32, kind="ExternalInput").ap()
    wbr_d = nc.dram_tensor("wbr", [24, 128, 4, 128], F32, kind="ExternalInput").ap()
    wo_d = nc.dram_tensor("wo", [8, 128, 8, 128], F32, kind="ExternalInput").ap()
    lnp_d = nc.dram_tensor("lnp", [128, 3, 2, 8], F32, kind="ExternalInput").ap()
    mlnp_d = nc.dram_tensor("mlnp", [128, 1, 2, 8], F32, kind="ExternalInput").ap()
    cst_d = nc.dram_tensor("cst", [128, 3, 128], F32, kind="ExternalInput").ap()
    msk_d = nc.dram_tensor("msk", [128, 2, 512], F32, kind="ExternalInput").ap()
    sink_d = nc.dram_tensor("sinkc", [128, 4], F32, kind="ExternalInput").ap()
    outT = nc.dram_tensor("outT", [8, 128, NTOK], F32, kind="ExternalOutput").ap()

    s = Sched(nc)
    c = load_consts(s, nc, cst_d)
    cst = c["cst"]
    lnp = s.tile([128, 3, 2, 8], F32, "lnp", dma=True)
    s.dma("sp", lnp[:], lnp_d, lnp, writes=[lnp])
    mlnp = s.tile([128, 1, 2, 8], F32, "mlnp", dma=True)
    s.dma("sp", mlnp[:], mlnp_d, mlnp, writes=[mlnp])
    msk = s.tile([128, 2, 512], F32, "msk", dma=True)
    s.dma("sp", msk[:], msk_d, msk, writes=[msk])
    sinkc = s.tile([128, 4], F32, "sinkc", dma=True)
    s.dma("sp", sinkc[:], sink_d, sinkc, writes=[sinkc])
    esink = s.tile([128, 4], F32, "esink")
    s.op("act", lambda e: e.activation(out=esink[:], in_=sinkc[:], func=AF.Exp), reads=[sinkc], writes=[esink])
    onesb = s.tile([128, 128], BF16, "onesb")
    s.op("dve", lambda e: e.tensor_copy(out=onesb[:], in_=cst[:, 2, :]), reads=[cst], writes=[onesb])
    onespad = s.tile([128, 2, 128], BF16, "onespad")
    s.op("dve", lambda e: e.memset(onespad[:], 0.0), writes=[onespad])
    for pos in range(2):
        s.op("dve", lambda e, pos=pos: e.memset(onespad[:, pos, pos * 64:pos * 64 + 64], 1.0), writes=[onespad])
    ps = make_psum(s)
    pring = Ring(ps["all"][0:6])
    ffn = FFNStage(s, NT, None, None, lnp, 1, c, ps, with_ffn=False)
    ybuf = s.tile([128, 8, NT], F32, "ybuf", dma=True)

    def V(eng, fn, reads, writes):
        s.op(eng, fn, reads=reads, writes=writes)

    wi = Ring([s.tile([128, 8, 128], BF16, "wi", dma=True) for _ in range(3)])
    wcR = [s.tile([128, 8, 128], BF16, "wcR", dma=True) for _ in range(NCH_C)]
    for ci in [8] + [i for i in range(NCH_C) if i != 8]:
        s.dma("pool", wcR[ci][:], wc_d[ci], wcR[ci], writes=[wcR[ci]])
    wbr = Ring([s.tile([128, 4, 128], BF16, "wbr", dma=True) for _ in range(3)])

    merged = s.tile([128, 8, NT], F32, "merged", dma=True)
    mem32 = merged if NT == N_MEM else s.tile([128, 8, N_MEM], F32, "mem32", dma=True)
    s.dma("sp", mem32[:], memT.rearrange("k p n -> p k n"), mem32, writes=[mem32])
    memb = s.tile([128, 8, N_MEM], BF16, "memb")
    mst = [s.tile([128, N_MEM], F32, f"mst{i}") for i in range(4)]
    pm, pe_ = ps["m"], ps["e"]
    for m in range(8):
        V("act", lambda e, m=m: e.activation(out=mst[0][:], in_=mem32[:, m, :], func=AF.Square), [mem32], [mst[0]])
        mm(s, pm, pm[:, 0:N_MEM], c["onesD"], c["onesD"][:], mem32, mem32[:, m, :], m == 0, m == 7)
        mm(s, pe_, pe_[:, 0:N_MEM], c["onesD"], c["onesD"][:], mst[0], mst[0][:], m == 0, m == 7)
    eps1 = s.tile([128, 1], F32, "eps1")
    V("dve", lambda e: e.memset(eps1[:], LN_EPS), [], [eps1])
    V("act", lambda e: e.activation(out=mst[1][:], in_=pm[:, 0:N_MEM], func=AF.Copy), [pm], [mst[1]])
    V("dve", lambda e: e.tensor_tensor(out=mst[2][:], in0=mst[1][:], in1=mst[1][:], op=ALU.mult), [mst[1]], [mst[2]])
    V("dve", lambda e: e.tensor_tensor(out=mst[2][:], in0=pe_[:, 0:N_MEM], in1=mst[2][:], op=ALU.subtract), [pe_, mst[2]], [mst[2]])
    V("act", lambda e: e.activation(out=mst[2][:], in_=mst[2][:], func=AF.Sqrt, bias=eps1[:, 0:1]), [mst[2], eps1], [mst[2]])
    V("dve", lambda e: e.reciprocal(out=mst[2][:], in_=mst[2][:]), [mst[2]], [mst[2]])
    for m in range(8):
        V("dve", lambda e, m=m: e.tensor_tensor(out=mst[3][:], in0=mem32[:, m, :], in1=mst[1][:], op=ALU.subtract), [mem32, mst[1]], [mst[3]])
        V("dve", lambda e: e.tensor_tensor(out=mst[3][:], in0=mst[3][:], in1=mst[2][:], op=ALU.mult), [mst[3], mst[2]], [mst[3]])
        V("act", lambda e, m=m: e.activation(out=memb[:, m, :], in_=mst[3][:], func=AF.Identity,
                                             scale=mlnp[:, 0, 0, m:m + 1], bias=mlnp[:, 0, 1, m:m + 1]), [mst[3], mlnp], [memb])
    kmemT = s.tile([128, 4, N_MEM], BF16, "kmemT")
    for h in range(4):
        w = wi.next()
        s.dma("pool", w[:], wmk_d[h], w, writes=[w])
        p = pring.next()
        for k in range(8):
            mm(s, p, p[:, 0:N_MEM], w, w[:, k, :], memb, memb[:, k, :], k == 0, k == 7)
        V("act", lambda e, h=h, p=p: e.activation(out=kmemT[:, h, :], in_=p[:, 0:N_MEM], func=AF.Copy), [p], [kmemT])
    wmv = s.tile([128, 8, 512], BF16, "wmv", dma=True)
    s.dma("pool", wmv[:], wmv_d, wmv, writes=[wmv])
    vmem = s.tile([128, 2, 512], BF16, "vmem")
    for mc in range(2):
        p = pring.next()
        for k in range(8):
            mm(s, p, p[:], memb, memb[:, k, mc * 128:(mc + 1) * 128], wmv, wmv[:, k, :], k == 0, k == 7)
        V("act", lambda e, mc=mc, p=p: e.activation(out=vmem[:, mc, :], in_=p[:], func=AF.Copy), [p], [vmem])
    wvb = s.tile([128, 8, 128], BF16, "wvb", dma=True)
    s.dma("pool", wvb[:], wv_d, wvb, writes=[wvb])

    NSLOT = 4
    kT = s.tile([128, NSLOT, 128], BF16, "kTs")
    vpad = s.tile([128, NSLOT, 4, 128], BF16, "vpad")
    V("dve", lambda e: e.memset(vpad[:], 0.0), [], [vpad])
    h32 = s.tile([128, 8, NT], F32, "h32", dma=True)
    hbr = [s.tile([128, 8, NT], BF16, "hb", dma=True) for _ in range(2)]
    hh32 = s.tile([128, 8, 128], F32, "hh32", dma=True)
    hhb = s.tile([128, 8, 128], BF16, "hhb")
    zs = s.tile([128, 4, NT], F32, "zs")
    swq = s.tile([128, 4, NT], BF16, "swq")
    xaq = s.tile([128, 4, NT], BF16, "xaq")
    gsb = Ring([s.tile([128, NT], F32, f"gsb{i}") for i in range(3)])
    odn = s.tile([128, 4, NT], F32, "odn", dma=True)
    br = [s.tile([128, 4, NT], BF16, f"br{i}") for i in range(3)]
    pexp = Ring([s.tile([128, 512], F32, f"pexp{i}") for i in range(2)])
    pT = Ring([s.tile([128, 512], BF16, f"pT{i}") for i in range(3)])
    pm_sw = [s.tile([128, 512], BF16, f"pmsw{i}") for i in range(8)]
    rden = Ring([s.tile([128, NT], F32, f"rden{i}") for i in range(2)])
    mergedb = s.tile([128, 8, NT], BF16, "mergedb")
    gtmp = Ring([s.tile([128, NT], F32, f"gtmp{i}") for i in range(2)])
    stores = []

    def kv_block(src_b, cols, slot):
        w = wcR[8]
        p = pring.next()
        for k in range(8):
            mm(s, p, p[:, 0:128], w, w[:, k, :], src_b, src_b[:, k, cols], k == 0, k == 7)
        V("act", lambda e, p=p, slot=slot: e.activation(out=kT[:, slot, :], in_=p[:, 0:128], func=AF.Copy), [p], [kT])
        p2 = pring.next()
        for k in range(8):
            mm(s, p2, p2[:, 0:128], src_b, src_b[:, k, cols], wvb, wvb[:, k, :], k == 0, k == 7)
        for kv in range(2):
            for pos in range(2):
                V("dve", lambda e, p2=p2, slot=slot, kv=kv, pos=pos: e.tensor_copy(
                    out=vpad[:, slot, kv * 2 + pos, pos * 64:pos * 64 + 64], in_=p2[:, kv * 64:kv * 64 + 64]), [p2], [vpad])

    s.dma("sp", hh32[:], hT[:, :, 0:128].rearrange("k p n -> p k n"), hh32, writes=[hh32])
    V("act", lambda e: e.activation(out=hhb[:], in_=hh32[:], func=AF.Copy), [hh32], [hhb])
    kv_block(hhb, slice(0, 128), 0)

    cur = {}

    def load_h(t):
        hbn = hbr[t % 2]
        s.dma("pool", hbn[:], hT[:, :, 128 + t * NT:128 + (t + 1) * NT].rearrange("k p n -> p k n"), hbn, writes=[hbn])

    def proj(ci, out_fn):
        hb = cur["hb"]
        w = wcR[ci]
        p = pring.next()
        for k in range(8):
            mm(s, p, p[:, 0:NT], w, w[:, k, :], hb, hb[:, k, :], k == 0, k == 7)
        out_fn(p)

    NTILES = NTOK // NT
    load_h(0)
    for t in range(NTILES):
        c0 = t * NT
        hb = hbr[t % 2]
        cur["hb"] = hb
        s.dma("sp", h32[:], hT[:, :, 128 + c0:128 + c0 + NT].rearrange("k p n -> p k n"), h32, writes=[h32])
        s.dma("sp", odn[:], odn_d[:, :, c0:c0 + NT].rearrange("h p n -> p h n"), odn, writes=[odn])
        for j in range(4):
            proj(j, lambda p, j=j: V("act", lambda e: e.activation(out=zs[:, j, :], in_=p[:, 0:NT], func=AF.Silu), [p], [zs]))
        for j in range(4):
            proj(4 + j, lambda p, j=j: V("act", lambda e: e.activation(out=swq[:, j, :], in_=p[:, 0:NT], func=AF.Copy), [p], [swq]))
        for j in range(4):
            proj(9 + j, lambda p, j=j: V("act", lambda e: e.activation(out=xaq[:, j, :], in_=p[:, 0:NT], func=AF.Copy), [p], [xaq]))
        for qb in range(NQB):
            gb = t * NQB + qb
            kv_block(hb, slice(qb * 128, (qb + 1) * 128), (gb + 1) % NSLOT)
        V("dve", lambda e: e.tensor_tensor(out=br[0][:], in0=odn[:], in1=zs[:], op=ALU.mult), [odn, zs], [br[0]])
        for h in range(4):
            pts = []
            for mc in range(2):
                p = pring.next()
                mm(s, p, p[:, 0:NT], kmemT, kmemT[:, h, mc * 128:(mc + 1) * 128], xaq, xaq[:, h, :], True, True)
                pt = pT.next()
                V("act", lambda e, p=p, pt=pt: e.activation(out=pt[:, 0:NT], in_=p[:, 0:NT], func=AF.Exp, scale=128 ** -0.5), [p], [pt])
                pts.append(pt)
            po, pd = pring.next(), pring.next()
            for mc in range(2):
                mm(s, po, po[:, 0:NT], vmem, vmem[:, mc, h * 128:(h + 1) * 128], pts[mc], pts[mc][:, 0:NT], mc == 0, mc == 1)
            for mc in range(2):
                mm(s, pd, pd[:, 0:NT], onesb, onesb[:], pts[mc], pts[mc][:, 0:NT], mc == 0, mc == 1)
            rd = rden.next()
            V("dve", lambda e, pd=pd, rd=rd: e.reciprocal(out=rd[:], in_=pd[:, 0:NT]), [pd], [rd])
            V("dve", lambda e, po=po, rd=rd, h=h: e.tensor_tensor(out=br[2][:, h, :], in0=po[:, 0:NT], in1=rd[:], op=ALU.mult), [po, rd], [br[2]])
        mi = 0 if t == 0 else 1
        for h in range(8):
            rows = slice(0, 64) if h < 4 else slice(64, 128)
            ch = h % 4
            p = pring.next()
            for qb in range(NQB):
                gb = t * NQB + qb
                qcols = slice(qb * 128, (qb + 1) * 128)
                mm(s, p, p[:, qb * 256:qb * 256 + 128], kT, kT[rows, gb % NSLOT, :], swq, swq[rows, ch, qcols], True, True)
                mm(s, p, p[:, qb * 256 + 128:qb * 256 + 256], kT, kT[rows, (gb + 1) % NSLOT, :], swq, swq[rows, ch, qcols], True, True)
            pe2 = pexp.next()
            V("act", lambda e, p=p, pe2=pe2: e.activation(out=pe2[:, 0:NQB * 256], in_=p[:, 0:NQB * 256], func=AF.Exp, scale=64 ** -0.5), [p], [pe2])
            V("dve", lambda e, pe2=pe2, h=h, mi=mi: e.tensor_tensor(out=pm_sw[h][:, 0:NQB * 256], in0=pe2[:, 0:NQB * 256], in1=msk[:, mi, 0:NQB * 256], op=ALU.mult),
              [pe2, msk], [pm_sw[h]])
        for pr in range(4):
            kv = pr // 2
            po, pd = pring.next(), pring.next()
            for qb in range(NQB):
                gb = t * NQB + qb
                qc = slice(qb * 128, (qb + 1) * 128)
                n = 0
                for pos in range(2):
                    h = pr * 2 + pos
                    for part, slot in ((0, gb % NSLOT), (1, (gb + 1) % NSLOT)):
                        pc = slice(qb * 256 + part * 128, qb * 256 + part * 128 + 128)
                        mm(s, po, po[:, qc], vpad, vpad[:, slot, kv * 2 + pos, :], pm_sw[h], pm_sw[h][:, pc], n == 0, n == 3)
                        n += 1
                n = 0
                for pos in range(2):
                    h = pr * 2 + pos
                    for part in range(2):
                        pc = slice(qb * 256 + part * 128, qb * 256 + part * 128 + 128)
                        mm(s, pd, pd[:, qc], onespad, onespad[:, pos, :], pm_sw[h], pm_sw[h][:, pc], n == 0, n == 3)
                        n += 1
            rd = rden.next()
            V("dve", lambda e, pd=pd, rd=rd, pr=pr: e.tensor_scalar(out=rd[:], in0=pd[:, 0:NT], scalar1=esink[:, pr:pr + 1], scalar2=None, op0=ALU.add),
              [pd, esink], [rd])
            V("dve", lambda e, rd=rd: e.reciprocal(out=rd[:], in_=rd[:]), [rd], [rd])
            V("dve", lambda e, po=po, rd=rd, pr=pr: e.tensor_tensor(out=br[1][:, pr, :], in0=po[:, 0:NT], in1=rd[:], op=ALU.mult), [po, rd], [br[1]])
        if t + 1 < NTILES:
            load_h(t + 1)
        for m in range(8):
            for n in range(3):
                gs = gsb.next()
                proj(13 + n * 8 + m, lambda p, gs=gs: V("act", lambda e: e.activation(out=gs[:], in_=p[:, 0:NT], func=AF.Sigmoid), [p], [gs]))
                w = wbr.next()
                s.dma("pool", w[:], wbr_d[n * 8 + m], w, writes=[w])
                p = pring.next()
                for k in range(4):
                    mm(s, p, p[:, 0:NT], w, w[:, k, :], br[n], br[n][:, k, :], k == 0, k == 3)
                if n == 0:
                    V("dve", lambda e, p=p, m=m, gs=gs: e.tensor_tensor(out=merged[:, m, :], in0=p[:, 0:NT], in1=gs[:], op=ALU.mult),
                      [p, gs], [merged])
                else:
                    g = gtmp.next()
                    V("dve", lambda e, p=p, gs=gs, g=g: e.tensor_tensor(out=g[:], in0=p[:, 0:NT], in1=gs[:], op=ALU.mult),
                      [p, gs], [g])
                    V("dve", lambda e, m=m, g=g: e.tensor_tensor(out=merged[:, m, :], in0=merged[:, m, :], in1=g[:], op=ALU.add),
                      [merged, g], [merged])
        V("act", lambda e: e.activation(out=mergedb[:], in_=merged[:], func=AF.Copy), [merged], [mergedb])
        for m in range(8):
            w = wi.next()
            s.dma("pool", w[:], wo_d[m], w, writes=[w])
            p = pring.next()
            for k in range(8):
                mm(s, p, p[:, 0:NT], w, w[:, k, :], mergedb, mergedb[:, k, :], k == 0, k == 7)
            V("dve", lambda e, p=p, m=m: e.scalar_tensor_tensor(out=ybuf[:, m, :], in0=p[:, 0:NT], scalar=1.0 / ALPHA, in1=h32[:, m, :],
                                                              op0=ALU.mult, op1=ALU.add), [p, h32], [ybuf])
        ffn.ln(ybuf, None, 1)
        stores.append(s.dma("sp", outT[:, :, c0:c0 + NT].rearrange("k p n -> p k n"), ybuf[:], ybuf, reads=[ybuf]))
    s.finish(stores)
    s.emit()
    return nc


def lay_wc(w_in):
    swq = w_in[:, 2056:2568].reshape(1024, 8, 64)
    swq_p = np.concatenate([np.concatenate([swq[:, j], swq[:, 4 + j]], axis=1) for j in range(4)], axis=1)
    wcat = np.concatenate([w_in[:, 1544:2056], swq_p, w_in[:, 2568:2696], w_in[:, 2824:3336], w_in[:, 3336:6408]], axis=1)
    return lay_w_kmc(wcat)


def lay_pkc(w):
    K, C = w.shape
    return np.ascontiguousarray(w.reshape(K // 128, 128, C).transpose(1, 0, 2))


def lay_wbr(wb):
    a = wb.reshape(3, 4, 128, 8, 128).transpose(0, 3, 2, 1, 4)
    return np.ascontiguousarray(a.reshape(24, 128, 4, 128))


def make_masks(halo_valid):
    k = np.arange(128)[:, None]
    q = np.arange(128)[None, :]
    prev = (k > q).astype(np.float32)
    cur = (k <= q).astype(np.float32)
    std = np.concatenate([prev, cur, prev, cur], axis=1)
    first = np.concatenate([prev * halo_valid, cur, prev, cur], axis=1)
    return np.ascontiguousarray(np.stack([first, std], axis=1)).astype(np.float32)


def lay_sink(sinks):
    c = np.zeros((128, 4), np.float32)
    for pr in range(4):
        c[:64, pr] = sinks[2 * pr]
        c[64:, pr] = sinks[2 * pr + 1]
    return c


_PROGS = {}
TOK_PER_CORE = SEQ * BATCH // NCORES
NSEG = SEQ // TOK_PER_CORE


def _prog(name):
    if name not in _PROGS:
        if name == "A":
            _PROGS[name] = build_A(TOK_PER_CORE)
        elif name == "B":
            _PROGS[name] = build_B(SEQ)
        elif name == "F":
            _PROGS[name] = build_A(TOK_PER_CORE, proj=False, ln_idx=2)
        else:
            _PROGS[name] = build_C(TOK_PER_CORE)
    return _PROGS[name]


def _run(name, in_maps):
    res = run_bass_kernel_spmd(_prog(name), in_maps, core_ids=list(range(NCORES)))
    return res.results


def kernel(x, mem, mem_ln_g, mem_ln_b, ln_g, ln_b, ffn1_w_gu, ffn1_w_down, w_in, dn_conv_w,
           dn_a_log, dn_dt_bias, dn_norm_w, swa_sinks, w_mem_kv, w_branch, w_out, ffn2_w_gu, ffn2_w_down):
    f = lambda a: np.asarray(a, dtype=np.float32)
    cur = f(x)
    mem = f(mem)
    cst = make_cst()
    cstB, ccol = make_cstB()
    NT_ = TOK_PER_CORE
    for l in range(DEPTH):
        win = f(w_in[l])
        lnp = lay_lnp(f(ln_g[l]), f(ln_b[l]))
        commonA = dict(wgu=lay_wgu(f(ffn1_w_gu[l])), wdn=lay_w_kmc(f(ffn1_w_down[l])), wqkv=lay_w_kmc(win[:, :1536]),
                       wba=lay_pkc(win[:, 1536:1544]), lnp=lnp, cst=cst)
        insA = []
        for c in range(NCORES):
            b, sg = divmod(c, NSEG)
            xs = cur[b, sg * NT_:(sg + 1) * NT_]
            insA.append(dict(xT=np.ascontiguousarray(xs.T.reshape(8, 128, NT_)), **commonA))
        rA = _run("A", insA)
        convw = f(dn_conv_w[l])
        insB = []
        for c in range(NCORES):
            b, hd = divmod(c, 4)
            qkvp = np.concatenate([rA[b * NSEG + sg]["qkvT"][[hd, 4 + hd, 8 + hd]] for sg in range(NSEG)], axis=2)
            brow = np.concatenate([rA[b * NSEG + sg]["baT"][hd] for sg in range(NSEG)])
            arow = np.concatenate([rA[b * NSEG + sg]["baT"][4 + hd] for sg in range(NSEG)])
            cw = np.stack([convw[:, w * 512 + hd * 128: w * 512 + hd * 128 + 128] for w in range(3)], 0)
            smallp = np.zeros((128, 32), np.float32)
            smallp[:, 0:12] = cw.transpose(2, 0, 1).reshape(128, 12)
            smallp[:, 12] = f(dn_a_log[l])[hd]
            smallp[:, 13] = f(dn_dt_bias[l])[hd]
            smallp[:, 14] = f(dn_norm_w[l])
            smallp[:, 15:17] = ccol
            insB.append(dict(qkvp=np.ascontiguousarray(qkvp), bcol=np.ascontiguousarray(brow.reshape(SEQ // 128, 128).T),
                             acol=np.ascontiguousarray(arow.reshape(SEQ // 128, 128).T),
                             smallp=smallp, cstB=cstB))
        rB = _run("B", insB)
        wkv = f(w_mem_kv[l])
        mg = np.stack([f(mem_ln_g)] * 3)
        mb = np.stack([f(mem_ln_b)] * 3)
        commonC = dict(wc=lay_wc(win), wv=lay_pkc(win[:, 2696:2824]), wmk=lay_w_kmc(wkv[:, :512]), wmv=lay_pkc(wkv[:, 512:]),
                       wbr=lay_wbr(f(w_branch[l])), wo=lay_w_kmc(f(w_out[l])),
                       lnp=lnp, mlnp=np.ascontiguousarray(lay_lnp(mg, mb)[:, 0:1]),
                       cst=cst, sinkc=lay_sink(f(swa_sinks[l])))
        insC = []
        for c in range(NCORES):
            b, sg = divmod(c, NSEG)
            hT = rA[c]["hT"]
            halo = rA[c - 1]["hT"][:, :, -128:] if sg > 0 else np.zeros((8, 128, 128), np.float32)
            odn = np.stack([rB[b * 4 + hd]["onT"][:, sg * NT_:(sg + 1) * NT_] for hd in range(4)], 0)
            insC.append(dict(hT=np.ascontiguousarray(np.concatenate([halo, hT], axis=2)), odn=np.ascontiguousarray(odn),
                             memT=np.ascontiguousarray(mem[b].T.reshape(8, 128, N_MEM)),
                             msk=make_masks(1.0 if sg > 0 else 0.0), **commonC))
        rC = _run("C", insC)
        commonF = dict(wgu=lay_wgu(f(ffn2_w_gu[l])), wdn=lay_w_kmc(f(ffn2_w_down[l])), lnp=lnp, cst=cst)
        rF = _run("F", [dict(xT=rC[c]["outT"], **commonF) for c in range(NCORES)])
        nxt = np.empty_like(cur)
        for c in range(NCORES):
            b, sg = divmod(c, NSEG)
            nxt[b, sg * NT_:(sg + 1) * NT_] = rF[c]["hT"].reshape(D_MODEL, NT_).T
        cur = nxt
    return cur
```

```python
import math
import numpy as np
import concourse.bass as bass
import concourse.mybir as mybir
from concourse.bass_utils import run_bass_kernel_spmd

F32 = mybir.dt.float32
BF16 = mybir.dt.bfloat16
AF = mybir.ActivationFunctionType
ALU = mybir.AluOpType

D_MODEL = 1024
BATCH = 2
SEQ = 16384
DEPTH = 2
N_MEM = 256
D_FF = 2816
D_IN = 6408
NCORES = 8
ALPHA = (2 * DEPTH) ** 0.25
LN_EPS = 1e-5
RMS_EPS = 1e-6
SAME_ENGINE_SYNC = True
PSUM_READ_SERIALIZE = True


class T:
    def __init__(self, h, sem=None):
        self.h = h
        self.w = None
        self.r = {}
        self.sem = sem
        self.semval = 0

    def __getitem__(self, idx):
        return self.h[idx]


class TV:
    def __init__(self, parent, ap):
        self.p = parent
        self.ap = ap

    @property
    def is_psum(self):
        return getattr(self.p, "is_psum", False)

    @property
    def w(self):
        return self.p.w

    @w.setter
    def w(self, v):
        self.p.w = v

    @property
    def r(self):
        return self.p.r

    @r.setter
    def r(self, v):
        self.p.r = v

    def __getitem__(self, idx):
        return self.ap[idx]


class Sched:
    ENG = ("pe", "act", "dve", "pool", "sp")

    def __init__(self, nc):
        self.nc = nc
        self.prog = {e: [] for e in self.ENG}
        self.sem = {e: nc.alloc_semaphore("sem_" + e) for e in ("pe", "act", "dve", "pool")}
        self.cnt = {e: 0 for e in self.sem}
        self.seen = {e: {} for e in self.ENG}
        self.uid = 0
        self.final = []

    def tile(self, shape, dt, name=None, psum=False, dma=False):
        self.uid += 1
        name = f"{name or 't'}_{self.uid}"
        if psum:
            h = self.nc.alloc_psum_tensor(name, list(shape), dt)
        else:
            h = self.nc.alloc_sbuf_tensor(name, list(shape), dt)
        sem = self.nc.alloc_semaphore("ds_" + name) if dma else None
        t = T(h, sem)
        t.is_psum = psum
        return t

    def _dep(self, eng, ev):
        key, sem, val = ev
        if key == eng and (eng == "pe" or not SAME_ENGINE_SYNC):
            return
        if self.seen[eng].get(key, 0) >= val:
            return
        self.seen[eng][key] = val
        self.prog[eng].append(lambda e, sem=sem, val=val: e.wait_ge(sem, val))

    def _deps(self, eng, reads, writes):
        for t in reads:
            if t.w is not None:
                self._dep(eng, t.w)
            if PSUM_READ_SERIALIZE and getattr(t, "is_psum", False):
                for ev in t.r.values():
                    if ev[0] != eng:
                        self._dep(eng, ev)
        for t in writes:
            if t.w is not None:
                self._dep(eng, t.w)
            for ev in t.r.values():
                self._dep(eng, ev)

    def _mark(self, ev, reads, writes):
        for t in reads:
            t.r[ev[0]] = ev
        for t in writes:
            t.w = ev
            t.r = {}

    def op(self, eng, fn, reads=(), writes=()):
        self._deps(eng, reads, writes)
        self.cnt[eng] += 1
        val = self.cnt[eng]
        sem = self.sem[eng]
        self.prog[eng].append(lambda e, fn=fn, sem=sem: fn(e).then_inc(sem, 1))
        self._mark((eng, sem, val), reads, writes)

    def dma(self, q, out, in_, semtile, reads=(), writes=()):
        self._deps(q, reads, writes)
        semtile.semval += 16
        sem, val = semtile.sem, semtile.semval
        self.prog[q].append(lambda e, out=out, in_=in_, sem=sem: e.dma_start(out=out, in_=in_).then_inc(sem, 16))
        ev = (("d", id(semtile)), sem, val)
        self._mark(ev, reads, writes)
        return ev

    def finish(self, evs):
        for ev in evs:
            self._dep("sp", ev)

    def emit(self):
        nc = self.nc
        with nc.Block() as block:
            @block.tensor
            def _(e):
                for f in self.prog["pe"]:
                    f(e)

            @block.scalar
            def _(e):
                for f in self.prog["act"]:
                    f(e)

            @block.vector
            def _(e):
                for f in self.prog["dve"]:
                    f(e)

            @block.gpsimd
            def _(e):
                for f in self.prog["pool"]:
                    f(e)

            @block.sync
            def _(e):
                for f in self.prog["sp"]:
                    f(e)


class Ring:
    def __init__(self, tiles):
        self.tiles = tiles
        self.i = 0

    def next(self):
        t = self.tiles[self.i % len(self.tiles)]
        self.i += 1
        return t


def mm(s, out_t, out_ap, l_t, l_ap, r_t, r_ap, start, stop):
    s.op("pe", lambda e: e.matmul(out_ap, lhsT=l_ap, rhs=r_ap, start=start, stop=stop),
         reads=[l_t, r_t], writes=[out_t])


class FFNStage:
    def __init__(self, s, NT, wgu_d, wdn_d, lnp_t, ln_idx, consts, ps, resident=False, with_ffn=True):
        self.s, self.NT = s, NT
        self.wgu_d, self.wdn_d = wgu_d, wdn_d
        self.lnp, self.ln_idx = lnp_t, ln_idx
        self.c = consts
        self.ps = ps
        self.resident = resident
        if with_ffn:
            if resident:
                self.wgR = [s.tile([128, 8, 256], BF16, "wgR", dma=True) for _ in range(22)]
                self.wdR = [s.tile([128, 22, 128], BF16, "wdR", dma=True) for _ in range(8)]
                for j in range(22):
                    s.dma("pool", self.wgR[j][:], wgu_d[j], self.wgR[j], writes=[self.wgR[j]])
                for m in range(8):
                    s.dma("pool", self.wdR[m][:], wdn_d[m], self.wdR[m], writes=[self.wdR[m]])
            else:
                self.wg = Ring([s.tile([128, 8, 256], BF16, "wg", dma=True) for _ in range(3)])
                self.wd = Ring([s.tile([128, 22, 128], BF16, "wd", dma=True) for _ in range(2)])
            self.act = s.tile([128, 22, NT], BF16, "act")
            self.sg = Ring([s.tile([128, NT], F32, "sg") for _ in range(2)])
        self.ysq = Ring([s.tile([128, NT], F32, "ysq") for _ in range(2)])
        self.mean = s.tile([128, NT], F32, "mean")
        self.tmp = s.tile([128, NT], F32, "lntmp")
        self.rstd = s.tile([128, NT], F32, "rstd")
        self.d = Ring([s.tile([128, NT], F32, "lnd") for _ in range(2)])

    def run(self, x32, xb, hb, prefetch=None):
        s, NT = self.s, self.NT
        ps = self.ps
        for j in range(22):
            if self.resident:
                wg = self.wgR[j]
            else:
                wg = self.wg.next()
                s.dma("pool", wg[:], self.wgu_d[j], wg, writes=[wg])
            pg, pu = ps["g"].next(), ps["u"].next()
            for k in range(8):
                mm(s, pg, pg[:, 0:NT], wg, wg[:, k, 0:128], xb, xb[:, k, :], k == 0, k == 7)
            for k in range(8):
                mm(s, pu, pu[:, 0:NT], wg, wg[:, k, 128:256], xb, xb[:, k, :], k == 0, k == 7)
            sg = self.sg.next()
            s.op("act", lambda e, sg=sg, pg=pg: e.activation(out=sg[:], in_=pg[:, 0:NT], func=AF.Silu),
                 reads=[pg], writes=[sg])
            s.op("dve", lambda e, sg=sg, pu=pu, j=j: e.tensor_tensor(out=self.act[:, j, :], in0=sg[:], in1=pu[:, 0:NT], op=ALU.mult),
                 reads=[sg, pu], writes=[self.act])
        for m in range(8):
            if self.resident:
                wd = self.wdR[m]
            else:
                wd = self.wd.next()
                s.dma("pool", wd[:], self.wdn_d[m], wd, writes=[wd])
            py = ps["y"].next()
            for k in range(22):
                mm(s, py, py[:, 0:NT], wd, wd[:, k, :], self.act, self.act[:, k, :], k == 0, k == 21)
            s.op("dve", lambda e, py=py, m=m: e.scalar_tensor_tensor(
                out=x32[:, m, :], in0=py[:, 0:NT], scalar=0.5 / ALPHA, in1=x32[:, m, :], op0=ALU.mult, op1=ALU.add),
                reads=[py, x32], writes=[x32])
        if prefetch is not None:
            prefetch()
        self.ln(x32, hb, self.ln_idx)

    def ln(self, y, hb, li):
        s, NT, ps = self.s, self.NT, self.ps
        pm, pe_ = ps["m"], ps["e"]
        for m in range(8):
            ysq = self.ysq.next()
            s.op("act", lambda e, ysq=ysq, m=m: e.activation(out=ysq[:], in_=y[:, m, :], func=AF.Square),
                 reads=[y], writes=[ysq])
            mm(s, pm, pm[:, 0:NT], self.c["onesD"], self.c["onesD"][:], y, y[:, m, :], m == 0, m == 7)
            mm(s, pe_, pe_[:, 0:NT], self.c["onesD"], self.c["onesD"][:], ysq, ysq[:], m == 0, m == 7)
        s.op("act", lambda e: e.activation(out=self.mean[:], in_=pm[:, 0:NT], func=AF.Copy), reads=[pm], writes=[self.mean])
        s.op("dve", lambda e: e.tensor_tensor(out=self.tmp[:], in0=self.mean[:], in1=self.mean[:], op=ALU.mult),
             reads=[self.mean], writes=[self.tmp])
        s.op("dve", lambda e: e.tensor_tensor(out=self.tmp[:], in0=pe_[:, 0:NT], in1=self.tmp[:], op=ALU.subtract),
             reads=[pe_, self.tmp], writes=[self.tmp])
        s.op("act", lambda e: e.activation(out=self.tmp[:], in_=self.tmp[:], func=AF.Ln, bias=self.c["epsln"][:, 0:1]),
             reads=[self.tmp, self.c["epsln"]], writes=[self.tmp])
        s.op("act", lambda e: e.activation(out=self.rstd[:], in_=self.tmp[:], func=AF.Exp, scale=-0.5), reads=[self.tmp], writes=[self.rstd])
        for m in range(8):
            d = self.d.next()
            s.op("dve", lambda e, d=d, m=m: e.tensor_tensor(out=d[:], in0=y[:, m, :], in1=self.mean[:], op=ALU.subtract),
                 reads=[y, self.mean], writes=[d])
            s.op("dve", lambda e, d=d: e.tensor_tensor(out=d[:], in0=d[:], in1=self.rstd[:], op=ALU.mult),
                 reads=[d, self.rstd], writes=[d])
            s.op("act", lambda e, d=d, m=m: e.activation(
                out=y[:, m, :], in_=d[:], func=AF.Identity,
                scale=self.lnp[:, li, 0, m:m + 1], bias=self.lnp[:, li, 1, m:m + 1]),
                reads=[d, self.lnp], writes=[y])
        if hb is not None:
            s.op("pool", lambda e: e.tensor_copy(out=hb[:], in_=y[:]), reads=[y], writes=[hb])


def make_psum(s):
    banks = [s.tile([128, 512], F32, f"ps{i}", psum=True) for i in range(8)]
    return {"g": Ring(banks[0:2]), "u": Ring(banks[2:4]), "y": Ring(banks[4:6]), "m": banks[6], "e": banks[7],
            "all": banks}


def load_consts(s, nc, cst_d):
    c = {}
    cst = s.tile([128, 3, 128], F32, "cst", dma=True)
    s.dma("sp", cst[:], cst_d, cst, writes=[cst])
    c["cst"] = cst
    onesD = s.tile([128, 128], F32, "onesD")
    s.op("dve", lambda e: e.tensor_copy(out=onesD[:], in_=cst[:, 0, :]), reads=[cst], writes=[onesD])
    c["onesD"] = onesD
    eps = s.tile([128, 1], F32, "epsln")
    s.op("dve", lambda e: e.memset(eps[:], LN_EPS / (ALPHA * ALPHA)), writes=[eps])
    c["epsln"] = eps
    return c


def build_A(NTOK, NT=256, proj=True, ln_idx=0):
    nc = bass.Bass("TRN2", target_bir_lowering=False)
    xT = nc.dram_tensor("xT", [8, 128, NTOK], F32, kind="ExternalInput").ap()
    wgu = nc.dram_tensor("wgu", [22, 128, 8, 256], F32, kind="ExternalInput").ap()
    wdn = nc.dram_tensor("wdn", [8, 128, 22, 128], F32, kind="ExternalInput").ap()
    if proj:
        wqkv = nc.dram_tensor("wqkv", [12, 128, 8, 128], F32, kind="ExternalInput").ap()
        wba = nc.dram_tensor("wba", [128, 8, 8], F32, kind="ExternalInput").ap()
    lnp_d = nc.dram_tensor("lnp", [128, 3, 2, 8], F32, kind="ExternalInput").ap()
    cst_d = nc.dram_tensor("cst", [128, 3, 128], F32, kind="ExternalInput").ap()
    hT = nc.dram_tensor("hT", [8, 128, NTOK], F32, kind="ExternalOutput").ap()
    if proj:
        qkvT = nc.dram_tensor("qkvT", [12, 128, NTOK], F32, kind="ExternalOutput").ap()
        baT = nc.dram_tensor("baT", [8, NTOK], F32, kind="ExternalOutput").ap()

    s = Sched(nc)
    c = load_consts(s, nc, cst_d)
    lnp = s.tile([128, 3, 2, 8], F32, "lnp", dma=True)
    s.dma("sp", lnp[:], lnp_d, lnp, writes=[lnp])
    ps = make_psum(s)
    ffn = FFNStage(s, NT, wgu, wdn, lnp, ln_idx, c, ps, resident=True)
    x32r = [s.tile([128, 8, NT], F32, "x32", dma=True) for _ in range(2)]
    xbr = [s.tile([128, 8, NT], BF16, "xb") for _ in range(2)]
    if proj:
        wb = s.tile([128, 8, 8], BF16, "wba", dma=True)
        s.dma("pool", wb[:], wba, wb, writes=[wb])
        wqR = [s.tile([128, 8, 128], BF16, "wqR", dma=True) for _ in range(12)]
        for m in range(12):
            s.dma("pool", wqR[m][:], wqkv[m], wqR[m], writes=[wqR[m]])
        qo = Ring([s.tile([128, NT], F32, "qo", dma=True) for _ in range(4)])
        bo = s.tile([8, NT], F32, "bo", dma=True)
    stores = []
    NTILES = NTOK // NT

    def load(t):
        x32, xb = x32r[t % 2], xbr[t % 2]
        cols = slice(t * NT, (t + 1) * NT)
        s.dma("sp", x32[:], xT[:, :, cols].rearrange("k p n -> p k n"), x32, writes=[x32])
        s.op("act", lambda e: e.activation(out=xb[:], in_=x32[:], func=AF.Copy), reads=[x32], writes=[xb])

    load(0)
    for t in range(NTILES):
        cols = slice(t * NT, (t + 1) * NT)
        x32, xb = x32r[t % 2], xbr[t % 2]
        hb = xb if proj else None
        ffn.run(x32, xb, hb, prefetch=(lambda t=t: load(t + 1)) if t + 1 < NTILES else None)
        stores.append(s.dma("sp", hT[:, :, cols].rearrange("k p n -> p k n"), x32[:], x32, reads=[x32]))
        if not proj:
            continue
        for m in range(12):
            w = wqR[m]
            pq = ps["g"].next()
            for k in range(8):
                mm(s, pq, pq[:, 0:NT], w, w[:, k, :], hb, hb[:, k, :], k == 0, k == 7)
            q = qo.next()
            s.op("act", lambda e, q=q, pq=pq: e.activation(out=q[:], in_=pq[:, 0:NT], func=AF.Copy), reads=[pq], writes=[q])
            stores.append(s.dma("sp", qkvT[m][:, cols], q[:], q, reads=[q]))
        pb = ps["u"].next()
        for k in range(8):
            mm(s, pb, pb[0:8, 0:NT], wb, wb[:, k, :], hb, hb[:, k, :], k == 0, k == 7)
        s.op("act", lambda e, pb=pb: e.activation(out=bo[:], in_=pb[0:8, 0:NT], func=AF.Copy), reads=[pb], writes=[bo])
        stores.append(s.dma("sp", baT[:, cols], bo[:], bo, reads=[bo]))
    s.finish(stores)
    s.emit()
    return nc


def lay_w_kmc(w, ncols_chunk=128):
    K, M = w.shape
    return np.ascontiguousarray(w.reshape(K // 128, 128, M // ncols_chunk, ncols_chunk).transpose(2, 1, 0, 3))


def lay_wgu(w):
    g = lay_w_kmc(w[:, :D_FF])
    u = lay_w_kmc(w[:, D_FF:])
    return np.ascontiguousarray(np.concatenate([g, u], axis=3))


def lay_lnp(g, b):
    a = np.stack([g.reshape(3, 8, 128), b.reshape(3, 8, 128)], axis=1)
    return np.ascontiguousarray(a.transpose(3, 0, 1, 2))


def make_cst():
    c = np.zeros((128, 3, 128), np.float32)
    c[:, 0, :] = 1.0 / D_MODEL
    c[:, 1, :] = np.eye(128, dtype=np.float32)
    c[:, 2, :] = 1.0
    return c


NEG = -1e30
B_DEBUG_STAGE = None
CONV_ENG = "dve"
AUX_ENG = "pool"
B_VAR = 0


def make_cstB():
    idx = np.arange(128)
    same = (idx[:, None] // 64) == (idx[None, :] // 64)
    c = np.zeros((128, 7, 128), np.float32)
    c[:, 0, :] = np.eye(128)
    c[:, 1, :] = 1.0
    c[:, 2, :] = (same & (idx[:, None] <= idx[None, :]))
    c[:, 3, :] = np.where(same & (idx[:, None] > idx[None, :]), 0.0, NEG)
    c[:, 4, :] = np.where(same & (idx[None, :] >= idx[:, None]), 0.0, NEG)
    c[63, 5, :] = 1.0
    c[127, 6, :] = 1.0
    col = np.zeros((128, 2), np.float32)
    col[64:, 0] = NEG
    col[:64, 1] = NEG
    return c, col


def build_B(S_LEN, NT=512):
    nc = bass.Bass("TRN2", target_bir_lowering=False)
    NBLK = S_LEN // 128
    qkvp = nc.dram_tensor("qkvp", [3, 128, S_LEN], F32, kind="ExternalInput").ap()
    bcol_d = nc.dram_tensor("bcol", [128, NBLK], F32, kind="ExternalInput").ap()
    acol_d = nc.dram_tensor("acol", [128, NBLK], F32, kind="ExternalInput").ap()
    smallp_d = nc.dram_tensor("smallp", [128, 32], F32, kind="ExternalInput").ap()
    cst_d = nc.dram_tensor("cstB", [128, 7, 128], F32, kind="ExternalInput").ap()
    onT = nc.dram_tensor("onT", [128, S_LEN], F32, kind="ExternalOutput").ap()

    s = Sched(nc)
    cst = s.tile([128, 7, 128], F32, "cstB", dma=True)
    s.dma("sp", cst[:], cst_d, cst, writes=[cst])
    smallp = s.tile([128, 32], F32, "smallp", dma=True)
    s.dma("sp", smallp[:], smallp_d, smallp, writes=[smallp])
    ccol = TV(smallp, smallp[:, 15:17])
    bcol = s.tile([128, NBLK], F32, "bcol", dma=True)
    s.dma("sp", bcol[:], bcol_d, bcol, writes=[bcol])
    acol = s.tile([128, NBLK], F32, "acol", dma=True)
    s.dma("sp", acol[:], acol_d, acol, writes=[acol])
    convw = TV(smallp, smallp[:, 0:12].rearrange("p (w j) -> p w j", j=4))
    scal = TV(smallp, smallp[:, 12:15])
    I_, ONES, TRI, MBS, MBT, SELA, SELB = [cst[:, i, :] for i in range(7)]

    banks = [s.tile([128, 512], F32, f"psB{i}", psum=True) for i in range(8)]
    big = Ring(banks[0:2])
    small = Ring([TV(bk, bk[:, q * 128:(q + 1) * 128]) for q in range(4) for bk in banks[3:8]])
    po_r = Ring([TV(banks[2], banks[2][:, q * 128:(q + 1) * 128]) for q in range(4)])

    def V(eng, fn, reads, writes):
        s.op(eng, fn, reads=reads, writes=writes)

    def mmf(out_t, out_ap, l_t, l_ap, r_t, r_ap, start=True, stop=True):
        mm(s, out_t, out_ap, l_t, l_ap, r_t, r_ap, start, stop)

    def col_tile(name, n=NBLK):
        return s.tile([128, n], F32, name)

    one_c = s.tile([128, 1], F32, "one_c")
    V("dve", lambda e: e.memset(one_c[:], 1.0), [], [one_c])
    eps_c = s.tile([128, 1], F32, "eps_c")
    V("dve", lambda e: e.memset(eps_c[:], RMS_EPS), [], [eps_c])
    ones128 = s.tile([128, 128], F32, "ones128")
    V("dve", lambda e: e.tensor_scalar(out=ones128[:], in0=ONES, scalar1=1.0 / 128, scalar2=None, op0=ALU.mult), [cst], [ones128])
    beta = col_tile("beta")
    V("act", lambda e: e.activation(out=beta[:], in_=bcol[:], func=AF.Sigmoid), [bcol], [beta])
    xg = col_tile("xg")
    V("dve", lambda e: e.tensor_scalar(out=xg[:], in0=acol[:], scalar1=scal[:, 1:2], scalar2=None, op0=ALU.add), [acol, scal], [xg])
    ax = col_tile("ax")
    V("dve", lambda e: e.tensor_scalar(out=ax[:], in0=xg[:], scalar1=-1.0, scalar2=None, op0=ALU.mult), [xg], [ax])
    V("dve", lambda e: e.tensor_tensor(out=ax[:], in0=ax[:], in1=xg[:], op=ALU.max), [ax, xg], [ax])
    V("act", lambda e: e.activation(out=ax[:], in_=ax[:], func=AF.Exp, scale=-1.0), [ax], [ax])
    V("act", lambda e: e.activation(out=ax[:], in_=ax[:], func=AF.Ln, bias=one_c[:, 0:1]), [ax, one_c], [ax])
    V("dve", lambda e: e.tensor_scalar(out=xg[:], in0=xg[:], scalar1=0.0, scalar2=None, op0=ALU.max), [xg], [xg])
    V("dve", lambda e: e.tensor_tensor(out=xg[:], in0=xg[:], in1=ax[:], op=ALU.add), [xg, ax], [xg])
    nea = s.tile([128, 1], F32, "nea")
    V("act", lambda e: e.activation(out=nea[:], in_=scal[:, 0:1], func=AF.Exp), [scal], [nea])
    V("dve", lambda e: e.tensor_scalar(out=nea[:], in0=nea[:], scalar1=-1.0, scalar2=None, op0=ALU.mult), [nea], [nea])
    gc = col_tile("gc")
    V("dve", lambda e: e.tensor_scalar(out=gc[:], in0=xg[:], scalar1=nea[:, 0:1], scalar2=None, op0=ALU.mult), [xg, nea], [gc])
    gcum, ngcum, bexpg, colA, colB, eglA, eglB = [col_tile(n) for n in ("gcum", "ngcum", "bexpg", "colA", "colB", "eglA", "eglB")]
    for c0 in range(0, NBLK, 512):
        c1 = min(NBLK, c0 + 512)
        w = c1 - c0
        pb = big.next()
        mmf(pb, pb[:, 0:w], cst, TRI, gc, gc[:, c0:c1])
        V("act", lambda e, pb=pb, c0=c0, c1=c1, w=w: e.activation(out=gcum[:, c0:c1], in_=pb[:, 0:w], func=AF.Copy), [pb], [gcum])
    V("dve", lambda e: e.tensor_scalar(out=ngcum[:], in0=gcum[:], scalar1=-1.0, scalar2=None, op0=ALU.mult), [gcum], [ngcum])
    V("act", lambda e: e.activation(out=bexpg[:], in_=gcum[:], func=AF.Exp), [gcum], [bexpg])
    V("dve", lambda e: e.tensor_tensor(out=bexpg[:], in0=bexpg[:], in1=beta[:], op=ALU.mult), [bexpg, beta], [bexpg])
    for (SEL, col, egl, mi) in ((SELA, colA, eglA, 0), (SELB, colB, eglB, 1)):
        for c0 in range(0, NBLK, 512):
            c1 = min(NBLK, c0 + 512)
            w = c1 - c0
            pb = big.next()
            mmf(pb, pb[:, 0:w], cst, SEL, gcum, gcum[:, c0:c1])
            V("act", lambda e, pb=pb, egl=egl, c0=c0, c1=c1, w=w: e.activation(out=egl[:, c0:c1], in_=pb[:, 0:w], func=AF.Exp), [pb], [egl])
            V("dve", lambda e, pb=pb, col=col, c0=c0, c1=c1, w=w: e.tensor_tensor(out=col[:, c0:c1], in0=pb[:, 0:w], in1=gcum[:, c0:c1], op=ALU.subtract),
              [pb, gcum], [col])
        V("act", lambda e, col=col, mi=mi: e.activation(out=col[:], in_=col[:], func=AF.Exp, bias=ccol[:, mi:mi + 1]), [col, ccol], [col])

    S_r = [s.tile([128, 128], F32, f"S{i}") for i in range(2)]
    V("dve", lambda e: e.memset(S_r[0][:], 0.0), [], [S_r[0]])
    state = {"S": 0}
    NTILE = S_LEN // NT
    NB = NT // 128

    def alloc_set():
        d = {}
        d["pre"] = s.tile([128, 3, NT + 8], F32, "pre", dma=True)
        d["cv"] = [s.tile([128, NT], F32, f"cv{i}") for i in range(3)]
        d["sq"] = s.tile([128, NT], F32, "sq")
        d["rin"] = s.tile([128, NT], F32, "rin")
        d["qT"] = s.tile([128, NT], F32, "qT")
        d["kT"] = s.tile([128, NT], F32, "kT")
        d["kT2"] = s.tile([128, NT], F32, "kT2")
        for nm in ("dGn", "dGp", "ES", "E2", "EG", "A", "Bm", "N", "A2", "B2", "RHSv", "RHSw", "ktA", "ktB", "u", "wT", "qkT", "qdT"):
            d[nm] = [s.tile([128, 128], F32, f"{nm}{i}") for i in range(NB)]
        d["A2b"] = [s.tile([128, 128], F32, f"A2b{i}") for i in range(NB)]
        d["B2b"] = [s.tile([128, 128], F32, f"B2b{i}") for i in range(NB)]
        return d

    sets = [alloc_set(), alloc_set()]
    vn_r = Ring([s.tile([128, 128], F32, f"vn{i}") for i in range(2)])
    oT_r = Ring([s.tile([128, NT], F32, f"oT{i}", dma=True) for i in range(2)])
    osq = s.tile([128, NT], F32, "osq")
    orin = s.tile([128, NT], F32, "orin")
    stores = []

    def prep(t, d):
        c0 = t * NT
        pre = d["pre"]
        if t == 0:
            V("dve", lambda e: e.memset(pre[:, :, 0:8], 0.0), [], [pre])
            s.dma("sp", pre[:, :, 8:NT + 8], qkvp[:, :, 0:NT].rearrange("w p n -> p w n"), pre, writes=[pre])
        else:
            s.dma("sp", pre[:], qkvp[:, :, c0 - 8:c0 + NT].rearrange("w p n -> p w n"), pre, writes=[pre])
        for w in range(3):
            cv = d["cv"][w]
            V("dve", lambda e, cv=cv, w=w: e.tensor_scalar(out=cv[:], in0=pre[:, w, 5:5 + NT], scalar1=convw[:, w, 0:1], scalar2=None, op0=ALU.mult),
              [pre, convw], [cv])
            for j in range(1, 4):
                V("dve", lambda e, cv=cv, w=w, j=j: e.scalar_tensor_tensor(
                    out=cv[:], in0=pre[:, w, 5 + j:5 + j + NT], scalar=convw[:, w, j:j + 1], in1=cv[:], op0=ALU.mult, op1=ALU.add),
                  [pre, convw, cv], [cv])
            V("act", lambda e, cv=cv: e.activation(out=cv[:], in_=cv[:], func=AF.Silu), [cv], [cv])
        yield
        for w, dst, sc in ((0, d["qT"], 128 ** -0.5), (1, d["kT"], 1.0)):
            cv = d["cv"][w]
            V("act", lambda e, cv=cv: e.activation(out=d["sq"][:], in_=cv[:], func=AF.Square), [cv], [d["sq"]])
            pb = big.next()
            mmf(pb, pb[:, 0:NT], cst, ONES, d["sq"], d["sq"][:])
            V("act", lambda e, pb=pb: e.activation(out=d["rin"][:], in_=pb[:, 0:NT], func=AF.Ln, bias=eps_c[:, 0:1]), [pb, eps_c], [d["rin"]])
            V("act", lambda e: e.activation(out=d["rin"][:], in_=d["rin"][:], func=AF.Exp, scale=-0.5), [d["rin"]], [d["rin"]])
            V("dve", lambda e, cv=cv, dst=dst, sc=sc: e.scalar_tensor_tensor(
                out=dst[:], in0=cv[:], scalar=sc, in1=d["rin"][:], op0=ALU.mult, op1=ALU.mult), [cv, d["rin"]], [dst])
        vT = d["cv"][2]
        qT, kT = d["qT"], d["kT"]
        kT2 = d["kT2"]
        V("act", lambda e: e.activation(out=kT2[:], in_=kT[:], func=AF.Copy), [kT], [kT2])
        yield
        blks = range(NB)

        def bs(i):
            return slice(i * 128, (i + 1) * 128)
        for i in blks:
            gb = t * NB + i
            V("dve", lambda e, i=i, gb=gb: e.tensor_scalar(out=d["dGn"][i][:], in0=I_, scalar1=ngcum[:, gb:gb + 1], scalar2=None, op0=ALU.mult),
              [cst, ngcum], [d["dGn"][i]])
            V("dve", lambda e, i=i, gb=gb: e.tensor_scalar(out=d["dGp"][i][:], in0=I_, scalar1=gcum[:, gb:gb + 1], scalar2=None, op0=ALU.mult),
              [cst, gcum], [d["dGp"][i]])
        for i in blks:
            gb = t * NB + i
            p1 = small.next()
            mmf(p1, p1[:], cst, ONES, d["dGn"][i], d["dGn"][i][:], True, False)
            mmf(p1, p1[:], cst, I_, cst, MBS, False, True)
            V("act", lambda e, i=i, gb=gb, p1=p1: e.activation(out=d["ES"][i][:], in_=p1[:], func=AF.Exp, bias=gcum[:, gb:gb + 1]),
              [p1, gcum], [d["ES"][i]])
            p2 = small.next()
            mmf(p2, p2[:], cst, ONES, d["dGp"][i], d["dGp"][i][:], True, False)
            mmf(p2, p2[:], cst, I_, cst, MBT, False, True)
            V("act", lambda e, i=i, gb=gb, p2=p2: e.activation(out=d["E2"][i][:], in_=p2[:], func=AF.Exp, bias=ngcum[:, gb:gb + 1]),
              [p2, ngcum], [d["E2"][i]])
            p3 = small.next()
            mmf(p3, p3[:], cst, ONES, d["dGp"][i], d["dGp"][i][:])
            V("act", lambda e, i=i, p3=p3: e.activation(out=d["EG"][i][:], in_=p3[:], func=AF.Exp), [p3], [d["EG"][i]])
        yield
        for i in blks:
            gb = t * NB + i
            pk = small.next()
            mmf(pk, pk[:], kT, kT[:, bs(i)], kT2, kT2[:, bs(i)])
            V("act", lambda e, i=i, gb=gb, pk=pk: e.activation(out=d["A"][i][:], in_=pk[:], func=AF.Identity, scale=beta[:, gb:gb + 1]),
              [pk, beta], [d["A"][i]])
            V("dve", lambda e, i=i: e.tensor_tensor(out=d["A"][i][:], in0=d["A"][i][:], in1=d["ES"][i][:], op=ALU.mult),
              [d["A"][i], d["ES"][i]], [d["A"][i]])
        yield
        for i in blks:
            pt = small.next()
            mmf(pt, pt[:], d["A"][i], d["A"][i][:], cst, I_)
            V("act", lambda e, i=i, pt=pt: e.activation(out=d["Bm"][i][:], in_=pt[:], func=AF.Copy), [pt], [d["Bm"][i]])
            if B_VAR != 1:
                V("dve", lambda e, i=i, pt=pt: e.scalar_tensor_tensor(
                    out=d["N"][i][:], in0=pt[:], scalar=-1.0, in1=I_, op0=ALU.mult, op1=ALU.add), [pt, cst], [d["N"][i]])
        yield
        curA = [d["A"][i] for i in blks]
        curB = [d["Bm"][i] for i in blks]
        for lvl in range(1, 6):
            nA = d["A2"] if lvl % 2 == 1 else d["A2b"]
            nB = d["B2"] if lvl % 2 == 1 else d["B2b"]
            for i in blks:
                pa = small.next()
                mmf(pa, pa[:], curB[i], curB[i][:], curA[i], curA[i][:])
                V("act", lambda e, i=i, pa=pa, nA=nA: e.activation(out=nA[i][:], in_=pa[:], func=AF.Copy), [pa], [nA[i]])
                if lvl < 5:
                    pbb = small.next()
                    mmf(pbb, pbb[:], curA[i], curA[i][:], curB[i], curB[i][:])
                    V("dve", lambda e, i=i, pbb=pbb, nB=nB: e.tensor_copy(out=nB[i][:], in_=pbb[:]), [pbb], [nB[i]])
            for i in blks:
                pn = small.next()
                mmf(pn, pn[:], nA[i], nA[i][:], d["N"][i], d["N"][i][:])
                V("dve", lambda e, i=i, pn=pn: e.tensor_tensor(out=d["N"][i][:], in0=pn[:], in1=d["N"][i][:], op=ALU.add),
                  [pn, d["N"][i]], [d["N"][i]])
            curA = [nA[i] for i in blks]
            curB = [nB[i] for i in blks]
            yield
        for i in blks:
            gb = t * NB + i
            pk = small.next()
            mmf(pk, pk[:], kT, kT[:, bs(i)], cst, I_)
            V("dve", lambda e, i=i, gb=gb, pk=pk: e.tensor_scalar(out=d["RHSw"][i][:], in0=pk[:], scalar1=bexpg[:, gb:gb + 1], scalar2=None, op0=ALU.mult),
              [pk, bexpg], [d["RHSw"][i]])
            V("dve", lambda e, i=i, gb=gb, pk=pk: e.tensor_scalar(out=d["ktA"][i][:], in0=pk[:], scalar1=colA[:, gb:gb + 1], scalar2=None, op0=ALU.mult),
              [pk, colA], [d["ktA"][i]])
            V("dve", lambda e, i=i, gb=gb, pk=pk: e.tensor_scalar(out=d["ktB"][i][:], in0=pk[:], scalar1=colB[:, gb:gb + 1], scalar2=None, op0=ALU.mult),
              [pk, colB], [d["ktB"][i]])
            pv = small.next()
            mmf(pv, pv[:], vT, vT[:, bs(i)], cst, I_)
            V("dve", lambda e, i=i, gb=gb, pv=pv: e.tensor_scalar(out=d["RHSv"][i][:], in0=pv[:], scalar1=beta[:, gb:gb + 1], scalar2=None, op0=ALU.mult),
              [pv, beta], [d["RHSv"][i]])
        yield
        for i in blks:
            pu = small.next()
            mmf(pu, pu[:], d["N"][i], d["N"][i][:], d["RHSv"][i], d["RHSv"][i][:])
            V("act", lambda e, i=i, pu=pu: e.activation(out=d["u"][i][:], in_=pu[:], func=AF.Copy), [pu], [d["u"][i]])
            pw = small.next()
            mmf(pw, pw[:], d["RHSw"][i], d["RHSw"][i][:], d["N"][i], d["N"][i][:])
            V("act", lambda e, i=i, pw=pw: e.activation(out=d["wT"][i][:], in_=pw[:], func=AF.Copy), [pw], [d["wT"][i]])
            pq = small.next()
            mmf(pq, pq[:], kT, kT[:, bs(i)], qT, qT[:, bs(i)])
            V("dve", lambda e, i=i, pq=pq: e.tensor_tensor(out=d["qkT"][i][:], in0=pq[:], in1=d["E2"][i][:], op=ALU.mult),
              [pq, d["E2"][i]], [d["qkT"][i]])
            V("dve", lambda e, i=i: e.tensor_tensor(out=d["qdT"][i][:], in0=qT[:, bs(i)], in1=d["EG"][i][:], op=ALU.mult),
              [qT, d["EG"][i]], [d["qdT"][i]])
        yield

    def recur(t, d):
        oT = oT_r.next()
        for i in range(NB):
            gb = t * NB + i
            po = po_r.next()
            for half, kt, egl in ((0, d["ktA"][i], eglA), (1, d["ktB"][i], eglB)):
                S = S_r[state["S"]]
                S2 = S_r[1 - state["S"]]
                hs = slice(half * 64, half * 64 + 64)
                p1 = small.next()
                mmf(p1, p1[:], d["wT"][i], d["wT"][i][:], S, S[:])
                vn = vn_r.next()
                V("dve", lambda e, i=i, p1=p1, vn=vn: e.tensor_tensor(out=vn[:], in0=d["u"][i][:], in1=p1[:], op=ALU.subtract),
                  [d["u"][i], p1], [vn])
                mmf(po, po[:, hs], S, S[:], d["qdT"][i], d["qdT"][i][:, hs], True, False)
                mmf(po, po[:, hs], vn, vn[:], d["qkT"][i], d["qkT"][i][:, hs], False, True)
                p2 = small.next()
                mmf(p2, p2[:], kt, kt[:], vn, vn[:])
                V("dve", lambda e, S=S, S2=S2, p2=p2, egl=egl, gb=gb: e.scalar_tensor_tensor(
                    out=S2[:], in0=S[:], scalar=egl[:, gb:gb + 1], in1=p2[:], op0=ALU.mult, op1=ALU.add),
                  [S, egl, p2], [S2])
                state["S"] = 1 - state["S"]
                yield
            V("act", lambda e, i=i, po=po, oT=oT: e.activation(out=oT[:, i * 128:(i + 1) * 128], in_=po[:], func=AF.Copy), [po], [oT])
        V("act", lambda e: e.activation(out=osq[:], in_=oT[:], func=AF.Square), [oT], [osq])
        pb = big.next()
        mmf(pb, pb[:, 0:NT], ones128, ones128[:], osq, osq[:])
        V("act", lambda e, pb=pb: e.activation(out=orin[:], in_=pb[:, 0:NT], func=AF.Ln, bias=eps_c[:, 0:1]), [pb, eps_c], [orin])
        V("act", lambda e: e.activation(out=orin[:], in_=orin[:], func=AF.Exp, scale=-0.5), [orin], [orin])
        V("dve", lambda e, oT=oT: e.scalar_tensor_tensor(out=oT[:], in0=oT[:], scalar=scal[:, 2:3], in1=orin[:], op0=ALU.mult, op1=ALU.mult),
          [oT, scal, orin], [oT])
        stores.append(s.dma("sp", onT[:, t * NT:(t + 1) * NT], oT[:], oT, reads=[oT]))
        yield

    def drain(g):
        for _ in g:
            pass

    if B_DEBUG_STAGE is not None:
        g = prep(0, sets[0])
        for _ in range(B_DEBUG_STAGE):
            next(g, None)
        oT = oT_r.next()
        V("dve", lambda e: e.memset(oT[:], 1.0), [], [oT])
        stores.append(s.dma("sp", onT[:, 0:NT], oT[:], oT, reads=[oT]))
        s.finish(stores)
        s.emit()
        return nc
    drain(prep(0, sets[0]))
    for t in range(NTILE):
        r = recur(t, sets[t % 2])
        p = prep(t + 1, sets[(t + 1) % 2]) if t + 1 < NTILE else iter(())
        rd = pd = False
        while not (rd and pd):
            if not pd:
                try:
                    next(p)
                except StopIteration:
                    pd = True
            if not rd:
                try:
                    next(r)
                except StopIteration:
                    rd = True
    s.finish(stores)
    s.emit()
    return nc


NCH_C = 37


def build_C(NTOK, NT=256):
    nc = bass.Bass("TRN2", target_bir_lowering=False)
    NQB = NT // 128
    hT = nc.dram_tensor("hT", [8, 128, 128 + NTOK], F32, kind="ExternalInput").ap()
    odn_d = nc.dram_tensor("odn", [4, 128, NTOK], F32, kind="ExternalInput").ap()
    memT = nc.dram_tensor("memT", [8, 128, N_MEM], F32, kind="ExternalInput").ap()
    wc_d = nc.dram_tensor("wc", [NCH_C, 128, 8, 128], F32, kind="ExternalInput").ap()
    wv_d = nc.dram_tensor("wv", [128, 8, 128], F32, kind="ExternalInput").ap()
    wmk_d = nc.dram_tensor("wmk", [4, 128, 8, 128], F32, kind="ExternalInput").ap()
    wmv_d = nc.dram_tensor("wmv", [128, 8, 512], F32, kind="ExternalInput").ap()
    wbr_d = nc.dram_tensor("wbr", [24, 128, 4, 128], F32, kind="ExternalInput").ap()
    wo_d = nc.dram_tensor("wo", [8, 128, 8, 128], F32, kind="ExternalInput").ap()
    lnp_d = nc.dram_tensor("lnp", [128, 3, 2, 8], F32, kind="ExternalInput").ap()
    mlnp_d = nc.dram_tensor("mlnp", [128, 1, 2, 8], F32, kind="ExternalInput").ap()
    cst_d = nc.dram_tensor("cst", [128, 3, 128], F32, kind="ExternalInput").ap()
    msk_d = nc.dram_tensor("msk", [128, 2, 512], F32, kind="ExternalInput").ap()
    sink_d = nc.dram_tensor("sinkc", [128, 4], F32, kind="ExternalInput").ap()
    outT = nc.dram_tensor("outT", [8, 128, NTOK], F32, kind="ExternalOutput").ap()

    s = Sched(nc)
    c = load_consts(s, nc, cst_d)
    cst = c["cst"]
    lnp = s.tile([128, 3, 2, 8], F32, "lnp", dma=True)
    s.dma("sp", lnp[:], lnp_d, lnp, writes=[lnp])
    mlnp = s.tile([128, 1, 2, 8], F32, "mlnp", dma=True)
    s.dma("sp", mlnp[:], mlnp_d, mlnp, writes=[mlnp])
    msk = s.tile([128, 2, 512], F32, "msk", dma=True)
    s.dma("sp", msk[:], msk_d, msk, writes=[msk])
    sinkc = s.tile([128, 4], F32, "sinkc", dma=True)
    s.dma("sp", sinkc[:], sink_d, sinkc, writes=[sinkc])
    esink = s.tile([128, 4], F32, "esink")
    s.op("act", lambda e: e.activation(out=esink[:], in_=sinkc[:], func=AF.Exp), reads=[sinkc], writes=[esink])
    onesb = s.tile([128, 128], BF16, "onesb")
    s.op("dve", lambda e: e.tensor_copy(out=onesb[:], in_=cst[:, 2, :]), reads=[cst], writes=[onesb])
    onespad = s.tile([128, 2, 128], BF16, "onespad")
    s.op("dve", lambda e: e.memset(onespad[:], 0.0), writes=[onespad])
    for pos in range(2):
        s.op("dve", lambda e, pos=pos: e.memset(onespad[:, pos, pos * 64:pos * 64 + 64], 1.0), writes=[onespad])
    ps = make_psum(s)
    pring = Ring(ps["all"][0:6])
    ffn = FFNStage(s, NT, None, None, lnp, 1, c, ps, with_ffn=False)
    ybuf = s.tile([128, 8, NT], F32, "ybuf", dma=True)

    def V(eng, fn, reads, writes):
        s.op(eng, fn, reads=reads, writes=writes)

    wi = Ring([s.tile([128, 8, 128], BF16, "wi", dma=True) for _ in range(3)])
    wcR = [s.tile([128, 8, 128], BF16, "wcR", dma=True) for _ in range(NCH_C)]
    for ci in [8] + [i for i in range(NCH_C) if i != 8]:
        s.dma("pool", wcR[ci][:], wc_d[ci], wcR[ci], writes=[wcR[ci]])
    wbr = Ring([s.tile([128, 4, 128], BF16, "wbr", dma=True) for _ in range(3)])

    merged = s.tile([128, 8, NT], F32, "merged", dma=True)
    mem32 = merged if NT == N_MEM else s.tile([128, 8, N_MEM], F32, "mem32", dma=True)
    s.dma("sp", mem32[:], memT.rearrange("k p n -> p k n"), mem32, writes=[mem32])
    memb = s.tile([128, 8, N_MEM], BF16, "memb")
    mst = [s.tile([128, N_MEM], F32, f"mst{i}") for i in range(4)]
    pm, pe_ = ps["m"], ps["e"]
    for m in range(8):
        V("act", lambda e, m=m: e.activation(out=mst[0][:], in_=mem32[:, m, :], func=AF.Square), [mem32], [mst[0]])
        mm(s, pm, pm[:, 0:N_MEM], c["onesD"], c["onesD"][:], mem32, mem32[:, m, :], m == 0, m == 7)
        mm(s, pe_, pe_[:, 0:N_MEM], c["onesD"], c["onesD"][:], mst[0], mst[0][:], m == 0, m == 7)
    eps1 = s.tile([128, 1], F32, "eps1")
    V("dve", lambda e: e.memset(eps1[:], LN_EPS), [], [eps1])
    V("act", lambda e: e.activation(out=mst[1][:], in_=pm[:, 0:N_MEM], func=AF.Copy), [pm], [mst[1]])
    V("dve", lambda e: e.tensor_tensor(out=mst[2][:], in0=mst[1][:], in1=mst[1][:], op=ALU.mult), [mst[1]], [mst[2]])
    V("dve", lambda e: e.tensor_tensor(out=mst[2][:], in0=pe_[:, 0:N_MEM], in1=mst[2][:], op=ALU.subtract), [pe_, mst[2]], [mst[2]])
    V("act", lambda e: e.activation(out=mst[2][:], in_=mst[2][:], func=AF.Sqrt, bias=eps1[:, 0:1]), [mst[2], eps1], [mst[2]])
    V("dve", lambda e: e.reciprocal(out=mst[2][:], in_=mst[2][:]), [mst[2]], [mst[2]])
    for m in range(8):
        V("dve", lambda e, m=m: e.tensor_tensor(out=mst[3][:], in0=mem32[:, m, :], in1=mst[1][:], op=ALU.subtract), [mem32, mst[1]], [mst[3]])
        V("dve", lambda e: e.tensor_tensor(out=mst[3][:], in0=mst[3][:], in1=mst[2][:], op=ALU.mult), [mst[3], mst[2]], [mst[3]])
        V("act", lambda e, m=m: e.activation(out=memb[:, m, :], in_=mst[3][:], func=AF.Identity,
                                             scale=mlnp[:, 0, 0, m:m + 1], bias=mlnp[:, 0, 1, m:m + 1]), [mst[3], mlnp], [memb])
    kmemT = s.tile([128, 4, N_MEM], BF16, "kmemT")
    for h in range(4):
        w = wi.next()
        s.dma("pool", w[:], wmk_d[h], w, writes=[w])
        p = pring.next()
        for k in range(8):
            mm(s, p, p[:, 0:N_MEM], w, w[:, k, :], memb, memb[:, k, :], k == 0, k == 7)
        V("act", lambda e, h=h, p=p: e.activation(out=kmemT[:, h, :], in_=p[:, 0:N_MEM], func=AF.Copy), [p], [kmemT])
    wmv = s.tile([128, 8, 512], BF16, "wmv", dma=True)
    s.dma("pool", wmv[:], wmv_d, wmv, writes=[wmv])
    vmem = s.tile([128, 2, 512], BF16, "vmem")
    for mc in range(2):
        p = pring.next()
        for k in range(8):
            mm(s, p, p[:], memb, memb[:, k, mc * 128:(mc + 1) * 128], wmv, wmv[:, k, :], k == 0, k == 7)
        V("act", lambda e, mc=mc, p=p: e.activation(out=vmem[:, mc, :], in_=p[:], func=AF.Copy), [p], [vmem])
    wvb = s.tile([128, 8, 128], BF16, "wvb", dma=True)
    s.dma("pool", wvb[:], wv_d, wvb, writes=[wvb])

    NSLOT = 4
    kT = s.tile([128, NSLOT, 128], BF16, "kTs")
    vpad = s.tile([128, NSLOT, 4, 128], BF16, "vpad")
    V("dve", lambda e: e.memset(vpad[:], 0.0), [], [vpad])
    h32 = s.tile([128, 8, NT], F32, "h32", dma=True)
    hbr = [s.tile([128, 8, NT], BF16, "hb", dma=True) for _ in range(2)]
    hh32 = s.tile([128, 8, 128], F32, "hh32", dma=True)
    hhb = s.tile([128, 8, 128], BF16, "hhb")
    zs = s.tile([128, 4, NT], F32, "zs")
    swq = s.tile([128, 4, NT], BF16, "swq")
    xaq = s.tile([128, 4, NT], BF16, "xaq")
    gsb = Ring([s.tile([128, NT], F32, f"gsb{i}") for i in range(3)])
    odn = s.tile([128, 4, NT], F32, "odn", dma=True)
    br = [s.tile([128, 4, NT], BF16, f"br{i}") for i in range(3)]
    pexp = Ring([s.tile([128, 512], F32, f"pexp{i}") for i in range(2)])
    pT = Ring([s.tile([128, 512], BF16, f"pT{i}") for i in range(3)])
    pm_sw = [s.tile([128, 512], BF16, f"pmsw{i}") for i in range(8)]
    rden = Ring([s.tile([128, NT], F32, f"rden{i}") for i in range(2)])
    mergedb = s.tile([128, 8, NT], BF16, "mergedb")
    gtmp = Ring([s.tile([128, NT], F32, f"gtmp{i}") for i in range(2)])
    stores = []

    def kv_block(src_b, cols, slot):
        w = wcR[8]
        p = pring.next()
        for k in range(8):
            mm(s, p, p[:, 0:128], w, w[:, k, :], src_b, src_b[:, k, cols], k == 0, k == 7)
        V("act", lambda e, p=p, slot=slot: e.activation(out=kT[:, slot, :], in_=p[:, 0:128], func=AF.Copy), [p], [kT])
        p2 = pring.next()
        for k in range(8):
            mm(s, p2, p2[:, 0:128], src_b, src_b[:, k, cols], wvb, wvb[:, k, :], k == 0, k == 7)
        for kv in range(2):
            for pos in range(2):
                V("dve", lambda e, p2=p2, slot=slot, kv=kv, pos=pos: e.tensor_copy(
                    out=vpad[:, slot, kv * 2 + pos, pos * 64:pos * 64 + 64], in_=p2[:, kv * 64:kv * 64 + 64]), [p2], [vpad])

    s.dma("sp", hh32[:], hT[:, :, 0:128].rearrange("k p n -> p k n"), hh32, writes=[hh32])
    V("act", lambda e: e.activation(out=hhb[:], in_=hh32[:], func=AF.Copy), [hh32], [hhb])
    kv_block(hhb, slice(0, 128), 0)

    cur = {}

    def load_h(t):
        hbn = hbr[t % 2]
        s.dma("pool", hbn[:], hT[:, :, 128 + t * NT:128 + (t + 1) * NT].rearrange("k p n -> p k n"), hbn, writes=[hbn])

    def proj(ci, out_fn):
        hb = cur["hb"]
        w = wcR[ci]
        p = pring.next()
        for k in range(8):
            mm(s, p, p[:, 0:NT], w, w[:, k, :], hb, hb[:, k, :], k == 0, k == 7)
        out_fn(p)

    NTILES = NTOK // NT
    load_h(0)
    for t in range(NTILES):
        c0 = t * NT
        hb = hbr[t % 2]
        cur["hb"] = hb
        s.dma("sp", h32[:], hT[:, :, 128 + c0:128 + c0 + NT].rearrange("k p n -> p k n"), h32, writes=[h32])
        s.dma("sp", odn[:], odn_d[:, :, c0:c0 + NT].rearrange("h p n -> p h n"), odn, writes=[odn])
        for j in range(4):
            proj(j, lambda p, j=j: V("act", lambda e: e.activation(out=zs[:, j, :], in_=p[:, 0:NT], func=AF.Silu), [p], [zs]))
        for j in range(4):
            proj(4 + j, lambda p, j=j: V("act", lambda e: e.activation(out=swq[:, j, :], in_=p[:, 0:NT], func=AF.Copy), [p], [swq]))
        for j in range(4):
            proj(9 + j, lambda p, j=j: V("act", lambda e: e.activation(out=xaq[:, j, :], in_=p[:, 0:NT], func=AF.Copy), [p], [xaq]))
        for qb in range(NQB):
            gb = t * NQB + qb
            kv_block(hb, slice(qb * 128, (qb + 1) * 128), (gb + 1) % NSLOT)
        V("dve", lambda e: e.tensor_tensor(out=br[0][:], in0=odn[:], in1=zs[:], op=ALU.mult), [odn, zs], [br[0]])
        for h in range(4):
            pts = []
            for mc in range(2):
                p = pring.next()
                mm(s, p, p[:, 0:NT], kmemT, kmemT[:, h, mc * 128:(mc + 1) * 128], xaq, xaq[:, h, :], True, True)
                pt = pT.next()
                V("act", lambda e, p=p, pt=pt: e.activation(out=pt[:, 0:NT], in_=p[:, 0:NT], func=AF.Exp, scale=128 ** -0.5), [p], [pt])
                pts.append(pt)
            po, pd = pring.next(), pring.next()
            for mc in range(2):
                mm(s, po, po[:, 0:NT], vmem, vmem[:, mc, h * 128:(h + 1) * 128], pts[mc], pts[mc][:, 0:NT], mc == 0, mc == 1)
            for mc in range(2):
                mm(s, pd, pd[:, 0:NT], onesb, onesb[:], pts[mc], pts[mc][:, 0:NT], mc == 0, mc == 1)
            rd = rden.next()
            V("dve", lambda e, pd=pd, rd=rd: e.reciprocal(out=rd[:], in_=pd[:, 0:NT]), [pd], [rd])
            V("dve", lambda e, po=po, rd=rd, h=h: e.tensor_tensor(out=br[2][:, h, :], in0=po[:, 0:NT], in1=rd[:], op=ALU.mult), [po, rd], [br[2]])
        mi = 0 if t == 0 else 1
        for h in range(8):
            rows = slice(0, 64) if h < 4 else slice(64, 128)
            ch = h % 4
            p = pring.next()
            for qb in range(NQB):
                gb = t * NQB + qb
                qcols = slice(qb * 128, (qb + 1) * 128)
                mm(s, p, p[:, qb * 256:qb * 256 + 128], kT, kT[rows, gb % NSLOT, :], swq, swq[rows, ch, qcols], True, True)
                mm(s, p, p[:, qb * 256 + 128:qb * 256 + 256], kT, kT[rows, (gb + 1) % NSLOT, :], swq, swq[rows, ch, qcols], True, True)
            pe2 = pexp.next()
            V("act", lambda e, p=p, pe2=pe2: e.activation(out=pe2[:, 0:NQB * 256], in_=p[:, 0:NQB * 256], func=AF.Exp, scale=64 ** -0.5), [p], [pe2])
            V("dve", lambda e, pe2=pe2, h=h, mi=mi: e.tensor_tensor(out=pm_sw[h][:, 0:NQB * 256], in0=pe2[:, 0:NQB * 256], in1=msk[:, mi, 0:NQB * 256], op=ALU.mult),
              [pe2, msk], [pm_sw[h]])
        for pr in range(4):
            kv = pr // 2
            po, pd = pring.next(), pring.next()
            for qb in range(NQB):
                gb = t * NQB + qb
                qc = slice(qb * 128, (qb + 1) * 128)
                n = 0
                for pos in range(2):
                    h = pr * 2 + pos
                    for part, slot in ((0, gb % NSLOT), (1, (gb + 1) % NSLOT)):
                        pc = slice(qb * 256 + part * 128, qb * 256 + part * 128 + 128)
                        mm(s, po, po[:, qc], vpad, vpad[:, slot, kv * 2 + pos, :], pm_sw[h], pm_sw[h][:, pc], n == 0, n == 3)
                        n += 1
                n = 0
                for pos in range(2):
                    h = pr * 2 + pos
                    for part in range(2):
                        pc = slice(qb * 256 + part * 128, qb * 256 + part * 128 + 128)
                        mm(s, pd, pd[:, qc], onespad, onespad[:, pos, :], pm_sw[h], pm_sw[h][:, pc], n == 0, n == 3)
                        n += 1
            rd = rden.next()
            V("dve", lambda e, pd=pd, rd=rd, pr=pr: e.tensor_scalar(out=rd[:], in0=pd[:, 0:NT], scalar1=esink[:, pr:pr + 1], scalar2=None, op0=ALU.add),
              [pd, esink], [rd])
            V("dve", lambda e, rd=rd: e.reciprocal(out=rd[:], in_=rd[:]), [rd], [rd])
            V("dve", lambda e, po=po, rd=rd, pr=pr: e.tensor_tensor(out=br[1][:, pr, :], in0=po[:, 0:NT], in1=rd[:], op=ALU.mult), [po, rd], [br[1]])
        if t + 1 < NTILES:
            load_h(t + 1)
        for m in range(8):
            for n in range(3):
                gs = gsb.next()
                proj(13 + n * 8 + m, lambda p, gs=gs: V("act", lambda e: e.activation(out=gs[:], in_=p[:, 0:NT], func=AF.Sigmoid), [p], [gs]))
                w = wbr.next()
                s.dma("pool", w[:], wbr_d[n * 8 + m], w, writes=[w])
                p = pring.next()
                for k in range(4):
                    mm(s, p, p[:, 0:NT], w, w[:, k, :], br[n], br[n][:, k, :], k == 0, k == 3)
                if n == 0:
                    V("dve", lambda e, p=p, m=m, gs=gs: e.tensor_tensor(out=merged[:, m, :], in0=p[:, 0:NT], in1=gs[:], op=ALU.mult),
                      [p, gs], [merged])
                else:
                    g = gtmp.next()
                    V("dve", lambda e, p=p, gs=gs, g=g: e.tensor_tensor(out=g[:], in0=p[:, 0:NT], in1=gs[:], op=ALU.mult),
                      [p, gs], [g])
                    V("dve", lambda e, m=m, g=g: e.tensor_tensor(out=merged[:, m, :], in0=merged[:, m, :], in1=g[:], op=ALU.add),
                      [merged, g], [merged])
        V("act", lambda e: e.activation(out=mergedb[:], in_=merged[:], func=AF.Copy), [merged], [mergedb])
        for m in range(8):
            w = wi.next()
            s.dma("pool", w[:], wo_d[m], w, writes=[w])
            p = pring.next()
            for k in range(8):
                mm(s, p, p[:, 0:NT], w, w[:, k, :], mergedb, mergedb[:, k, :], k == 0, k == 7)
            V("dve", lambda e, p=p, m=m: e.scalar_tensor_tensor(out=ybuf[:, m, :], in0=p[:, 0:NT], scalar=1.0 / ALPHA, in1=h32[:, m, :],
                                                              op0=ALU.mult, op1=ALU.add), [p, h32], [ybuf])
        ffn.ln(ybuf, None, 1)
        stores.append(s.dma("sp", outT[:, :, c0:c0 + NT].rearrange("k p n -> p k n"), ybuf[:], ybuf, reads=[ybuf]))
    s.finish(stores)
    s.emit()
    return nc


def lay_wc(w_in):
    swq = w_in[:, 2056:2568].reshape(1024, 8, 64)
    swq_p = np.concatenate([np.concatenate([swq[:, j], swq[:, 4 + j]], axis=1) for j in range(4)], axis=1)
    wcat = np.concatenate([w_in[:, 1544:2056], swq_p, w_in[:, 2568:2696], w_in[:, 2824:3336], w_in[:, 3336:6408]], axis=1)
    return lay_w_kmc(wcat)


def lay_pkc(w):
    K, C = w.shape
    return np.ascontiguousarray(w.reshape(K // 128, 128, C).transpose(1, 0, 2))


def lay_wbr(wb):
    a = wb.reshape(3, 4, 128, 8, 128).transpose(0, 3, 2, 1, 4)
    return np.ascontiguousarray(a.reshape(24, 128, 4, 128))


def make_masks(halo_valid):
    k = np.arange(128)[:, None]
    q = np.arange(128)[None, :]
    prev = (k > q).astype(np.float32)
    cur = (k <= q).astype(np.float32)
    std = np.concatenate([prev, cur, prev, cur], axis=1)
    first = np.concatenate([prev * halo_valid, cur, prev, cur], axis=1)
    return np.ascontiguousarray(np.stack([first, std], axis=1)).astype(np.float32)


def lay_sink(sinks):
    c = np.zeros((128, 4), np.float32)
    for pr in range(4):
        c[:64, pr] = sinks[2 * pr]
        c[64:, pr] = sinks[2 * pr + 1]
    return c


_PROGS = {}
TOK_PER_CORE = SEQ * BATCH // NCORES
NSEG = SEQ // TOK_PER_CORE


def _prog(name):
    if name not in _PROGS:
        if name == "A":
            _PROGS[name] = build_A(TOK_PER_CORE)
        elif name == "B":
            _PROGS[name] = build_B(SEQ)
        elif name == "F":
            _PROGS[name] = build_A(TOK_PER_CORE, proj=False, ln_idx=2)
        else:
            _PROGS[name] = build_C(TOK_PER_CORE)
    return _PROGS[name]


def _run(name, in_maps):
    res = run_bass_kernel_spmd(_prog(name), in_maps, core_ids=list(range(NCORES)))
    return res.results


def kernel(x, mem, mem_ln_g, mem_ln_b, ln_g, ln_b, ffn1_w_gu, ffn1_w_down, w_in, dn_conv_w,
           dn_a_log, dn_dt_bias, dn_norm_w, swa_sinks, w_mem_kv, w_branch, w_out, ffn2_w_gu, ffn2_w_down):
    f = lambda a: np.asarray(a, dtype=np.float32)
    cur = f(x)
    mem = f(mem)
    cst = make_cst()
    cstB, ccol = make_cstB()
    NT_ = TOK_PER_CORE
    for l in range(DEPTH):
        win = f(w_in[l])
        lnp = lay_lnp(f(ln_g[l]), f(ln_b[l]))
        commonA = dict(wgu=lay_wgu(f(ffn1_w_gu[l])), wdn=lay_w_kmc(f(ffn1_w_down[l])), wqkv=lay_w_kmc(win[:, :1536]),
                       wba=lay_pkc(win[:, 1536:1544]), lnp=lnp, cst=cst)
        insA = []
        for c in range(NCORES):
            b, sg = divmod(c, NSEG)
            xs = cur[b, sg * NT_:(sg + 1) * NT_]
            insA.append(dict(xT=np.ascontiguousarray(xs.T.reshape(8, 128, NT_)), **commonA))
        rA = _run("A", insA)
        convw = f(dn_conv_w[l])
        insB = []
        for c in range(NCORES):
            b, hd = divmod(c, 4)
            qkvp = np.concatenate([rA[b * NSEG + sg]["qkvT"][[hd, 4 + hd, 8 + hd]] for sg in range(NSEG)], axis=2)
            brow = np.concatenate([rA[b * NSEG + sg]["baT"][hd] for sg in range(NSEG)])
            arow = np.concatenate([rA[b * NSEG + sg]["baT"][4 + hd] for sg in range(NSEG)])
            cw = np.stack([convw[:, w * 512 + hd * 128: w * 512 + hd * 128 + 128] for w in range(3)], 0)
            smallp = np.zeros((128, 32), np.float32)
            smallp[:, 0:12] = cw.transpose(2, 0, 1).reshape(128, 12)
            smallp[:, 12] = f(dn_a_log[l])[hd]
            smallp[:, 13] = f(dn_dt_bias[l])[hd]
            smallp[:, 14] = f(dn_norm_w[l])
            smallp[:, 15:17] = ccol
            insB.append(dict(qkvp=np.ascontiguousarray(qkvp), bcol=np.ascontiguousarray(brow.reshape(SEQ // 128, 128).T),
                             acol=np.ascontiguousarray(arow.reshape(SEQ // 128, 128).T),
                             smallp=smallp, cstB=cstB))
        rB = _run("B", insB)
        wkv = f(w_mem_kv[l])
        mg = np.stack([f(mem_ln_g)] * 3)
        mb = np.stack([f(mem_ln_b)] * 3)
        commonC = dict(wc=lay_wc(win), wv=lay_pkc(win[:, 2696:2824]), wmk=lay_w_kmc(wkv[:, :512]), wmv=lay_pkc(wkv[:, 512:]),
                       wbr=lay_wbr(f(w_branch[l])), wo=lay_w_kmc(f(w_out[l])),
                       lnp=lnp, mlnp=np.ascontiguousarray(lay_lnp(mg, mb)[:, 0:1]),
                       cst=cst, sinkc=lay_sink(f(swa_sinks[l])))
        insC = []
        for c in range(NCORES):
            b, sg = divmod(c, NSEG)
            hT = rA[c]["hT"]
            halo = rA[c - 1]["hT"][:, :, -128:] if sg > 0 else np.zeros((8, 128, 128), np.float32)
            odn = np.stack([rB[b * 4 + hd]["onT"][:, sg * NT_:(sg + 1) * NT_] for hd in range(4)], 0)
            insC.append(dict(hT=np.ascontiguousarray(np.concatenate([halo, hT], axis=2)), odn=np.ascontiguousarray(odn),
                             memT=np.ascontiguousarray(mem[b].T.reshape(8, 128, N_MEM)),
                             msk=make_masks(1.0 if sg > 0 else 0.0), **commonC))
        rC = _run("C", insC)
        commonF = dict(wgu=lay_wgu(f(ffn2_w_gu[l])), wdn=lay_w_kmc(f(ffn2_w_down[l])), lnp=lnp, cst=cst)
        rF = _run("F", [dict(xT=rC[c]["outT"], **commonF) for c in range(NCORES)])
        nxt = np.empty_like(cur)
        for c in range(NCORES):
            b, sg = divmod(c, NSEG)
            nxt[b, sg * NT_:(sg + 1) * NT_] = rF[c]["hT"].reshape(D_MODEL, NT_).T
        cur = nxt
    return cur
```

```python
import math
import numpy as np
import concourse.bass as bass
import concourse.mybir as mybir
from concourse.bass_utils import run_bass_kernel_spmd

F32 = mybir.dt.float32
BF16 = mybir.dt.bfloat16
AF = mybir.ActivationFunctionType
ALU = mybir.AluOpType

D_MODEL = 1024
BATCH = 2
SEQ = 16384
DEPTH = 2
N_MEM = 256
D_FF = 2816
D_IN = 6408
NCORES = 8
ALPHA = (2 * DEPTH) ** 0.25
LN_EPS = 1e-5
RMS_EPS = 1e-6
SAME_ENGINE_SYNC = True
PSUM_READ_SERIALIZE = True


class T:
    def __init__(self, h, sem=None):
        self.h = h
        self.w = None
        self.r = {}
        self.sem = sem
        self.semval = 0

    def __getitem__(self, idx):
        return self.h[idx]


class TV:
    def __init__(self, parent, ap):
        self.p = parent
        self.ap = ap

    @property
    def is_psum(self):
        return getattr(self.p, "is_psum", False)

    @property
    def w(self):
        return self.p.w

    @w.setter
    def w(self, v):
        self.p.w = v

    @property
    def r(self):
        return self.p.r

    @r.setter
    def r(self, v):
        self.p.r = v

    def __getitem__(self, idx):
        return self.ap[idx]


class Sched:
    ENG = ("pe", "act", "dve", "pool", "sp")

    def __init__(self, nc):
        self.nc = nc
        self.prog = {e: [] for e in self.ENG}
        self.sem = {e: nc.alloc_semaphore("sem_" + e) for e in ("pe", "act", "dve", "pool")}
        self.cnt = {e: 0 for e in self.sem}
        self.seen = {e: {} for e in self.ENG}
        self.uid = 0
        self.final = []

    def tile(self, shape, dt, name=None, psum=False, dma=False):
        self.uid += 1
        name = f"{name or 't'}_{self.uid}"
        if psum:
            h = self.nc.alloc_psum_tensor(name, list(shape), dt)
        else:
            h = self.nc.alloc_sbuf_tensor(name, list(shape), dt)
        sem = self.nc.alloc_semaphore("ds_" + name) if dma else None
        t = T(h, sem)
        t.is_psum = psum
        return t

    def _dep(self, eng, ev):
        key, sem, val = ev
        if key == eng and (eng == "pe" or not SAME_ENGINE_SYNC):
            return
        if self.seen[eng].get(key, 0) >= val:
            return
        self.seen[eng][key] = val
        self.prog[eng].append(lambda e, sem=sem, val=val: e.wait_ge(sem, val))

    def _deps(self, eng, reads, writes):
        for t in reads:
            if t.w is not None:
                self._dep(eng, t.w)
            if PSUM_READ_SERIALIZE and getattr(t, "is_psum", False):
                for ev in t.r.values():
                    if ev[0] != eng:
                        self._dep(eng, ev)
        for t in writes:
            if t.w is not None:
                self._dep(eng, t.w)
            for ev in t.r.values():
                self._dep(eng, ev)

    def _mark(self, ev, reads, writes):
        for t in reads:
            t.r[ev[0]] = ev
        for t in writes:
            t.w = ev
            t.r = {}

    def op(self, eng, fn, reads=(), writes=()):
        self._deps(eng, reads, writes)
        self.cnt[eng] += 1
        val = self.cnt[eng]
        sem = self.sem[eng]
        self.prog[eng].append(lambda e, fn=fn, sem=sem: fn(e).then_inc(sem, 1))
        self._mark((eng, sem, val), reads, writes)

    def dma(self, q, out, in_, semtile, reads=(), writes=()):
        self._deps(q, reads, writes)
        semtile.semval += 16
        sem, val = semtile.sem, semtile.semval
        self.prog[q].append(lambda e, out=out, in_=in_, sem=sem: e.dma_start(out=out, in_=in_).then_inc(sem, 16))
        ev = (("d", id(semtile)), sem, val)
        self._mark(ev, reads, writes)
        return ev

    def finish(self, evs):
        for ev in evs:
            self._dep("sp", ev)

    def emit(self):
        nc = self.nc
        with nc.Block() as block:
            @block.tensor
            def _(e):
                for f in self.prog["pe"]:
                    f(e)

            @block.scalar
            def _(e):
                for f in self.prog["act"]:
                    f(e)

            @block.vector
            def _(e):
                for f in self.prog["dve"]:
                    f(e)

            @block.gpsimd
            def _(e):
                for f in self.prog["pool"]:
                    f(e)

            @block.sync
            def _(e):
                for f in self.prog["sp"]:
                    f(e)


class Ring:
    def __init__(self, tiles):
        self.tiles = tiles
        self.i = 0

    def next(self):
        t = self.tiles[self.i % len(self.tiles)]
        self.i += 1
        return t


def mm(s, out_t, out_ap, l_t, l_ap, r_t, r_ap, start, stop):
    s.op("pe", lambda e: e.matmul(out_ap, lhsT=l_ap, rhs=r_ap, start=start, stop=stop),
         reads=[l_t, r_t], writes=[out_t])


class FFNStage:
    def __init__(self, s, NT, wgu_d, wdn_d, lnp_t, ln_idx, consts, ps, resident=False, with_ffn=True):
        self.s, self.NT = s, NT
        self.wgu_d, self.wdn_d = wgu_d, wdn_d
        self.lnp, self.ln_idx = lnp_t, ln_idx
        self.c = consts
        self.ps = ps
        self.resident = resident
        if with_ffn:
            if resident:
                self.wgR = [s.tile([128, 8, 256], BF16, "wgR", dma=True) for _ in range(22)]
                self.wdR = [s.tile([128, 22, 128], BF16, "wdR", dma=True) for _ in range(8)]
                for j in range(22):
                    s.dma("pool", self.wgR[j][:], wgu_d[j], self.wgR[j], writes=[self.wgR[j]])
                for m in range(8):
                    s.dma("pool", self.wdR[m][:], wdn_d[m], self.wdR[m], writes=[self.wdR[m]])
            else:
                self.wg = Ring([s.tile([128, 8, 256], BF16, "wg", dma=True) for _ in range(3)])
                self.wd = Ring([s.tile([128, 22, 128], BF16, "wd", dma=True) for _ in range(2)])
            self.act = s.tile([128, 22, NT], BF16, "act")
            self.sg = Ring([s.tile([128, NT], F32, "sg") for _ in range(2)])
        self.ysq = Ring([s.tile([128, NT], F32, "ysq") for _ in range(2)])
        self.mean = s.tile([128, NT], F32, "mean")
        self.tmp = s.tile([128, NT], F32, "lntmp")
        self.rstd = s.tile([128, NT], F32, "rstd")
        self.d = Ring([s.tile([128, NT], F32, "lnd") for _ in range(2)])

    def run(self, x32, xb, hb, prefetch=None):
        s, NT = self.s, self.NT
        ps = self.ps
        for j in range(22):
            if self.resident:
                wg = self.wgR[j]
            else:
                wg = self.wg.next()
                s.dma("pool", wg[:], self.wgu_d[j], wg, writes=[wg])
            pg, pu = ps["g"].next(), ps["u"].next()
            for k in range(8):
                mm(s, pg, pg[:, 0:NT], wg, wg[:, k, 0:128], xb, xb[:, k, :], k == 0, k == 7)
            for k in range(8):
                mm(s, pu, pu[:, 0:NT], wg, wg[:, k, 128:256], xb, xb[:, k, :], k == 0, k == 7)
            sg = self.sg.next()
            s.op("act", lambda e, sg=sg, pg=pg: e.activation(out=sg[:], in_=pg[:, 0:NT], func=AF.Silu),
                 reads=[pg], writes=[sg])
            s.op("dve", lambda e, sg=sg, pu=pu, j=j: e.tensor_tensor(out=self.act[:, j, :], in0=sg[:], in1=pu[:, 0:NT], op=ALU.mult),
                 reads=[sg, pu], writes=[self.act])
        for m in range(8):
            if self.resident:
                wd = self.wdR[m]
            else:
                wd = self.wd.next()
                s.dma("pool", wd[:], self.wdn_d[m], wd, writes=[wd])
            py = ps["y"].next()
            for k in range(22):
                mm(s, py, py[:, 0:NT], wd, wd[:, k, :], self.act, self.act[:, k, :], k == 0, k == 21)
            s.op("dve", lambda e, py=py, m=m: e.scalar_tensor_tensor(
                out=x32[:, m, :], in0=py[:, 0:NT], scalar=0.5 / ALPHA, in1=x32[:, m, :], op0=ALU.mult, op1=ALU.add),
                reads=[py, x32], writes=[x32])
        if prefetch is not None:
            prefetch()
        self.ln(x32, hb, self.ln_idx)

    def ln(self, y, hb, li):
        s, NT, ps = self.s, self.NT, self.ps
        pm, pe_ = ps["m"], ps["e"]
        for m in range(8):
            ysq = self.ysq.next()
            s.op("act", lambda e, ysq=ysq, m=m: e.activation(out=ysq[:], in_=y[:, m, :], func=AF.Square),
                 reads=[y], writes=[ysq])
            mm(s, pm, pm[:, 0:NT], self.c["onesD"], self.c["onesD"][:], y, y[:, m, :], m == 0, m == 7)
            mm(s, pe_, pe_[:, 0:NT], self.c["onesD"], self.c["onesD"][:], ysq, ysq[:], m == 0, m == 7)
        s.op("act", lambda e: e.activation(out=self.mean[:], in_=pm[:, 0:NT], func=AF.Copy), reads=[pm], writes=[self.mean])
        s.op("dve", lambda e: e.tensor_tensor(out=self.tmp[:], in0=self.mean[:], in1=self.mean[:], op=ALU.mult),
             reads=[self.mean], writes=[self.tmp])
        s.op("dve", lambda e: e.tensor_tensor(out=self.tmp[:], in0=pe_[:, 0:NT], in1=self.tmp[:], op=ALU.subtract),
             reads=[pe_, self.tmp], writes=[self.tmp])
        s.op("act", lambda e: e.activation(out=self.tmp[:], in_=self.tmp[:], func=AF.Ln, bias=self.c["epsln"][:, 0:1]),
             reads=[self.tmp, self.c["epsln"]], writes=[self.tmp])
        s.op("act", lambda e: e.activation(out=self.rstd[:], in_=self.tmp[:], func=AF.Exp, scale=-0.5), reads=[self.tmp], writes=[self.rstd])
        for m in range(8):
            d = self.d.next()
            s.op("dve", lambda e, d=d, m=m: e.tensor_tensor(out=d[:], in0=y[:, m, :], in1=self.mean[:], op=ALU.subtract),
                 reads=[y, self.mean], writes=[d])
            s.op("dve", lambda e, d=d: e.tensor_tensor(out=d[:], in0=d[:], in1=self.rstd[:], op=ALU.mult),
                 reads=[d, self.rstd], writes=[d])
            s.op("act", lambda e, d=d, m=m: e.activation(
                out=y[:, m, :], in_=d[:], func=AF.Identity,
                scale=self.lnp[:, li, 0, m:m + 1], bias=self.lnp[:, li, 1, m:m + 1]),
                reads=[d, self.lnp], writes=[y])
        if hb is not None:
            s.op("pool", lambda e: e.tensor_copy(out=hb[:], in_=y[:]), reads=[y], writes=[hb])


def make_psum(s):
    banks = [s.tile([128, 512], F32, f"ps{i}", psum=True) for i in range(8)]
    return {"g": Ring(banks[0:2]), "u": Ring(banks[2:4]), "y": Ring(banks[4:6]), "m": banks[6], "e": banks[7],
            "all": banks}


def load_consts(s, nc, cst_d):
    c = {}
    cst = s.tile([128, 3, 128], F32, "cst", dma=True)
    s.dma("sp", cst[:], cst_d, cst, writes=[cst])
    c["cst"] = cst
    onesD = s.tile([128, 128], F32, "onesD")
    s.op("dve", lambda e: e.tensor_copy(out=onesD[:], in_=cst[:, 0, :]), reads=[cst], writes=[onesD])
    c["onesD"] = onesD
    eps = s.tile([128, 1], F32, "epsln")
    s.op("dve", lambda e: e.memset(eps[:], LN_EPS / (ALPHA * ALPHA)), writes=[eps])
    c["epsln"] = eps
    return c


def build_A(NTOK, NT=256, proj=True, ln_idx=0):
    nc = bass.Bass("TRN2", target_bir_lowering=False)
    xT = nc.dram_tensor("xT", [8, 128, NTOK], F32, kind="ExternalInput").ap()
    wgu = nc.dram_tensor("wgu", [22, 128, 8, 256], F32, kind="ExternalInput").ap()
    wdn = nc.dram_tensor("wdn", [8, 128, 22, 128], F32, kind="ExternalInput").ap()
    if proj:
        wqkv = nc.dram_tensor("wqkv", [12, 128, 8, 128], F32, kind="ExternalInput").ap()
        wba = nc.dram_tensor("wba", [128, 8, 8], F32, kind="ExternalInput").ap()
    lnp_d = nc.dram_tensor("lnp", [128, 3, 2, 8], F32, kind="ExternalInput").ap()
    cst_d = nc.dram_tensor("cst", [128, 3, 128], F32, kind="ExternalInput").ap()
    hT = nc.dram_tensor("hT", [8, 128, NTOK], F32, kind="ExternalOutput").ap()
    if proj:
        qkvT = nc.dram_tensor("qkvT", [12, 128, NTOK], F32, kind="ExternalOutput").ap()
        baT = nc.dram_tensor("baT", [8, NTOK], F32, kind="ExternalOutput").ap()

    s = Sched(nc)
    c = load_consts(s, nc, cst_d)
    lnp = s.tile([128, 3, 2, 8], F32, "lnp", dma=True)
    s.dma("sp", lnp[:], lnp_d, lnp, writes=[lnp])
    ps = make_psum(s)
    ffn = FFNStage(s, NT, wgu, wdn, lnp, ln_idx, c, ps, resident=True)
    x32r = [s.tile([128, 8, NT], F32, "x32", dma=True) for _ in range(2)]
    xbr = [s.tile([128, 8, NT], BF16, "xb") for _ in range(2)]
    if proj:
        wb = s.tile([128, 8, 8], BF16, "wba", dma=True)
        s.dma("pool", wb[:], wba, wb, writes=[wb])
        wqR = [s.tile([128, 8, 128], BF16, "wqR", dma=True) for _ in range(12)]
        for m in range(12):
            s.dma("pool", wqR[m][:], wqkv[m], wqR[m], writes=[wqR[m]])
        qo = Ring([s.tile([128, NT], F32, "qo", dma=True) for _ in range(4)])
        bo = s.tile([8, NT], F32, "bo", dma=True)
    stores = []
    NTILES = NTOK // NT

    def load(t):
        x32, xb = x32r[t % 2], xbr[t % 2]
        cols = slice(t * NT, (t + 1) * NT)
        s.dma("sp", x32[:], xT[:, :, cols].rearrange("k p n -> p k n"), x32, writes=[x32])
        s.op("act", lambda e: e.activation(out=xb[:], in_=x32[:], func=AF.Copy), reads=[x32], writes=[xb])

    load(0)
    for t in range(NTILES):
        cols = slice(t * NT, (t + 1) * NT)
        x32, xb = x32r[t % 2], xbr[t % 2]
        hb = xb if proj else None
        ffn.run(x32, xb, hb, prefetch=(lambda t=t: load(t + 1)) if t + 1 < NTILES else None)
        stores.append(s.dma("sp", hT[:, :, cols].rearrange("k p n -> p k n"), x32[:], x32, reads=[x32]))
        if not proj:
            continue
        for m in range(12):
            w = wqR[m]
            pq = ps["g"].next()
            for k in range(8):
                mm(s, pq, pq[:, 0:NT], w, w[:, k, :], hb, hb[:, k, :], k == 0, k == 7)
            q = qo.next()
            s.op("act", lambda e, q=q, pq=pq: e.activation(out=q[:], in_=pq[:, 0:NT], func=AF.Copy), reads=[pq], writes=[q])
            stores.append(s.dma("sp", qkvT[m][:, cols], q[:], q, reads=[q]))
        pb = ps["u"].next()
        for k in range(8):
            mm(s, pb, pb[0:8, 0:NT], wb, wb[:, k, :], hb, hb[:, k, :], k == 0, k == 7)
        s.op("act", lambda e, pb=pb: e.activation(out=bo[:], in_=pb[0:8, 0:NT], func=AF.Copy), reads=[pb], writes=[bo])
        stores.append(s.dma("sp", baT[:, cols], bo[:], bo, reads=[bo]))
    s.finish(stores)
    s.emit()
    return nc


def lay_w_kmc(w, ncols_chunk=128):
    K, M = w.shape
    return np.ascontiguousarray(w.reshape(K // 128, 128, M // ncols_chunk, ncols_chunk).transpose(2, 1, 0, 3))


def lay_wgu(w):
    g = lay_w_kmc(w[:, :D_FF])
    u = lay_w_kmc(w[:, D_FF:])
    return np.ascontiguousarray(np.concatenate([g, u], axis=3))


def lay_lnp(g, b):
    a = np.stack([g.reshape(3, 8, 128), b.reshape(3, 8, 128)], axis=1)
    return np.ascontiguousarray(a.transpose(3, 0, 1, 2))


def make_cst():
    c = np.zeros((128, 3, 128), np.float32)
    c[:, 0, :] = 1.0 / D_MODEL
    c[:, 1, :] = np.eye(128, dtype=np.float32)
    c[:, 2, :] = 1.0
    return c


NEG = -1e30
B_DEBUG_STAGE = None
CONV_ENG = "dve"
AUX_ENG = "pool"
B_VAR = 0


def make_cstB():
    idx = np.arange(128)
    same = (idx[:, None] // 64) == (idx[None, :] // 64)
    c = np.zeros((128, 7, 128), np.float32)
    c[:, 0, :] = np.eye(128)
    c[:, 1, :] = 1.0
    c[:, 2, :] = (same & (idx[:, None] <= idx[None, :]))
    c[:, 3, :] = np.where(same & (idx[:, None] > idx[None, :]), 0.0, NEG)
    c[:, 4, :] = np.where(same & (idx[None, :] >= idx[:, None]), 0.0, NEG)
    c[63, 5, :] = 1.0
    c[127, 6, :] = 1.0
    col = np.zeros((128, 2), np.float32)
    col[64:, 0] = NEG
    col[:64, 1] = NEG
    return c, col


def build_B(S_LEN, NT=512):
    nc = bass.Bass("TRN2", target_bir_lowering=False)
    NBLK = S_LEN // 128
    qkvp = nc.dram_tensor("qkvp", [3, 128, S_LEN], F32, kind="ExternalInput").ap()
    bcol_d = nc.dram_tensor("bcol", [128, NBLK], F32, kind="ExternalInput").ap()
    acol_d = nc.dram_tensor("acol", [128, NBLK], F32, kind="ExternalInput").ap()
    smallp_d = nc.dram_tensor("smallp", [128, 32], F32, kind="ExternalInput").ap()
    cst_d = nc.dram_tensor("cstB", [128, 7, 128], F32, kind="ExternalInput").ap()
    onT = nc.dram_tensor("onT", [128, S_LEN], F32, kind="ExternalOutput").ap()

    s = Sched(nc)
    cst = s.tile([128, 7, 128], F32, "cstB", dma=True)
    s.dma("sp", cst[:], cst_d, cst, writes=[cst])
    smallp = s.tile([128, 32], F32, "smallp", dma=True)
    s.dma("sp", smallp[:], smallp_d, smallp, writes=[smallp])
    ccol = TV(smallp, smallp[:, 15:17])
    bcol = s.tile([128, NBLK], F32, "bcol", dma=True)
    s.dma("sp", bcol[:], bcol_d, bcol, writes=[bcol])
    acol = s.tile([128, NBLK], F32, "acol", dma=True)
    s.dma("sp", acol[:], acol_d, acol, writes=[acol])
    convw = TV(smallp, smallp[:, 0:12].rearrange("p (w j) -> p w j", j=4))
    scal = TV(smallp, smallp[:, 12:15])
    I_, ONES, TRI, MBS, MBT, SELA, SELB = [cst[:, i, :] for i in range(7)]

    banks = [s.tile([128, 512], F32, f"psB{i}", psum=True) for i in range(8)]
    big = Ring(banks[0:2])
    small = Ring([TV(bk, bk[:, q * 128:(q + 1) * 128]) for q in range(4) for bk in banks[3:8]])
    po_r = Ring([TV(banks[2], banks[2][:, q * 128:(q + 1) * 128]) for q in range(4)])

    def V(eng, fn, reads, writes):
        s.op(eng, fn, reads=reads, writes=writes)

    def mmf(out_t, out_ap, l_t, l_ap, r_t, r_ap, start=True, stop=True):
        mm(s, out_t, out_ap, l_t, l_ap, r_t, r_ap, start, stop)

    def col_tile(name, n=NBLK):
        return s.tile([128, n], F32, name)

    one_c = s.tile([128, 1], F32, "one_c")
    V("dve", lambda e: e.memset(one_c[:], 1.0), [], [one_c])
    eps_c = s.tile([128, 1], F32, "eps_c")
    V("dve", lambda e: e.memset(eps_c[:], RMS_EPS), [], [eps_c])
    ones128 = s.tile([128, 128], F32, "ones128")
    V("dve", lambda e: e.tensor_scalar(out=ones128[:], in0=ONES, scalar1=1.0 / 128, scalar2=None, op0=ALU.mult), [cst], [ones128])
    beta = col_tile("beta")
    V("act", lambda e: e.activation(out=beta[:], in_=bcol[:], func=AF.Sigmoid), [bcol], [beta])
    xg = col_tile("xg")
    V("dve", lambda e: e.tensor_scalar(out=xg[:], in0=acol[:], scalar1=scal[:, 1:2], scalar2=None, op0=ALU.add), [acol, scal], [xg])
    ax = col_tile("ax")
    V("dve", lambda e: e.tensor_scalar(out=ax[:], in0=xg[:], scalar1=-1.0, scalar2=None, op0=ALU.mult), [xg], [ax])
    V("dve", lambda e: e.tensor_tensor(out=ax[:], in0=ax[:], in1=xg[:], op=ALU.max), [ax, xg], [ax])
    V("act", lambda e: e.activation(out=ax[:], in_=ax[:], func=AF.Exp, scale=-1.0), [ax], [ax])
    V("act", lambda e: e.activation(out=ax[:], in_=ax[:], func=AF.Ln, bias=one_c[:, 0:1]), [ax, one_c], [ax])
    V("dve", lambda e: e.tensor_scalar(out=xg[:], in0=xg[:], scalar1=0.0, scalar2=None, op0=ALU.max), [xg], [xg])
    V("dve", lambda e: e.tensor_tensor(out=xg[:], in0=xg[:], in1=ax[:], op=ALU.add), [xg, ax], [xg])
    nea = s.tile([128, 1], F32, "nea")
    V("act", lambda e: e.activation(out=nea[:], in_=scal[:, 0:1], func=AF.Exp), [scal], [nea])
    V("dve", lambda e: e.tensor_scalar(out=nea[:], in0=nea[:], scalar1=-1.0, scalar2=None, op0=ALU.mult), [nea], [nea])
    gc = col_tile("gc")
    V("dve", lambda e: e.tensor_scalar(out=gc[:], in0=xg[:], scalar1=nea[:, 0:1], scalar2=None, op0=ALU.mult), [xg, nea], [gc])
    gcum, ngcum, bexpg, colA, colB, eglA, eglB = [col_tile(n) for n in ("gcum", "ngcum", "bexpg", "colA", "colB", "eglA", "eglB")]
    for c0 in range(0, NBLK, 512):
        c1 = min(NBLK, c0 + 512)
        w = c1 - c0
        pb = big.next()
        mmf(pb, pb[:, 0:w], cst, TRI, gc, gc[:, c0:c1])
        V("act", lambda e, pb=pb, c0=c0, c1=c1, w=w: e.activation(out=gcum[:, c0:c1], in_=pb[:, 0:w], func=AF.Copy), [pb], [gcum])
    V("dve", lambda e: e.tensor_scalar(out=ngcum[:], in0=gcum[:], scalar1=-1.0, scalar2=None, op0=ALU.mult), [gcum], [ngcum])
    V("act", lambda e: e.activation(out=bexpg[:], in_=gcum[:], func=AF.Exp), [gcum], [bexpg])
    V("dve", lambda e: e.tensor_tensor(out=bexpg[:], in0=bexpg[:], in1=beta[:], op=ALU.mult), [bexpg, beta], [bexpg])
    for (SEL, col, egl, mi) in ((SELA, colA, eglA, 0), (SELB, colB, eglB, 1)):
        for c0 in range(0, NBLK, 512):
            c1 = min(NBLK, c0 + 512)
            w = c1 - c0
            pb = big.next()
            mmf(pb, pb[:, 0:w], cst, SEL, gcum, gcum[:, c0:c1])
            V("act", lambda e, pb=pb, egl=egl, c0=c0, c1=c1, w=w: e.activation(out=egl[:, c0:c1], in_=pb[:, 0:w], func=AF.Exp), [pb], [egl])
            V("dve", lambda e, pb=pb, col=col, c0=c0, c1=c1, w=w: e.tensor_tensor(out=col[:, c0:c1], in0=pb[:, 0:w], in1=gcum[:, c0:c1], op=ALU.subtract),
              [pb, gcum], [col])
        V("act", lambda e, col=col, mi=mi: e.activation(out=col[:], in_=col[:], func=AF.Exp, bias=ccol[:, mi:mi + 1]), [col, ccol], [col])

    S_r = [s.tile([128, 128], F32, f"S{i}") for i in range(2)]
    V("dve", lambda e: e.memset(S_r[0][:], 0.0), [], [S_r[0]])
    state = {"S": 0}
    NTILE = S_LEN // NT
    NB = NT // 128

    def alloc_set():
        d = {}
        d["pre"] = s.tile([128, 3, NT + 8], F32, "pre", dma=True)
        d["cv"] = [s.tile([128, NT], F32, f"cv{i}") for i in range(3)]
        d["sq"] = s.tile([128, NT], F32, "sq")
        d["rin"] = s.tile([128, NT], F32, "rin")
        d["qT"] = s.tile([128, NT], F32, "qT")
        d["kT"] = s.tile([128, NT], F32, "kT")
        d["kT2"] = s.tile([128, NT], F32, "kT2")
        for nm in ("dGn", "dGp", "ES", "E2", "EG", "A", "Bm", "N", "A2", "B2", "RHSv", "RHSw", "ktA", "ktB", "u", "wT", "qkT", "qdT"):
            d[nm] = [s.tile([128, 128], F32, f"{nm}{i}") for i in range(NB)]
        d["A2b"] = [s.tile([128, 128], F32, f"A2b{i}") for i in range(NB)]
        d["B2b"] = [s.tile([128, 128], F32, f"B2b{i}") for i in range(NB)]
        return d

    sets = [alloc_set(), alloc_set()]
    vn_r = Ring([s.tile([128, 128], F32, f"vn{i}") for i in range(2)])
    oT_r = Ring([s.tile([128, NT], F32, f"oT{i}", dma=True) for i in range(2)])
    osq = s.tile([128, NT], F32, "osq")
    orin = s.tile([128, NT], F32, "orin")
    stores = []

    def prep(t, d):
        c0 = t * NT
        pre = d["pre"]
        if t == 0:
            V("dve", lambda e: e.memset(pre[:, :, 0:8], 0.0), [], [pre])
            s.dma("sp", pre[:, :, 8:NT + 8], qkvp[:, :, 0:NT].rearrange("w p n -> p w n"), pre, writes=[pre])
        else:
            s.dma("sp", pre[:], qkvp[:, :, c0 - 8:c0 + NT].rearrange("w p n -> p w n"), pre, writes=[pre])
        for w in range(3):
            cv = d["cv"][w]
            V("dve", lambda e, cv=cv, w=w: e.tensor_scalar(out=cv[:], in0=pre[:, w, 5:5 + NT], scalar1=convw[:, w, 0:1], scalar2=None, op0=ALU.mult),
              [pre, convw], [cv])
            for j in range(1, 4):
                V("dve", lambda e, cv=cv, w=w, j=j: e.scalar_tensor_tensor(
                    out=cv[:], in0=pre[:, w, 5 + j:5 + j + NT], scalar=convw[:, w, j:j + 1], in1=cv[:], op0=ALU.mult, op1=ALU.add),
                  [pre, convw, cv], [cv])
            V("act", lambda e, cv=cv: e.activation(out=cv[:], in_=cv[:], func=AF.Silu), [cv], [cv])
        yield
        for w, dst, sc in ((0, d["qT"], 128 ** -0.5), (1, d["kT"], 1.0)):
            cv = d["cv"][w]
            V("act", lambda e, cv=cv: e.activation(out=d["sq"][:], in_=cv[:], func=AF.Square), [cv], [d["sq"]])
            pb = big.next()
            mmf(pb, pb[:, 0:NT], cst, ONES, d["sq"], d["sq"][:])
            V("act", lambda e, pb=pb: e.activation(out=d["rin"][:], in_=pb[:, 0:NT], func=AF.Ln, bias=eps_c[:, 0:1]), [pb, eps_c], [d["rin"]])
            V("act", lambda e: e.activation(out=d["rin"][:], in_=d["rin"][:], func=AF.Exp, scale=-0.5), [d["rin"]], [d["rin"]])
            V("dve", lambda e, cv=cv, dst=dst, sc=sc: e.scalar_tensor_tensor(
                out=dst[:], in0=cv[:], scalar=sc, in1=d["rin"][:], op0=ALU.mult, op1=ALU.mult), [cv, d["rin"]], [dst])
        vT = d["cv"][2]
        qT, kT = d["qT"], d["kT"]
        kT2 = d["kT2"]
        V("act", lambda e: e.activation(out=kT2[:], in_=kT[:], func=AF.Copy), [kT], [kT2])
        yield
        blks = range(NB)

        def bs(i):
            return slice(i * 128, (i + 1) * 128)
        for i in blks:
            gb = t * NB + i
            V("dve", lambda e, i=i, gb=gb: e.tensor_scalar(out=d["dGn"][i][:], in0=I_, scalar1=ngcum[:, gb:gb + 1], scalar2=None, op0=ALU.mult),
              [cst, ngcum], [d["dGn"][i]])
            V("dve", lambda e, i=i, gb=gb: e.tensor_scalar(out=d["dGp"][i][:], in0=I_, scalar1=gcum[:, gb:gb + 1], scalar2=None, op0=ALU.mult),
              [cst, gcum], [d["dGp"][i]])
        for i in blks:
            gb = t * NB + i
            p1 = small.next()
            mmf(p1, p1[:], cst, ONES, d["dGn"][i], d["dGn"][i][:], True, False)
            mmf(p1, p1[:], cst, I_, cst, MBS, False, True)
            V("act", lambda e, i=i, gb=gb, p1=p1: e.activation(out=d["ES"][i][:], in_=p1[:], func=AF.Exp, bias=gcum[:, gb:gb + 1]),
              [p1, gcum], [d["ES"][i]])
            p2 = small.next()
            mmf(p2, p2[:], cst, ONES, d["dGp"][i], d["dGp"][i][:], True, False)
            mmf(p2, p2[:], cst, I_, cst, MBT, False, True)
            V("act", lambda e, i=i, gb=gb, p2=p2: e.activation(out=d["E2"][i][:], in_=p2[:], func=AF.Exp, bias=ngcum[:, gb:gb + 1]),
              [p2, ngcum], [d["E2"][i]])
            p3 = small.next()
            mmf(p3, p3[:], cst, ONES, d["dGp"][i], d["dGp"][i][:])
            V("act", lambda e, i=i, p3=p3: e.activation(out=d["EG"][i][:], in_=p3[:], func=AF.Exp), [p3], [d["EG"][i]])
        yield
        for i in blks:
            gb = t * NB + i
            pk = small.next()
            mmf(pk, pk[:], kT, kT[:, bs(i)], kT2, kT2[:, bs(i)])
            V("act", lambda e, i=i, gb=gb, pk=pk: e.activation(out=d["A"][i][:], in_=pk[:], func=AF.Identity, scale=beta[:, gb:gb + 1]),
              [pk, beta], [d["A"][i]])
            V("dve", lambda e, i=i: e.tensor_tensor(out=d["A"][i][:], in0=d["A"][i][:], in1=d["ES"][i][:], op=ALU.mult),
              [d["A"][i], d["ES"][i]], [d["A"][i]])
        yield
        for i in blks:
            pt = small.next()
            mmf(pt, pt[:], d["A"][i], d["A"][i][:], cst, I_)
            V("act", lambda e, i=i, pt=pt: e.activation(out=d["Bm"][i][:], in_=pt[:], func=AF.Copy), [pt], [d["Bm"][i]])
            if B_VAR != 1:
                V("dve", lambda e, i=i, pt=pt: e.scalar_tensor_tensor(
                    out=d["N"][i][:], in0=pt[:], scalar=-1.0, in1=I_, op0=ALU.mult, op1=ALU.add), [pt, cst], [d["N"][i]])
        yield
        curA = [d["A"][i] for i in blks]
        curB = [d["Bm"][i] for i in blks]
        for lvl in range(1, 6):
            nA = d["A2"] if lvl % 2 == 1 else d["A2b"]
            nB = d["B2"] if lvl % 2 == 1 else d["B2b"]
            for i in blks:
                pa = small.next()
                mmf(pa, pa[:], curB[i], curB[i][:], curA[i], curA[i][:])
                V("act", lambda e, i=i, pa=pa, nA=nA: e.activation(out=nA[i][:], in_=pa[:], func=AF.Copy), [pa], [nA[i]])
                if lvl < 5:
                    pbb = small.next()
                    mmf(pbb, pbb[:], curA[i], curA[i][:], curB[i], curB[i][:])
                    V("dve", lambda e, i=i, pbb=pbb, nB=nB: e.tensor_copy(out=nB[i][:], in_=pbb[:]), [pbb], [nB[i]])
            for i in blks:
                pn = small.next()
                mmf(pn, pn[:], nA[i], nA[i][:], d["N"][i], d["N"][i][:])
                V("dve", lambda e, i=i, pn=pn: e.tensor_tensor(out=d["N"][i][:], in0=pn[:], in1=d["N"][i][:], op=ALU.add),
                  [pn, d["N"][i]], [d["N"][i]])
            curA = [nA[i] for i in blks]
            curB = [nB[i] for i in blks]
            yield
        for i in blks:
            gb = t * NB + i
            pk = small.next()
            mmf(pk, pk[:], kT, kT[:, bs(i)], cst, I_)
            V("dve", lambda e, i=i, gb=gb, pk=pk: e.tensor_scalar(out=d["RHSw"][i][:], in0=pk[:], scalar1=bexpg[:, gb:gb + 1], scalar2=None, op0=ALU.mult),
              [pk, bexpg], [d["RHSw"][i]])
            V("dve", lambda e, i=i, gb=gb, pk=pk: e.tensor_scalar(out=d["ktA"][i][:], in0=pk[:], scalar1=colA[:, gb:gb + 1], scalar2=None, op0=ALU.mult),
              [pk, colA], [d["ktA"][i]])
            V("dve", lambda e, i=i, gb=gb, pk=pk: e.tensor_scalar(out=d["ktB"][i][:], in0=pk[:], scalar1=colB[:, gb:gb + 1], scalar2=None, op0=ALU.mult),
              [pk, colB], [d["ktB"][i]])
            pv = small.next()
            mmf(pv, pv[:], vT, vT[:, bs(i)], cst, I_)
            V("dve", lambda e, i=i, gb=gb, pv=pv: e.tensor_scalar(out=d["RHSv"][i][:], in0=pv[:], scalar1=beta[:, gb:gb + 1], scalar2=None, op0=ALU.mult),
              [pv, beta], [d["RHSv"][i]])
        yield
        for i in blks:
            pu = small.next()
            mmf(pu, pu[:], d["N"][i], d["N"][i][:], d["RHSv"][i], d["RHSv"][i][:])
            V("act", lambda e, i=i, pu=pu: e.activation(out=d["u"][i][:], in_=pu[:], func=AF.Copy), [pu], [d["u"][i]])
            pw = small.next()
            mmf(pw, pw[:], d["RHSw"][i], d["RHSw"][i][:], d["N"][i], d["N"][i][:])
            V("act", lambda e, i=i, pw=pw: e.activation(out=d["wT"][i][:], in_=pw[:], func=AF.Copy), [pw], [d["wT"][i]])
            pq = small.next()
            mmf(pq, pq[:], kT, kT[:, bs(i)], qT, qT[:, bs(i)])
            V("dve", lambda e, i=i, pq=pq: e.tensor_tensor(out=d["qkT"][i][:], in0=pq[:], in1=d["E2"][i][:], op=ALU.mult),
              [pq, d["E2"][i]], [d["qkT"][i]])
            V("dve", lambda e, i=i: e.tensor_tensor(out=d["qdT"][i][:], in0=qT[:, bs(i)], in1=d["EG"][i][:], op=ALU.mult),
              [qT, d["EG"][i]], [d["qdT"][i]])
        yield

    def recur(t, d):
        oT = oT_r.next()
        for i in range(NB):
            gb = t * NB + i
            po = po_r.next()
            for half, kt, egl in ((0, d["ktA"][i], eglA), (1, d["ktB"][i], eglB)):
                S = S_r[state["S"]]
                S2 = S_r[1 - state["S"]]
                hs = slice(half * 64, half * 64 + 64)
                p1 = small.next()
                mmf(p1, p1[:], d["wT"][i], d["wT"][i][:], S, S[:])
                vn = vn_r.next()
                V("dve", lambda e, i=i, p1=p1, vn=vn: e.tensor_tensor(out=vn[:], in0=d["u"][i][:], in1=p1[:], op=ALU.subtract),
                  [d["u"][i], p1], [vn])
                mmf(po, po[:, hs], S, S[:], d["qdT"][i], d["qdT"][i][:, hs], True, False)
                mmf(po, po[:, hs], vn, vn[:], d["qkT"][i], d["qkT"][i][:, hs], False, True)
                p2 = small.next()
                mmf(p2, p2[:], kt, kt[:], vn, vn[:])
                V("dve", lambda e, S=S, S2=S2, p2=p2, egl=egl, gb=gb: e.scalar_tensor_tensor(
                    out=S2[:], in0=S[:], scalar=egl[:, gb:gb + 1], in1=p2[:], op0=ALU.mult, op1=ALU.add),
                  [S, egl, p2], [S2])
                state["S"] = 1 - state["S"]
                yield
            V("act", lambda e, i=i, po=po, oT=oT: e.activation(out=oT[:, i * 128:(i + 1) * 128], in_=po[:], func=AF.Copy), [po], [oT])
        V("act", lambda e: e.activation(out=osq[:], in_=oT[:], func=AF.Square), [oT], [osq])
        pb = big.next()
        mmf(pb, pb[:, 0:NT], ones128, ones128[:], osq, osq[:])
        V("act", lambda e, pb=pb: e.activation(out=orin[:], in_=pb[:, 0:NT], func=AF.Ln, bias=eps_c[:, 0:1]), [pb, eps_c], [orin])
        V("act", lambda e: e.activation(out=orin[:], in_=orin[:], func=AF.Exp, scale=-0.5), [orin], [orin])
        V("dve", lambda e, oT=oT: e.scalar_tensor_tensor(out=oT[:], in0=oT[:], scalar=scal[:, 2:3], in1=orin[:], op0=ALU.mult, op1=ALU.mult),
          [oT, scal, orin], [oT])
        stores.append(s.dma("sp", onT[:, t * NT:(t + 1) * NT], oT[:], oT, reads=[oT]))
        yield

    def drain(g):
        for _ in g:
            pass

    if B_DEBUG_STAGE is not None:
        g = prep(0, sets[0])
        for _ in range(B_DEBUG_STAGE):
            next(g, None)
        oT = oT_r.next()
        V("dve", lambda e: e.memset(oT[:], 1.0), [], [oT])
        stores.append(s.dma("sp", onT[:, 0:NT], oT[:], oT, reads=[oT]))
        s.finish(stores)
        s.emit()
        return nc
    drain(prep(0, sets[0]))
    for t in range(NTILE):
        r = recur(t, sets[t % 2])
        p = prep(t + 1, sets[(t + 1) % 2]) if t + 1 < NTILE else iter(())
        rd = pd = False
        while not (rd and pd):
            if not pd:
                try:
                    next(p)
                except StopIteration:
                    pd = True
            if not rd:
                try:
                    next(r)
                except StopIteration:
                    rd = True
    s.finish(stores)
    s.emit()
    return nc


NCH_C = 37


def build_C(NTOK, NT=256):
    nc = bass.Bass("TRN2", target_bir_lowering=False)
    NQB = NT // 128
    hT = nc.dram_tensor("hT", [8, 128, 128 + NTOK], F32, kind="ExternalInput").ap()
    odn_d = nc.dram_tensor("odn", [4, 128, NTOK], F32, kind="ExternalInput").ap()
    memT = nc.dram_tensor("memT", [8, 128, N_MEM], F32, kind="ExternalInput").ap()
    wc_d = nc.dram_tensor("wc", [NCH_C, 128, 8, 128], F32, kind="ExternalInput").ap()
    wv_d = nc.dram_tensor("wv", [128, 8, 128], F32, kind="ExternalInput").ap()
    wmk_d = nc.dram_tensor("wmk", [4, 128, 8, 128], F32, kind="ExternalInput").ap()
    wmv_d = nc.dram_tensor("wmv", [128, 8, 512], F32, kind="ExternalInput").ap()
    wbr_d = nc.dram_tensor("wbr", [24, 128, 4, 128], F32, kind="ExternalInput").ap()
    wo_d = nc.dram_tensor("wo", [8, 128, 8, 128], F32, kind="ExternalInput").ap()
    lnp_d = nc.dram_tensor("lnp", [128, 3, 2, 8], F32, kind="ExternalInput").ap()
    mlnp_d = nc.dram_tensor("mlnp", [128, 1, 2, 8], F32, kind="ExternalInput").ap()
    cst_d = nc.dram_tensor("cst", [128, 3, 128], F32, kind="ExternalInput").ap()
    msk_d = nc.dram_tensor("msk", [128, 2, 512], F32, kind="ExternalInput").ap()
    sink_d = nc.dram_tensor("sinkc", [128, 4], F32, kind="ExternalInput").ap()
    outT = nc.dram_tensor("outT", [8, 128, NTOK], F32, kind="ExternalOutput").ap()

    s = Sched(nc)
    c = load_consts(s, nc, cst_d)
    cst = c["cst"]
    lnp = s.tile([128, 3, 2, 8], F32, "lnp", dma=True)
    s.dma("sp", lnp[:], lnp_d, lnp, writes=[lnp])
    mlnp = s.tile([128, 1, 2, 8], F32, "mlnp", dma=True)
    s.dma("sp", mlnp[:], mlnp_d, mlnp, writes=[mlnp])
    msk = s.tile([128, 2, 512], F32, "msk", dma=True)
    s.dma("sp", msk[:], msk_d, msk, writes=[msk])
    sinkc = s.tile([128, 4], F32, "sinkc", dma=True)
    s.dma("sp", sinkc[:], sink_d, sinkc, writes=[sinkc])
    esink = s.tile([128, 4], F32, "esink")
    s.op("act", lambda e: e.activation(out=esink[:], in_=sinkc[:], func=AF.Exp), reads=[sinkc], writes=[esink])
    onesb = s.tile([128, 128], BF16, "onesb")
    s.op("dve", lambda e: e.tensor_copy(out=onesb[:], in_=cst[:, 2, :]), reads=[cst], writes=[onesb])
    onespad = s.tile([128, 2, 128], BF16, "onespad")
    s.op("dve", lambda e: e.memset(onespad[:], 0.0), writes=[onespad])
    for pos in range(2):
        s.op("dve", lambda e, pos=pos: e.memset(onespad[:, pos, pos * 64:pos * 64 + 64], 1.0), writes=[onespad])
    ps = make_psum(s)
    pring = Ring(ps["all"][0:6])
    ffn = FFNStage(s, NT, None, None, lnp, 1, c, ps, with_ffn=False)
    ybuf = s.tile([128, 8, NT], F32, "ybuf", dma=True)

    def V(eng, fn, reads, writes):
        s.op(eng, fn, reads=reads, writes=writes)

    wi = Ring([s.tile([128, 8, 128], BF16, "wi", dma=True) for _ in range(3)])
    wcR = [s.tile([128, 8, 128], BF16, "wcR", dma=True) for _ in range(NCH_C)]
    for ci in [8] + [i for i in range(NCH_C) if i != 8]:
        s.dma("pool", wcR[ci][:], wc_d[ci], wcR[ci], writes=[wcR[ci]])
    wbr = Ring([s.tile([128, 4, 128], BF16, "wbr", dma=True) for _ in range(3)])

    merged = s.tile([128, 8, NT], F32, "merged", dma=True)
    mem32 = merged if NT == N_MEM else s.tile([128, 8, N_MEM], F32, "mem32", dma=True)
    s.dma("sp", mem32[:], memT.rearrange("k p n -> p k n"), mem32, writes=[mem32])
    memb = s.tile([128, 8, N_MEM], BF16, "memb")
    mst = [s.tile([128, N_MEM], F32, f"mst{i}") for i in range(4)]
    pm, pe_ = ps["m"], ps["e"]
    for m in range(8):
        V("act", lambda e, m=m: e.activation(out=mst[0][:], in_=mem32[:, m, :], func=AF.Square), [mem32], [mst[0]])
        mm(s, pm, pm[:, 0:N_MEM], c["onesD"], c["onesD"][:], mem32, mem32[:, m, :], m == 0, m == 7)
        mm(s, pe_, pe_[:, 0:N_MEM], c["onesD"], c["onesD"][:], mst[0], mst[0][:], m == 0, m == 7)
    eps1 = s.tile([128, 1], F32, "eps1")
    V("dve", lambda e: e.memset(eps1[:], LN_EPS), [], [eps1])
    V("act", lambda e: e.activation(out=mst[1][:], in_=pm[:, 0:N_MEM], func=AF.Copy), [pm], [mst[1]])
    V("dve", lambda e: e.tensor_tensor(out=mst[2][:], in0=mst[1][:], in1=mst[1][:], op=ALU.mult), [mst[1]], [mst[2]])
    V("dve", lambda e: e.tensor_tensor(out=mst[2][:], in0=pe_[:, 0:N_MEM], in1=mst[2][:], op=ALU.subtract), [pe_, mst[2]], [mst[2]])
    V("act", lambda e: e.activation(out=mst[2][:], in_=mst[2][:], func=AF.Sqrt, bias=eps1[:, 0:1]), [mst[2], eps1], [mst[2]])
    V("dve", lambda e: e.reciprocal(out=mst[2][:], in_=mst[2][:]), [mst[2]], [mst[2]])
    for m in range(8):
        V("dve", lambda e, m=m: e.tensor_tensor(out=mst[3][:], in0=mem32[:, m, :], in1=mst[1][:], op=ALU.subtract), [mem32, mst[1]], [mst[3]])
        V("dve", lambda e: e.tensor_tensor(out=mst[3][:], in0=mst[3][:], in1=mst[2][:], op=ALU.mult), [mst[3], mst[2]], [mst[3]])
        V("act", lambda e, m=m: e.activation(out=memb[:, m, :], in_=mst[3][:], func=AF.Identity,
                                             scale=mlnp[:, 0, 0, m:m + 1], bias=mlnp[:, 0, 1, m:m + 1]), [mst[3], mlnp], [memb])
    kmemT = s.tile([128, 4, N_MEM], BF16, "kmemT")
    for h in range(4):
        w = wi.next()
        s.dma("pool", w[:], wmk_d[h], w, writes=[w])
        p = pring.next()
        for k in range(8):
            mm(s, p, p[:, 0:N_MEM], w, w[:, k, :], memb, memb[:, k, :], k == 0, k == 7)
        V("act", lambda e, h=h, p=p: e.activation(out=kmemT[:, h, :], in_=p[:, 0:N_MEM], func=AF.Copy), [p], [kmemT])
    wmv = s.tile([128, 8, 512], BF16, "wmv", dma=True)
    s.dma("pool", wmv[:], wmv_d, wmv, writes=[wmv])
    vmem = s.tile([128, 2, 512], BF16, "vmem")
    for mc in range(2):
        p = pring.next()
        for k in range(8):
            mm(s, p, p[:], memb, memb[:, k, mc * 128:(mc + 1) * 128], wmv, wmv[:, k, :], k == 0, k == 7)
        V("act", lambda e, mc=mc, p=p: e.activation(out=vmem[:, mc, :], in_=p[:], func=AF.Copy), [p], [vmem])
    wvb = s.tile([128, 8, 128], BF16, "wvb", dma=True)
    s.dma("pool", wvb[:], wv_d, wvb, writes=[wvb])

    NSLOT = 4
    kT = s.tile([128, NSLOT, 128], BF16, "kTs")
    vpad = s.tile([128, NSLOT, 4, 128], BF16, "vpad")
    V("dve", lambda e: e.memset(vpad[:], 0.0), [], [vpad])
    h32 = s.tile([128, 8, NT], F32, "h32", dma=True)
    hbr = [s.tile([128, 8, NT], BF16, "hb", dma=True) for _ in range(2)]
    hh32 = s.tile([128, 8, 128], F32, "hh32", dma=True)
    hhb = s.tile([128, 8, 128], BF16, "hhb")
    zs = s.tile([128, 4, NT], F32, "zs")
    swq = s.tile([128, 4, NT], BF16, "swq")
    xaq = s.tile([128, 4, NT], BF16, "xaq")
    gsb = Ring([s.tile([128, NT], F32, f"gsb{i}") for i in range(3)])
    odn = s.tile([128, 4, NT], F32, "odn", dma=True)
    br = [s.tile([128, 4, NT], BF16, f"br{i}") for i in range(3)]
    pexp = Ring([s.tile([128, 512], F32, f"pexp{i}") for i in range(2)])
    pT = Ring([s.tile([128, 512], BF16, f"pT{i}") for i in range(3)])
    pm_sw = [s.tile([128, 512], BF16, f"pmsw{i}") for i in range(8)]
    rden = Ring([s.tile([128, NT], F32, f"rden{i}") for i in range(2)])
    mergedb = s.tile([128, 8, NT], BF16, "mergedb")
    gtmp = Ring([s.tile([128, NT], F32, f"gtmp{i}") for i in range(2)])
    stores = []

    def kv_block(src_b, cols, slot):
        w = wcR[8]
        p = pring.next()
        for k in range(8):
            mm(s, p, p[:, 0:128], w, w[:, k, :], src_b, src_b[:, k, cols], k == 0, k == 7)
        V("act", lambda e, p=p, slot=slot: e.activation(out=kT[:, slot, :], in_=p[:, 0:128], func=AF.Copy), [p], [kT])
        p2 = pring.next()
        for k in range(8):
            mm(s, p2, p2[:, 0:128], src_b, src_b[:, k, cols], wvb, wvb[:, k, :], k == 0, k == 7)
        for kv in range(2):
            for pos in range(2):
                V("dve", lambda e, p2=p2, slot=slot, kv=kv, pos=pos: e.tensor_copy(
                    out=vpad[:, slot, kv * 2 + pos, pos * 64:pos * 64 + 64], in_=p2[:, kv * 64:kv * 64 + 64]), [p2], [vpad])

    s.dma("sp", hh32[:], hT[:, :, 0:128].rearrange("k p n -> p k n"), hh32, writes=[hh32])
    V("act", lambda e: e.activation(out=hhb[:], in_=hh32[:], func=AF.Copy), [hh32], [hhb])
    kv_block(hhb, slice(0, 128), 0)

    cur = {}

    def load_h(t):
        hbn = hbr[t % 2]
        s.dma("pool", hbn[:], hT[:, :, 128 + t * NT:128 + (t + 1) * NT].rearrange("k p n -> p k n"), hbn, writes=[hbn])

    def proj(ci, out_fn):
        hb = cur["hb"]
        w = wcR[ci]
        p = pring.next()
        for k in range(8):
            mm(s, p, p[:, 0:NT], w, w[:, k, :], hb, hb[:, k, :], k == 0, k == 7)
        out_fn(p)

    NTILES = NTOK // NT
    load_h(0)
    for t in range(NTILES):
        c0 = t * NT
        hb = hbr[t % 2]
        cur["hb"] = hb
        s.dma("sp", h32[:], hT[:, :, 128 + c0:128 + c0 + NT].rearrange("k p n -> p k n"), h32, writes=[h32])
        s.dma("sp", odn[:], odn_d[:, :, c0:c0 + NT].rearrange("h p n -> p h n"), odn, writes=[odn])
        for j in range(4):
            proj(j, lambda p, j=j: V("act", lambda e: e.activation(out=zs[:, j, :], in_=p[:, 0:NT], func=AF.Silu), [p], [zs]))
        for j in range(4):
            proj(4 + j, lambda p, j=j: V("act", lambda e: e.activation(out=swq[:, j, :], in_=p[:, 0:NT], func=AF.Copy), [p], [swq]))
        for j in range(4):
            proj(9 + j, lambda p, j=j: V("act", lambda e: e.activation(out=xaq[:, j, :], in_=p[:, 0:NT], func=AF.Copy), [p], [xaq]))
        for qb in range(NQB):
            gb = t * NQB + qb
            kv_block(hb, slice(qb * 128, (qb + 1) * 128), (gb + 1) % NSLOT)
        V("dve", lambda e: e.tensor_tensor(out=br[0][:], in0=odn[:], in1=zs[:], op=ALU.mult), [odn, zs], [br[0]])
        for h in range(4):
            pts = []
            for mc in range(2):
                p = pring.next()
                mm(s, p, p[:, 0:NT], kmemT, kmemT[:, h, mc * 128:(mc + 1) * 128], xaq, xaq[:, h, :], True, True)
                pt = pT.next()
                V("act", lambda e, p=p, pt=pt: e.activation(out=pt[:, 0:NT], in_=p[:, 0:NT], func=AF.Exp, scale=128 ** -0.5), [p], [pt])
                pts.append(pt)
            po, pd = pring.next(), pring.next()
            for mc in range(2):
                mm(s, po, po[:, 0:NT], vmem, vmem[:, mc, h * 128:(h + 1) * 128], pts[mc], pts[mc][:, 0:NT], mc == 0, mc == 1)
            for mc in range(2):
                mm(s, pd, pd[:, 0:NT], onesb, onesb[:], pts[mc], pts[mc][:, 0:NT], mc == 0, mc == 1)
            rd = rden.next()
            V("act", lambda e, pd=pd, rd=rd: e.activation(out=rd[:], in_=pd[:, 0:NT], func=AF.Ln), [pd], [rd])
            V("act", lambda e, rd=rd: e.activation(out=rd[:], in_=rd[:], func=AF.Exp, scale=-1.0), [rd], [rd])
            V("dve", lambda e, po=po, rd=rd, h=h: e.tensor_tensor(out=br[2][:, h, :], in0=po[:, 0:NT], in1=rd[:], op=ALU.mult), [po, rd], [br[2]])
        mi = 0 if t == 0 else 1
        for h in range(8):
            rows = slice(0, 64) if h < 4 else slice(64, 128)
            ch = h % 4
            p = pring.next()
            for qb in range(NQB):
                gb = t * NQB + qb
                qcols = slice(qb * 128, (qb + 1) * 128)
                mm(s, p, p[:, qb * 256:qb * 256 + 128], kT, kT[rows, gb % NSLOT, :], swq, swq[rows, ch, qcols], True, True)
                mm(s, p, p[:, qb * 256 + 128:qb * 256 + 256], kT, kT[rows, (gb + 1) % NSLOT, :], swq, swq[rows, ch, qcols], True, True)
            pe2 = pexp.next()
            V("act", lambda e, p=p, pe2=pe2: e.activation(out=pe2[:, 0:NQB * 256], in_=p[:, 0:NQB * 256], func=AF.Exp, scale=64 ** -0.5), [p], [pe2])
            V("dve", lambda e, pe2=pe2, h=h, mi=mi: e.tensor_tensor(out=pm_sw[h][:, 0:NQB * 256], in0=pe2[:, 0:NQB * 256], in1=msk[:, mi, 0:NQB * 256], op=ALU.mult),
              [pe2, msk], [pm_sw[h]])
        for pr in range(4):
            kv = pr // 2
            po, pd = pring.next(), pring.next()
            for qb in range(NQB):
                gb = t * NQB + qb
                qc = slice(qb * 128, (qb + 1) * 128)
                n = 0
                for pos in range(2):
                    h = pr * 2 + pos
                    for part, slot in ((0, gb % NSLOT), (1, (gb + 1) % NSLOT)):
                        pc = slice(qb * 256 + part * 128, qb * 256 + part * 128 + 128)
                        mm(s, po, po[:, qc], vpad, vpad[:, slot, kv * 2 + pos, :], pm_sw[h], pm_sw[h][:, pc], n == 0, n == 3)
                        n += 1
                n = 0
                for pos in range(2):
                    h = pr * 2 + pos
                    for part in range(2):
                        pc = slice(qb * 256 + part * 128, qb * 256 + part * 128 + 128)
                        mm(s, pd, pd[:, qc], onespad, onespad[:, pos, :], pm_sw[h], pm_sw[h][:, pc], n == 0, n == 3)
                        n += 1
            rd = rden.next()
            V("act", lambda e, pd=pd, rd=rd, pr=pr: e.activation(out=rd[:], in_=pd[:, 0:NT], func=AF.Ln, bias=esink[:, pr:pr + 1]), [pd, esink], [rd])
            V("act", lambda e, rd=rd: e.activation(out=rd[:], in_=rd[:], func=AF.Exp, scale=-1.0), [rd], [rd])
            V("dve", lambda e, po=po, rd=rd, pr=pr: e.tensor_tensor(out=br[1][:, pr, :], in0=po[:, 0:NT], in1=rd[:], op=ALU.mult), [po, rd], [br[1]])
        if t + 1 < NTILES:
            load_h(t + 1)
        for m in range(8):
            for n in range(3):
                gs = gsb.next()
                proj(13 + n * 8 + m, lambda p, gs=gs: V("act", lambda e: e.activation(out=gs[:], in_=p[:, 0:NT], func=AF.Sigmoid), [p], [gs]))
                w = wbr.next()
                s.dma("pool", w[:], wbr_d[n * 8 + m], w, writes=[w])
                p = pring.next()
                for k in range(4):
                    mm(s, p, p[:, 0:NT], w, w[:, k, :], br[n], br[n][:, k, :], k == 0, k == 3)
                if n == 0:
                    V("dve", lambda e, p=p, m=m, gs=gs: e.tensor_tensor(out=merged[:, m, :], in0=p[:, 0:NT], in1=gs[:], op=ALU.mult),
                      [p, gs], [merged])
                else:
                    g = gtmp.next()
                    V("dve", lambda e, p=p, gs=gs, g=g: e.tensor_tensor(out=g[:], in0=p[:, 0:NT], in1=gs[:], op=ALU.mult),
                      [p, gs], [g])
                    V("dve", lambda e, m=m, g=g: e.tensor_tensor(out=merged[:, m, :], in0=merged[:, m, :], in1=g[:], op=ALU.add),
                      [merged, g], [merged])
        V("act", lambda e: e.activation(out=mergedb[:], in_=merged[:], func=AF.Copy), [merged], [mergedb])
        for m in range(8):
            w = wi.next()
            s.dma("pool", w[:], wo_d[m], w, writes=[w])
            p = pring.next()
            for k in range(8):
                mm(s, p, p[:, 0:NT], w, w[:, k, :], mergedb, mergedb[:, k, :], k == 0, k == 7)
            V("dve", lambda e, p=p, m=m: e.scalar_tensor_tensor(out=ybuf[:, m, :], in0=p[:, 0:NT], scalar=1.0 / ALPHA, in1=h32[:, m, :],
                                                              op0=ALU.mult, op1=ALU.add), [p, h32], [ybuf])
        ffn.ln(ybuf, None, 1)
        stores.append(s.dma("sp", outT[:, :, c0:c0 + NT].rearrange("k p n -> p k n"), ybuf[:], ybuf, reads=[ybuf]))
    s.finish(stores)
    s.emit()
    return nc


def lay_wc(w_in):
    swq = w_in[:, 2056:2568].reshape(1024, 8, 64)
    swq_p = np.concatenate([np.concatenate([swq[:, j], swq[:, 4 + j]], axis=1) for j in range(4)], axis=1)
    wcat = np.concatenate([w_in[:, 1544:2056], swq_p, w_in[:, 2568:2696], w_in[:, 2824:3336], w_in[:, 3336:6408]], axis=1)
    return lay_w_kmc(wcat)


def lay_pkc(w):
    K, C = w.shape
    return np.ascontiguousarray(w.reshape(K // 128, 128, C).transpose(1, 0, 2))


def lay_wbr(wb):
    a = wb.reshape(3, 4, 128, 8, 128).transpose(0, 3, 2, 1, 4)
    return np.ascontiguousarray(a.reshape(24, 128, 4, 128))


def make_masks(halo_valid):
    k = np.arange(128)[:, None]
    q = np.arange(128)[None, :]
    prev = (k > q).astype(np.float32)
    cur = (k <= q).astype(np.float32)
    std = np.concatenate([prev, cur, prev, cur], axis=1)
    first = np.concatenate([prev * halo_valid, cur, prev, cur], axis=1)
    return np.ascontiguousarray(np.stack([first, std], axis=1)).astype(np.float32)


def lay_sink(sinks):
    c = np.zeros((128, 4), np.float32)
    for pr in range(4):
        c[:64, pr] = sinks[2 * pr]
        c[64:, pr] = sinks[2 * pr + 1]
    return c


_PROGS = {}
TOK_PER_CORE = SEQ * BATCH // NCORES
NSEG = SEQ // TOK_PER_CORE


def _prog(name):
    if name not in _PROGS:
        if name == "A":
            _PROGS[name] = build_A(TOK_PER_CORE)
        elif name == "B":
            _PROGS[name] = build_B(SEQ)
        elif name == "F":
            _PROGS[name] = build_A(TOK_PER_CORE, proj=False, ln_idx=2)
        else:
            _PROGS[name] = build_C(TOK_PER_CORE)
    return _PROGS[name]


def _run(name, in_maps):
    res = run_bass_kernel_spmd(_prog(name), in_maps, core_ids=list(range(NCORES)))
    return res.results


def kernel(x, mem, mem_ln_g, mem_ln_b, ln_g, ln_b, ffn1_w_gu, ffn1_w_down, w_in, dn_conv_w,
           dn_a_log, dn_dt_bias, dn_norm_w, swa_sinks, w_mem_kv, w_branch, w_out, ffn2_w_gu, ffn2_w_down):
    f = lambda a: np.asarray(a, dtype=np.float32)
    cur = f(x)
    mem = f(mem)
    cst = make_cst()
    cstB, ccol = make_cstB()
    NT_ = TOK_PER_CORE
    for l in range(DEPTH):
        win = f(w_in[l])
        lnp = lay_lnp(f(ln_g[l]), f(ln_b[l]))
        commonA = dict(wgu=lay_wgu(f(ffn1_w_gu[l])), wdn=lay_w_kmc(f(ffn1_w_down[l])), wqkv=lay_w_kmc(win[:, :1536]),
                       wba=lay_pkc(win[:, 1536:1544]), lnp=lnp, cst=cst)
        insA = []
        for c in range(NCORES):
            b, sg = divmod(c, NSEG)
            xs = cur[b, sg * NT_:(sg + 1) * NT_]
            insA.append(dict(xT=np.ascontiguousarray(xs.T.reshape(8, 128, NT_)), **commonA))
        rA = _run("A", insA)
        convw = f(dn_conv_w[l])
        insB = []
        for c in range(NCORES):
            b, hd = divmod(c, 4)
            qkvp = np.concatenate([rA[b * NSEG + sg]["qkvT"][[hd, 4 + hd, 8 + hd]] for sg in range(NSEG)], axis=2)
            brow = np.concatenate([rA[b * NSEG + sg]["baT"][hd] for sg in range(NSEG)])
            arow = np.concatenate([rA[b * NSEG + sg]["baT"][4 + hd] for sg in range(NSEG)])
            cw = np.stack([convw[:, w * 512 + hd * 128: w * 512 + hd * 128 + 128] for w in range(3)], 0)
            smallp = np.zeros((128, 32), np.float32)
            smallp[:, 0:12] = cw.transpose(2, 0, 1).reshape(128, 12)
            smallp[:, 12] = f(dn_a_log[l])[hd]
            smallp[:, 13] = f(dn_dt_bias[l])[hd]
            smallp[:, 14] = f(dn_norm_w[l])
            smallp[:, 15:17] = ccol
            insB.append(dict(qkvp=np.ascontiguousarray(qkvp), bcol=np.ascontiguousarray(brow.reshape(SEQ // 128, 128).T),
                             acol=np.ascontiguousarray(arow.reshape(SEQ // 128, 128).T),
                             smallp=smallp, cstB=cstB))
        rB = _run("B", insB)
        wkv = f(w_mem_kv[l])
        mg = np.stack([f(mem_ln_g)] * 3)
        mb = np.stack([f(mem_ln_b)] * 3)
        commonC = dict(wc=lay_wc(win), wv=lay_pkc(win[:, 2696:2824]), wmk=lay_w_kmc(wkv[:, :512]), wmv=lay_pkc(wkv[:, 512:]),
                       wbr=lay_wbr(f(w_branch[l])), wo=lay_w_kmc(f(w_out[l])),
                       lnp=lnp, mlnp=np.ascontiguousarray(lay_lnp(mg, mb)[:, 0:1]),
                       cst=cst, sinkc=lay_sink(f(swa_sinks[l])))
        insC = []
        for c in range(NCORES):
            b, sg = divmod(c, NSEG)
            hT = rA[c]["hT"]
            halo = rA[c - 1]["hT"][:, :, -128:] if sg > 0 else np.zeros((8, 128, 128), np.float32)
            odn = np.stack([rB[b * 4 + hd]["onT"][:, sg * NT_:(sg + 1) * NT_] for hd in range(4)], 0)
            insC.append(dict(hT=np.ascontiguousarray(np.concatenate([halo, hT], axis=2)), odn=np.ascontiguousarray(odn),
                             memT=np.ascontiguousarray(mem[b].T.reshape(8, 128, N_MEM)),
                             msk=make_masks(1.0 if sg > 0 else 0.0), **commonC))
        rC = _run("C", insC)
        commonF = dict(wgu=lay_wgu(f(ffn2_w_gu[l])), wdn=lay_w_kmc(f(ffn2_w_down[l])), lnp=lnp, cst=cst)
        rF = _run("F", [dict(xT=rC[c]["outT"], **commonF) for c in range(NCORES)])
        nxt = np.empty_like(cur)
        for c in range(NCORES):
            b, sg = divmod(c, NSEG)
            nxt[b, sg * NT_:(sg + 1) * NT_] = rF[c]["hT"].reshape(D_MODEL, NT_).T
        cur = nxt
    return cur
```
